# Optimizing a Trainium2 kernel written in Bass

```python
import math
import jax
import jax.numpy as jnp
from jax import lax
import numpy as np

D_MODEL = 1024
BATCH = 8
SEQ = 8192
DEPTH = 2
DEC_BATCH = 4
DEC_SEQ = 4096
PAST_LEN = 128

D_MIX = 1024
D_PLE = 256
GRID_W = 64
ROPE_THETA = 10000.0
EPS = 1e-6
Q_BLOCK = 128
NEG_INF = -1e30

A_HEADS = 4
A_QK_DIM = 32
A_V_DIM = 64
B_HEADS = 4
B_KV_HEADS = 2
B_HEAD_DIM = 64
B_WINDOW = 128
C_HEADS = 4
C_HEAD_DIM = 64
C_MAX_ROWS = 8
C_WIN_COLS = 16
D_HEADS = 4
D_Q_RANK = 256
D_KV_RANK = 128
D_NOPE = 64
D_ROPE = 32
D_V = 64

IN_SIZES = (
    A_HEADS * 2 * A_QK_DIM, A_HEADS * 2 * A_QK_DIM, A_HEADS * A_V_DIM,
    B_HEADS * B_HEAD_DIM, B_KV_HEADS * B_HEAD_DIM, B_KV_HEADS * B_HEAD_DIM,
    C_HEADS * C_HEAD_DIM, C_HEADS * C_HEAD_DIM, C_HEADS * C_HEAD_DIM,
    D_Q_RANK, D_KV_RANK, D_ROPE,
    D_MIX,
)
D_IN = sum(IN_SIZES)

kernel_name = 'hybrid_parallel_head_encoder'


def _split_points():
    pts, acc = [], 0
    for n in IN_SIZES[:-1]:
        acc += n
        pts.append(acc)
    return pts


def rmsnorm(x, g):
    xf = x.astype(jnp.float32)
    y = xf * lax.rsqrt(jnp.mean(xf * xf, axis=-1, keepdims=True) + EPS)
    return (y * g.astype(jnp.float32)).astype(x.dtype)


def rope_tables(n, dim):
    inv = 1.0 / (ROPE_THETA ** (jnp.arange(0, dim, 2, dtype=jnp.float32) / dim))
    ang = jnp.arange(n, dtype=jnp.float32)[:, None] * inv[None, :]
    return jnp.cos(ang), jnp.sin(ang)


def apply_rope(x, cos, sin):
    half = x.shape[-1] // 2
    xf = x.astype(jnp.float32)
    x1, x2 = xf[..., :half], xf[..., half:]
    c, sn = cos[:, None, :], sin[:, None, :]
    return jnp.concatenate([x1 * c - x2 * sn, x2 * c + x1 * sn], axis=-1).astype(x.dtype)


def sweep_query_blocks(fn, *qs):
    b, s = qs[0].shape[:2]
    nb = s // Q_BLOCK
    blk = tuple(jnp.moveaxis(q.reshape((b, nb, Q_BLOCK) + q.shape[2:]), 1, 0) for q in qs)
    out = lax.map(lambda a: fn(*a), blk)
    out = jnp.moveaxis(out, 0, 1)
    return out.reshape((b, s) + out.shape[3:])


def diff_attention(q, k, v, lam_vecs, lam_init, subln_g, cos, sin):
    b, s = q.shape[:2]
    q = apply_rope(q.reshape(b, s, A_HEADS * 2, A_QK_DIM), cos, sin).reshape(b, s, A_HEADS, 2, A_QK_DIM)
    k = apply_rope(k.reshape(b, s, A_HEADS * 2, A_QK_DIM), cos, sin).reshape(b, s, A_HEADS, 2, A_QK_DIM)
    lv = lam_vecs.astype(jnp.float32)
    lam = jnp.exp(jnp.sum(lv[0] * lv[1])) - jnp.exp(jnp.sum(lv[2] * lv[3])) + lam_init
    scale = A_QK_DIM ** -0.5

    def block(qb):
        sc = jnp.einsum('bqhmd,bkhmd->bhmqk', qb, k).astype(jnp.float32) * scale
        pr = jax.nn.softmax(sc, axis=-1)
        pr = pr[:, :, 0] - lam * pr[:, :, 1]
        return jnp.einsum('bhqk,bkhd->bqhd', pr.astype(v.dtype), v)

    o = sweep_query_blocks(block, q)
    o = rmsnorm(o, subln_g) * (1.0 - lam_init)
    return o.reshape(b, s, A_HEADS * A_V_DIM)


def window_gqa(q, k, v, sink, cos, sin):
    b, s = q.shape[:2]
    g = B_HEADS // B_KV_HEADS
    w = B_WINDOW
    nb = s // w
    q = apply_rope(q, cos, sin)
    k = apply_rope(k, cos, sin)

    def band(t):
        tb = jnp.pad(t, ((0, 0), (w, w), (0, 0), (0, 0))).reshape(b, nb + 2, w, B_KV_HEADS, B_HEAD_DIM)
        return jnp.concatenate([tb[:, :-2], tb[:, 1:-1], tb[:, 2:]], axis=2)

    kb, vb = band(k), band(v)
    qb = q.reshape(b, nb, w, B_KV_HEADS, g, B_HEAD_DIM)
    sc = jnp.einsum('bnqkgd,bnckd->bnkgqc', qb, kb).astype(jnp.float32) * (B_HEAD_DIM ** -0.5)
    rel = jnp.arange(3 * w)[None, :] - w - jnp.arange(w)[:, None]
    key_pos = jnp.arange(nb)[:, None] * w - w + jnp.arange(3 * w)[None, :]
    valid = (jnp.abs(rel) <= w)[None] & ((key_pos >= 0) & (key_pos < s))[:, None, :]
    sc = jnp.where(valid[None, :, None, None], sc, NEG_INF)
    sink_col = jnp.broadcast_to(sink.astype(jnp.float32).reshape(1, 1, B_KV_HEADS, g, 1, 1), sc.shape[:-1] + (1,))
    pr = jax.nn.softmax(jnp.concatenate([sc, sink_col], axis=-1), axis=-1)[..., :-1]
    o = jnp.einsum('bnkgqc,bnckd->bnqkgd', pr.astype(v.dtype), vb)
    return o.reshape(b, s, B_HEADS * B_HEAD_DIM)


def neighbourhood_attention(q, k, v, rpb):
    b, s = q.shape[:2]
    rows = s // GRID_W
    kr = min(C_MAX_ROWS, rows)
    shp = (b, rows, GRID_W, C_HEADS, C_HEAD_DIM)
    qg, kg, vg = q.reshape(shp), k.reshape(shp), v.reshape(shp)
    r = jnp.arange(rows)
    row_idx = jnp.clip(r - kr // 2, 0, rows - kr)[:, None] + jnp.arange(kr)[None, :]
    kw, vw = kg[:, row_idx], vg[:, row_idx]
    sc = jnp.einsum('brqhd,brjchd->bhrqjc', qg, kw).astype(jnp.float32) * (C_HEAD_DIM ** -0.5)
    c = jnp.arange(GRID_W)
    col_start = jnp.clip(c - C_WIN_COLS // 2, 0, GRID_W - C_WIN_COLS)
    col_valid = (c[None, :] >= col_start[:, None]) & (c[None, :] < col_start[:, None] + C_WIN_COLS)
    dr = row_idx - r[:, None] + (C_MAX_ROWS - 1)
    dc = jnp.clip(c[None, :] - c[:, None], -(C_WIN_COLS - 1), C_WIN_COLS - 1) + (C_WIN_COLS - 1)
    bias = rpb.astype(jnp.float32)[:, dr[:, None, :, None], dc[None, :, None, :]]
    sc = jnp.where(col_valid[:, None, :], sc + bias[None], NEG_INF)
    pr = jax.nn.softmax(sc, axis=(-2, -1))
    o = jnp.einsum('bhrqjc,brjchd->brqhd', pr.astype(v.dtype), vw)
    return o.reshape(b, s, C_HEADS * C_HEAD_DIM)


def mla(c_q, c_kv, k_rope, q_norm_g, kv_norm_g, w_uq, w_ukv, cos, sin):
    b, s = c_q.shape[:2]
    q = (rmsnorm(c_q, q_norm_g) @ w_uq).reshape(b, s, D_HEADS, D_NOPE + D_ROPE)
    kv = (rmsnorm(c_kv, kv_norm_g) @ w_ukv).reshape(b, s, D_HEADS, D_NOPE + D_V)
    q_nope, q_pe = q[..., :D_NOPE], apply_rope(q[..., D_NOPE:], cos, sin)
    k_nope, v = kv[..., :D_NOPE], kv[..., D_NOPE:]
    k_pe = apply_rope(k_rope.reshape(b, s, 1, D_ROPE), cos, sin)[:, :, 0]
    scale = (D_NOPE + D_ROPE) ** -0.5

    def block(qn, qp):
        sc = (jnp.einsum('bqhd,bkhd->bhqk', qn, k_nope).astype(jnp.float32)
              + jnp.einsum('bqhd,bkd->bhqk', qp, k_pe).astype(jnp.float32)) * scale
        pr = jax.nn.softmax(sc, axis=-1)
        return jnp.einsum('bhqk,bkhd->bqhd', pr.astype(v.dtype), v)

    o = sweep_query_blocks(block, q_nope, q_pe)
    return o.reshape(b, s, D_HEADS * D_V)


def trunk(x, p, norm_g, w_in, a_lambda, a_subln_g, b_sink, c_rpb, d_q_norm_g, d_kv_norm_g,
          d_w_uq, d_w_ukv, w_out, ple_norm_g, w_ple_gate, w_ple_proj, final_norm_g):
    b, s, _ = x.shape
    cos32, sin32 = rope_tables(s, A_QK_DIM)
    cos64, sin64 = rope_tables(s, B_HEAD_DIM)
    cosd, sind = rope_tables(s, D_ROPE)
    pts = _split_points()
    h = x
    for i in range(DEPTH):
        hn = rmsnorm(h, norm_g[i])
        z = hn @ w_in[i]
        (aq, ak, av, bq, bk, bv, cq, ck, cv, dcq, dckv, dkr, gate) = jnp.split(z, pts, axis=-1)
        lam_init = 0.8 - 0.6 * math.exp(-0.3 * i)
        o_a = diff_attention(aq.reshape(b, s, A_HEADS, 2, A_QK_DIM), ak.reshape(b, s, A_HEADS, 2, A_QK_DIM),
                             av.reshape(b, s, A_HEADS, A_V_DIM), a_lambda[i], lam_init, a_subln_g[i],
                             cos32, sin32)
        o_b = window_gqa(bq.reshape(b, s, B_HEADS, B_HEAD_DIM), bk.reshape(b, s, B_KV_HEADS, B_HEAD_DIM),
                         bv.reshape(b, s, B_KV_HEADS, B_HEAD_DIM), b_sink[i], cos64, sin64)
        o_c = neighbourhood_attention(cq.reshape(b, s, C_HEADS, C_HEAD_DIM), ck.reshape(b, s, C_HEADS, C_HEAD_DIM),
                                      cv.reshape(b, s, C_HEADS, C_HEAD_DIM), c_rpb[i])
        o_d = mla(dcq, dckv, dkr, d_q_norm_g[i], d_kv_norm_g[i], d_w_uq[i], d_w_ukv[i], cosd, sind)
        mix = jnp.concatenate([o_a, o_b, o_c, o_d], axis=-1) * jax.nn.silu(gate)
        h = h + mix @ w_out[i]
        ple_gate = jax.nn.sigmoid(rmsnorm(h, ple_norm_g[i]) @ w_ple_gate[i])
        h = h + ple_gate * (p[i] @ w_ple_proj[i])
    return rmsnorm(h, final_norm_g)


def setup_inputs(seed: int = 0) -> dict:
    key = jax.random.key(seed)
    ks = jax.random.split(key, 19)
    f32 = jnp.float32

    def nrm(k, shape, scale=1.0):
        return jax.random.normal(k, shape, f32) * scale

    def gain(k, shape):
        return 1.0 + 0.05 * jax.random.normal(k, shape, f32)

    return {
        'x_prompt': nrm(ks[0], (BATCH, SEQ, D_MODEL)),
        'x_sample': nrm(ks[1], (DEC_BATCH, DEC_SEQ, D_MODEL)),
        'p_prompt': nrm(ks[2], (DEPTH, BATCH, SEQ, D_PLE)),
        'p_sample': nrm(ks[3], (DEPTH, DEC_BATCH, DEC_SEQ, D_PLE)),
        'norm_g': gain(ks[4], (DEPTH, D_MODEL)),
        'w_in': nrm(ks[5], (DEPTH, D_MODEL, D_IN), D_MODEL ** -0.5),
        'a_lambda': nrm(ks[6], (DEPTH, 4, A_QK_DIM), 0.1),
        'a_subln_g': gain(ks[7], (DEPTH, A_V_DIM)),
        'b_sink': nrm(ks[8], (DEPTH, B_HEADS), 0.5),
        'c_rpb': nrm(ks[9], (DEPTH, C_HEADS, 2 * C_MAX_ROWS - 1, 2 * C_WIN_COLS - 1), 0.1),
        'd_q_norm_g': gain(ks[10], (DEPTH, D_Q_RANK)),
        'd_kv_norm_g': gain(ks[11], (DEPTH, D_KV_RANK)),
        'd_w_uq': nrm(ks[12], (DEPTH, D_Q_RANK, D_HEADS * (D_NOPE + D_ROPE)), D_Q_RANK ** -0.5),
        'd_w_ukv': nrm(ks[13], (DEPTH, D_KV_RANK, D_HEADS * (D_NOPE + D_V)), D_KV_RANK ** -0.5),
        'w_out': nrm(ks[14], (DEPTH, D_MIX, D_MODEL), D_MIX ** -0.5),
        'ple_norm_g': gain(ks[15], (DEPTH, D_MODEL)),
        'w_ple_gate': nrm(ks[16], (DEPTH, D_MODEL, D_MODEL), D_MODEL ** -0.5),
        'w_ple_proj': nrm(ks[17], (DEPTH, D_PLE, D_MODEL), D_PLE ** -0.5),
        'final_norm_g': gain(ks[18], (D_MODEL,)),
    }


def reference(x_prompt, x_sample, p_prompt, p_sample, norm_g, w_in, a_lambda, a_subln_g, b_sink, c_rpb,
              d_q_norm_g, d_kv_norm_g, d_w_uq, d_w_ukv, w_out, ple_norm_g, w_ple_gate, w_ple_proj,
              final_norm_g):
    y_prompt = trunk(x_prompt, p_prompt, norm_g, w_in, a_lambda, a_subln_g, b_sink, c_rpb, d_q_norm_g,
                     d_kv_norm_g, d_w_uq, d_w_ukv, w_out, ple_norm_g, w_ple_gate, w_ple_proj, final_norm_g)
    y_sample = trunk(x_sample, p_sample, norm_g, w_in, a_lambda, a_subln_g, b_sink, c_rpb, d_q_norm_g,
                     d_kv_norm_g, d_w_uq, d_w_ukv, w_out, ple_norm_g, w_ple_gate, w_ple_proj, final_norm_g)
    return (y_prompt, y_sample)
```

```python
import contextlib
import math

import numpy as np
import concourse.bass as bass
import concourse.mybir as mybir
from concourse.bass_utils import run_bass_kernel_spmd

F32 = mybir.dt.float32
BF16 = mybir.dt.bfloat16
AF = mybir.ActivationFunctionType
ALU = mybir.AluOpType
AX = mybir.AxisListType

ENGS = ['pe', 'act', 'dve', 'pool', 'sp']
ENGOBJ = {'pe': 'tensor', 'act': 'scalar', 'dve': 'vector', 'pool': 'gpsimd', 'sp': 'sync'}

D_MODEL = 1024
EPS = 1e-6
NEG = -1e30
NC1 = 3488
F_OFF = [i * 128 for i in range(11)] + [1408]
F_M = [128] * 11 + [32]
T_AVBV = 1440
T_CV = 1824
T_DC = 2080
T_GATE = 2464
LD_ENG = 'sp'
DEBUG_NAMES = None
PHASES = 'PADBCE'
EMBED_WAIT = True
ST_ENG = 'pool'


def I(name, *args, **kw):
    return (name, args, kw)


class Buf:
    __slots__ = ('name', 'lw', 'rd', 'ord', 'fw')

    def __init__(self, name):
        self.name = name
        self.lw = []
        self.rd = []
        self.ord = []
        self.fw = None


class _Op:
    __slots__ = ('fn', 'waits', 'signal', 'dma_sem')

    def __init__(self, fn):
        self.fn = fn
        self.waits = []
        self.signal = False
        self.dma_sem = None


class Prog:
    def __init__(self, nc):
        self.nc = nc
        self.ops = {e: [] for e in ENGS}
        self.dma_cnt = {}

    def _dep(self, op, eng, ev):
        if ev[0] == 'e':
            if ev[1] == eng and eng == 'pe':
                return
            self.ops[ev[1]][ev[2]].signal = True
        op.waits.append(ev)

    def op(self, eng, fn, reads=(), writes=(), pwrites=(), dma=None):
        o = _Op(fn)
        lst = self.ops[eng]
        idx = len(lst)
        for b in reads:
            for w in b.lw:
                self._dep(o, eng, w)
        for b in writes:
            if not b.rd:
                for w in b.lw:
                    self._dep(o, eng, w)
            for r in b.rd:
                self._dep(o, eng, r)
            for r in b.ord:
                self._dep(o, eng, r)
        for b in pwrites:
            if b.rd:
                b.ord = b.rd
                b.rd = []
                b.lw = []
            for r in b.ord:
                self._dep(o, eng, r)
            if b.fw is not None:
                self._dep(o, eng, b.fw)
        if dma is not None:
            n = self.dma_cnt.get(dma, 0) + 1
            self.dma_cnt[dma] = n
            o.dma_sem = dma
            ev = ('d', dma, 16 * n)
        else:
            ev = ('e', eng, idx)
        for b in writes:
            b.lw = [ev]
            b.rd = []
            b.ord = []
            b.fw = ev
        for b in pwrites:
            b.lw.append(ev)
        for b in reads:
            b.rd.append(ev)
        lst.append(o)
        return ev

    def barrier(self):
        evs = []
        for e in ENGS:
            for i in range(len(self.ops[e]) - 1, -1, -1):
                o = self.ops[e][i]
                if o.dma_sem is None and o.fn is not None:
                    evs.append(('e', e, i))
                    break
        for k, n in self.dma_cnt.items():
            evs.append(('d', k, 16 * n))
        for e in ENGS:
            o = _Op(None)
            for ev in evs:
                if ev[0] == 'e':
                    if ev[1] == e and e == 'pe':
                        continue
                    self.ops[ev[1]][ev[2]].signal = True
                o.waits.append(ev)
            self.ops[e].append(o)

    def emit(self):
        nc = self.nc
        with contextlib.ExitStack() as st:
            esem = {e: st.enter_context(nc.semaphore('s_' + e)) for e in ENGS}
            dsem = {k: st.enter_context(nc.semaphore('d_%s' % (k,))) for k in self.dma_cnt}
            sigidx = {}
            for e in ENGS:
                c = 0
                arr = []
                for o in self.ops[e]:
                    if o.signal:
                        c += 1
                    arr.append(c)
                sigidx[e] = arr
            block = st.enter_context(nc.Block())

            def make(e):
                def body(eng):
                    seen = {}
                    for o in self.ops[e]:
                        need = {}
                        for ev in o.waits:
                            if ev[0] == 'e':
                                key = ('e', ev[1])
                                val = sigidx[ev[1]][ev[2]]
                            else:
                                key = ('d', ev[1])
                                val = ev[2]
                            if need.get(key, 0) < val:
                                need[key] = val
                        todo = []
                        for key, val in need.items():
                            if seen.get(key, 0) >= val:
                                continue
                            seen[key] = val
                            todo.append((esem[key[1]] if key[0] == 'e' else dsem[key[1]], val))
                        emb = None
                        if o.fn is not None and todo and EMBED_WAIT:
                            emb = todo.pop()
                        for sem, val in todo:
                            eng.wait_ge(sem, val)
                        if o.fn is None:
                            continue
                        ins = getattr(eng, o.fn[0])(*o.fn[1], **o.fn[2])
                        if emb is not None:
                            ins._wait_ge(emb[0], emb[1])
                        if DEBUG_NAMES is not None:
                            DEBUG_NAMES[getattr(ins.ins, 'name', None)] = (e, o.fn[0], str(o.fn[1])[:300], str({k: str(v)[:200] for k, v in o.fn[2].items()}))
                        if o.dma_sem is not None:
                            ins.then_inc(dsem[o.dma_sem], 16)
                        elif o.signal:
                            ins.then_inc(esem[e], 1)
                return body
            for e in ENGS:
                getattr(block, ENGOBJ[e])(make(e))
        return {e: len(self.ops[e]) for e in ENGS}


class Tile:
    __slots__ = ('t', 'b', 'key')

    def __init__(self, t, b, key):
        self.t = t
        self.b = b
        self.key = key


class Cx:
    def __init__(self, nc):
        self.nc = nc
        self.pg = Prog(nc)
        self.uid = 0

    def tile(self, st, name, shape, dt):
        self.uid += 1
        nm = '%s_%d' % (name, self.uid)
        t = st.enter_context(self.nc.sbuf_tensor(nm, list(shape), dt))
        return Tile(t, Buf(nm), name)


def _dma(cx, eng, out, in_, reads=(), writes=(), pwrites=(), key=None):
    return cx.pg.op(eng, I('dma_start', out=out, in_=in_), reads=reads, writes=writes,
                    pwrites=pwrites, dma=key)


def load_w(cx, st, dst, src, C, N, gcol=None, tag='w', piece_bufs=None):
    pg = cx.pg
    NS = 1104
    stg = cx.stg
    k = cx.stg_k
    order = [(c, n0) for c in range(C if C is not None else 1) for n0 in range(0, N, NS)]
    if piece_bufs is not None:
        order = [(c, n0) for n0 in range(0, N, NS) for c in range(C)]
    for (c, n0) in order:
        if True:
            w = min(NS, N - n0)
            s = stg[k % len(stg)]
            rows = src[c * 128:(c + 1) * 128, n0:n0 + w]
            si = k % len(stg)
            _dma(cx, LD_ENG, s.t[:, 0:w], rows, writes=[s.b], key='stg%d' % si)
            dview = dst.t[:, c, n0:n0 + w] if C is not None else dst.t[:, n0:n0 + w]
            dbuf = dst.b if piece_bufs is None else piece_bufs[n0 // NS]
            if gcol is not None:
                pg.op('act', I('activation', out=dview, in_=s.t[:, 0:w], func=AF.Copy, scale=gcol.t[:, c:c + 1]),
                      reads=[s.b, gcol.b], pwrites=[dbuf])
            else:
                pg.op('act', I('activation', out=dview, in_=s.t[:, 0:w], func=AF.Copy),
                      reads=[s.b], pwrites=[dbuf])
            k += 1
    cx.stg_k = k


class Banks:
    def __init__(self, cx):
        self.ps = cx.ps
        self.bufs = [Buf('bank%d' % i) for i in range(8)]
        self.pair_bufs = [Buf('pair%d' % i) for i in range(4)]
        self.rr = 0

    def bank(self, i):
        return self.ps[i // 2][:, i % 2, :], self.bufs[i]

    def bank_bf(self, i):
        return self.ps[i // 2][:, i % 2, :].bitcast(BF16), self.bufs[i]

    def next(self, lo=0, hi=8):
        i = lo + self.rr % (hi - lo)
        self.rr += 1
        return i


def rstd_from_ms(cx, ms, out, n, reads_extra=()):
    pg = cx.pg
    pg.op('act', I('activation', out=out.t[:, 0:n], in_=ms.t[:, 0:n], func=AF.Ln, bias=cx.epsc.t[:, 0:1]),
          reads=[ms.b, cx.epsc.b], writes=[out.b])
    pg.op('act', I('activation', out=out.t[:, 0:n], in_=out.t[:, 0:n], func=AF.Exp, scale=-0.5),
          reads=[out.b], writes=[out.b])


def phase_proj(cx, S, l, h_src, dr):
    pg = cx.pg
    nc = cx.nc
    NBK = S // 512
    with contextlib.ExitStack() as st:
        bk = Banks(cx)
        T = lambda name, shape, dt: cx.tile(st, name, shape, dt)
        w1 = T('w1', [128, 8, NC1], BF16)
        gcol = T('gcol', [128, 8], F32)
        _dma(cx, LD_ENG, gcol.t[:], dr['norm_g'][l], writes=[gcol.b], key='gcol')
        cx.stg = [T('stg%d' % i, [128, 1104], F32) for i in range(3)]
        cx.stg_k = 0
        w1p = [Buf('w1p%d' % i) for i in range(4)]

        def w1r(off, n):
            return [w1p[i] for i in range(off // 1104, (off + n - 1) // 1104 + 1)]
        wuq = T('wuq', [128, 2, 384], BF16)
        wuqr = T('wuqr', [128, 2, 384], BF16)
        wuk = T('wuk', [128, 256], BF16)
        wuv = T('wuv', [128, 256], BF16)
        gqkv = T('gqkv', [128, 384], F32)
        _dma(cx, LD_ENG, gqkv.t[:, 0:256], dr['d_q_norm_g'][l:l + 1, :].partition_broadcast(128),
             pwrites=[gqkv.b], key='gqkv')
        _dma(cx, LD_ENG, gqkv.t[:, 256:384], dr['d_kv_norm_g'][l:l + 1, :].partition_broadcast(128),
             pwrites=[gqkv.b], key='gqkv')

        pm32 = T('pm32', [128, 128], BF16)
        pm64 = T('pm64', [128, 128], BF16)
        pmf = T('pmf', [128, 2, 128], F32)
        _dma(cx, LD_ENG, pmf.t[:], dr['perm'][:, :, :], writes=[pmf.b], key='pmf')
        pg.op('dve', I('tensor_copy', out=pm32.t[:], in_=pmf.t[:, 0, :]), reads=[pmf.b], writes=[pm32.b])
        pg.op('dve', I('tensor_copy', out=pm64.t[:], in_=pmf.t[:, 1, :]), reads=[pmf.b], writes=[pm64.b])
        xbs = [T('xb%d' % i, [128, 512], BF16) for i in range(2)]
        for xb_ in xbs:
            pg.op('pool', I('memset', xb_.t[:], 0.0), writes=[xb_.b])
        hb = [T('hb%d' % i, [128, 4, 1024], F32) for i in range(2)]
        rA = [T('rA%d' % i, [128, 2, 512], F32) for i in range(2)]
        rB = [T('rB%d' % i, [128, 2, 512], F32) for i in range(2)]
        hn = T('hn', [128, 4, 1024], BF16)
        hnT = T('hnT', [128, 8, 512], BF16)
        junk = T('junk', [128, 1024], BF16)
        ss = T('ss', [128, 4], F32)
        rstd = T('rstd', [128, 4], F32)
        t1 = [T('t1_%d' % i, [128, 512], F32) for i in range(2)]
        t2 = [T('t2_%d' % i, [128, 512], F32) for i in range(2)]
        fo = [T('fo%d' % i, [128, 512], BF16) for i in range(4)]
        av1 = [T('av1_%d' % i, [128, 4, 4, 65], BF16) for i in range(2)]
        bv1 = [T('bv1_%d' % i, [128, 4, 2, 65], BF16) for i in range(2)]
        cv1 = [T('cv1_%d' % i, [128, 4, 4, 65], BF16) for i in range(2)]
        dv1 = [T('dv1_%d' % i, [128, 4, 4, 65], BF16) for i in range(2)]
        for tl in av1 + bv1 + cv1 + dv1:
            pg.op('pool', I('memset', tl.t[:], 1.0), writes=[tl.b])
        sg = [T('sg%d' % i, [128, 1024], F32) for i in range(2)]
        ms2_l = [T('ms2_%d' % i, [128, 2], F32) for i in range(2)]
        r2_l = [T('r2_%d' % i, [128, 2], F32) for i in range(2)]
        cn_l = [T('cn_%d' % i, [128, 384], BF16) for i in range(2)]
        cnT_l = [T('cnT%d' % i, [128, 3, 128], BF16) for i in range(2)]
        dqs = [T('dqs%d' % i, [128, 4, 512], BF16) for i in range(2)]
        dks = [T('dks%d' % i, [64, 4, 512], BF16) for i in range(2)]
        fok = 0
        sgk = 0

        def prologue(b):
            h = hb[b % 2]
            ra = rA[b % 2]
            rb = rB[b % 2]
            tok = slice(b * 512, (b + 1) * 512)
            _dma(cx, LD_ENG, h.t[:], h_src[tok, :].rearrange("(t p) f -> p t f", p=128), writes=[h.b],
                 key='hb%d' % (b % 2))
            _dma(cx, LD_ENG, ra.t[:], dr['rope32'][:, :, tok], writes=[ra.b], key='rA%d' % (b % 2))
            _dma(cx, LD_ENG, rb.t[:], dr['rope64'][:, :, tok], writes=[rb.b], key='rB%d' % (b % 2))
            for t in range(4):
                pg.op('act', I('activation', out=junk.t[:], in_=h.t[:, t, :], func=AF.Square,
                                                             scale=1.0 / 32, accum_out=ss.t[:, t:t + 1]),
                      reads=[h.b], writes=[junk.b], pwrites=[ss.b])
            rstd_from_ms(cx, ss, rstd, 4)
            for t in range(4):
                pg.op('act', I('activation', out=hn.t[:, t, :], in_=h.t[:, t, :], func=AF.Copy,
                               scale=rstd.t[:, t:t + 1]),
                      reads=[h.b, rstd.b], pwrites=[hn.b])

        prologue(0)
        load_w(cx, st, w1, dr['w1'][l], 8, NC1, gcol, piece_bufs=w1p)
        load_w(cx, st, wuq, dr['wuq'][l], 2, 384)
        load_w(cx, st, wuqr, dr['wuqr'][l], 2, 384)
        load_w(cx, st, wuk, dr['wuk'][l], None, 256)
        load_w(cx, st, wuv, dr['wuv'][l], None, 256)
        for b in range(NBK):
            h = hb[b % 2]
            ra = rA[b % 2]
            rb = rB[b % 2]
            tok = slice(b * 512, (b + 1) * 512)
            for t in range(4):
                bi = bk.next()
                pb, pbuf = bk.bank_bf(bi)
                for c in range(8):
                    pg.op('pe', I('transpose',
                        out=pb[:, c * 128:(c + 1) * 128], in_=hn.t[:, t, c * 128:(c + 1) * 128],
                        identity=cx.ident.t[:]),
                        reads=[hn.b, cx.ident.b], writes=[pbuf] if c == 0 else (), pwrites=() if c == 0 else [pbuf])
                pg.op('dve', I('tensor_copy',
                    out=hnT.t[:, :, t * 128:(t + 1) * 128],
                    in_=pb[:, 0:1024].rearrange("p (c k) -> p c k", c=8)),
                    reads=[pbuf], pwrites=[hnT.b])

            def fmm(ci, bi):
                pbk, pbuf = bk.bank(bi)
                M = F_M[ci]
                for c in range(8):
                    pg.op('pe', I('matmul',
                        pbk[0:M, :], lhsT=w1.t[:, c, F_OFF[ci]:F_OFF[ci] + M], rhs=hnT.t[:, c, :],
                        start=(c == 0), stop=(c == 7)),
                        reads=w1r(F_OFF[ci], M) + [hnT.b], writes=[pbuf] if c == 0 else (), pwrites=() if c == 0 else [pbuf])
                return pbk, pbuf

            def fout(fo_t, M, dst):
                _dma(cx, ST_ENG, dst, fo_t.t[0:M, :], reads=[fo_t.b], key='fo_' + fo_t.key)

            rope_jobs = [
                (0, pm32, ra, dr['aqT'][0:128, tok]), (1, pm32, ra, dr['aqT'][128:256, tok]),
                (2, pm32, ra, dr['akT'][0:128, tok]), (3, pm32, ra, dr['akT'][128:256, tok]),
                (4, pm64, rb, dr['bqT'][0:128, tok]), (5, pm64, rb, dr['bqT'][128:256, tok]),
                (6, pm64, rb, dr['bkT'][0:128, tok]),
                (11, pm32, ra, dr['dkpeT'][0:32, tok]),
            ]
            for (cx_i, pm, rt, dst) in rope_jobs:
                M = F_M[cx_i]
                b1 = bk.next()
                b2 = bk.next()
                px, pxb = fmm(cx_i, b1)
                xb_ = xbs[fok % 2]
                pg.op('dve', I('tensor_copy', out=xb_.t[0:M, :], in_=px[0:M, :]),
                      reads=[pxb], writes=[xb_.b])
                pr, prb = bk.bank(b2)
                pg.op('pe', I('matmul', pr[:, :], lhsT=pm.t[:, :], rhs=xb_.t[:, :], start=True, stop=True),
                      reads=[pm.b, xb_.b], writes=[prb])
                ta = t1[fok % 2]
                tb = t2[fok % 2]
                f = fo[fok % 4]
                fok += 1
                pg.op('dve', I('tensor_tensor',
                    out=ta.t[0:M, :], in0=px[0:M, :], in1=rt.t[0:M, 0, :], op=ALU.mult),
                    reads=[pxb, rt.b], writes=[ta.b])
                pg.op('dve', I('tensor_tensor',
                    out=tb.t[0:M, :], in0=pr[0:M, :], in1=rt.t[0:M, 1, :], op=ALU.mult),
                    reads=[prb, rt.b], writes=[tb.b])
                pg.op('pool', I('tensor_tensor',
                    out=f.t[0:M, :], in0=ta.t[0:M, :], in1=tb.t[0:M, :], op=ALU.add),
                    reads=[ta.b, tb.b], writes=[f.b])
                fout(f, M, dst)
            plain_jobs = [(7, 0.125, dr['cqT'][0:128, tok]), (8, 0.125, dr['cqT'][128:256, tok]),
                          (9, 1.0, dr['ckT'][0:128, tok]), (10, 1.0, dr['ckT'][128:256, tok])]
            for (ci, sc, dst) in plain_jobs:
                b1 = bk.next()
                px, pxb = fmm(ci, b1)
                f = fo[fok % 4]
                fok += 1
                pg.op('act', I('activation', out=f.t[:, :], in_=px[:, :], func=AF.Copy,
                                                                      scale=sc),
                      reads=[pxb], writes=[f.b])
                fout(f, 128, dst)

            if b + 1 < NBK:
                prologue(b + 1)
            a1 = av1[b % 2]
            b1v = bv1[b % 2]
            c1 = cv1[b % 2]
            d1 = dv1[b % 2]
            dq = dqs[b % 2]
            dk = dks[b % 2]
            tails = []
            tails_b = []
            for t in range(4):
                tsl = slice(t * 128, (t + 1) * 128)

                def tmm(off, n, bi):
                    pbk, pbuf = bk.bank(bi)
                    for c in range(8):
                        pg.op('pe', I('matmul',
                            pbk[:, 0:n], lhsT=hnT.t[:, c, tsl], rhs=w1.t[:, c, off:off + n],
                            start=(c == 0), stop=(c == 7)),
                            reads=w1r(off, n) + [hnT.b], writes=[pbuf] if c == 0 else (),
                            pwrites=() if c == 0 else [pbuf])
                    return pbk, pbuf
                p0, p0b = tmm(T_AVBV, 384, bk.next())
                pg.op('act', I('activation',
                    out=a1.t[:, t, :, 0:64], in_=p0[:, 0:256].rearrange("p (h d) -> p h d", h=4), func=AF.Copy),
                    reads=[p0b], pwrites=[a1.b])
                pg.op('act', I('activation',
                    out=b1v.t[:, t, :, 0:64], in_=p0[:, 256:384].rearrange("p (h d) -> p h d", h=2), func=AF.Copy),
                    reads=[p0b], pwrites=[b1v.b])
                p1, p1b = tmm(T_CV, 256, bk.next())
                pg.op('act', I('activation',
                    out=c1.t[:, t, :, 0:64], in_=p1[:, 0:256].rearrange("p (h d) -> p h d", h=4), func=AF.Copy),
                    reads=[p1b], pwrites=[c1.b])
                s = sg[sgk % 2]
                sgk += 1
                for n in range(2):
                    p3, p3b = tmm(T_GATE + n * 512, 512, bk.next())
                    pg.op('act', I('activation',
                        out=s.t[:, n * 512:(n + 1) * 512], in_=p3[:, :], func=AF.Silu),
                        reads=[p3b], writes=[s.b] if n == 0 else (), pwrites=() if n == 0 else [s.b])
                _dma(cx, ST_ENG, dr['sg'][b * 512 + t * 128: b * 512 + (t + 1) * 128, :], s.t[:], reads=[s.b],
                     key='sg_' + s.key)
                ms2, r2, cn = ms2_l[t % 2], r2_l[t % 2], cn_l[t % 2]
                p2, p2b = tmm(T_DC, 384, bk.next())
                pg.op('act', I('activation', out=junk.t[:, 0:256], in_=p2[:, 0:256], func=AF.Square,
                                                           scale=1.0 / 16, accum_out=ms2.t[:, 0:1]),
                      reads=[p2b], writes=[ms2.b, junk.b])
                pg.op('act', I('activation', out=junk.t[:, 0:128], in_=p2[:, 256:384], func=AF.Square,
                                                           scale=128 ** -0.5, accum_out=ms2.t[:, 1:2]),
                      reads=[p2b], writes=[junk.b], pwrites=[ms2.b])
                rstd_from_ms(cx, ms2, r2, 2)
                pg.op('dve', I('scalar_tensor_tensor',
                    out=cn.t[:, 0:256], in0=p2[:, 0:256], scalar=r2.t[:, 0:1], in1=gqkv.t[:, 0:256],
                    op0=ALU.mult, op1=ALU.mult), reads=[p2b, r2.b, gqkv.b], writes=[cn.b])
                pg.op('dve', I('scalar_tensor_tensor',
                    out=cn.t[:, 256:384], in0=p2[:, 256:384], scalar=r2.t[:, 1:2], in1=gqkv.t[:, 256:384],
                    op0=ALU.mult, op1=ALU.mult), reads=[p2b, r2.b, gqkv.b], pwrites=[cn.b])
                ta = t1[fok % 2]
                tb = t2[fok % 2]
                fok += 1

                def mla_tail(t=t, tsl=tsl, cn=cn, ta=ta, tb=tb, cnT=cnT_l[t % 2]):
                    bi = bk.next()
                    pb, pbuf = bk.bank_bf(bi)
                    for c in range(3):
                        pg.op('pe', I('transpose',
                            out=pb[:, c * 128:(c + 1) * 128], in_=cn.t[:, c * 128:(c + 1) * 128], identity=cx.ident.t[:]),
                            reads=[cn.b, cx.ident.b], writes=[pbuf] if c == 0 else (), pwrites=() if c == 0 else [pbuf])
                    pg.op('dve', I('tensor_copy',
                        out=cnT.t[:, :, :], in_=pb[:, 0:384].rearrange("p (c k) -> p c k", c=3)),
                        reads=[pbuf], writes=[cnT.b])
                    def tail_b():
                        bq_i, bqr_i, bk_i, bv_i = bk.next(), bk.next(), bk.next(), bk.next()
                        pq, pqb = bk.bank(bq_i)
                        pqr, pqrb = bk.bank(bqr_i)
                        pk, pkb = bk.bank(bk_i)
                        pv, pvb = bk.bank(bv_i)
                        for (pp, ppb, ww) in ((pq, pqb, wuq), (pqr, pqrb, wuqr)):
                            first = True
                            for hh in range(4):
                                for c in range(2):
                                    pg.op('pe', I('matmul',
                                        pp[0:96, hh * 128:(hh + 1) * 128], lhsT=ww.t[:, c, hh * 96:(hh + 1) * 96],
                                        rhs=cnT.t[:, c, :], start=(c == 0), stop=(c == 1)),
                                        reads=[ww.b, cnT.b], writes=[ppb] if first else (), pwrites=() if first else [ppb])
                                    first = False
                        for hh in range(4):
                            pg.op('pe', I('matmul',
                                pk[0:64, hh * 128:(hh + 1) * 128], lhsT=wuk.t[:, hh * 64:(hh + 1) * 64], rhs=cnT.t[:, 2, :],
                                start=True, stop=True),
                                reads=[wuk.b, cnT.b], writes=[pkb] if hh == 0 else (), pwrites=() if hh == 0 else [pkb])
                        pg.op('pe', I('matmul', pv[:, 0:256], lhsT=cnT.t[:, 2, :], rhs=wuv.t[:, :], start=True, stop=True),
                              reads=[wuv.b, cnT.b], writes=[pvb])
                        pg.op('act', I('activation',
                            out=dq.t[0:64, :, tsl], in_=pq[0:64, :].rearrange("p (h k) -> p h k", h=4), func=AF.Copy),
                            reads=[pqb], pwrites=[dq.b])
                        cosb = ra.t[64:96, 0, tsl].unsqueeze(1).to_broadcast([32, 4, 128])
                        sinb = ra.t[64:96, 1, tsl].unsqueeze(1).to_broadcast([32, 4, 128])
                        pg.op('dve', I('tensor_tensor',
                            out=ta.t[64:96, :].rearrange("p (h k) -> p h k", h=4),
                            in0=pq[64:96, :].rearrange("p (h k) -> p h k", h=4), in1=cosb, op=ALU.mult),
                            reads=[pqb, ra.b], writes=[ta.b])
                        pg.op('dve', I('tensor_tensor',
                            out=tb.t[64:96, :].rearrange("p (h k) -> p h k", h=4),
                            in0=pqr[64:96, :].rearrange("p (h k) -> p h k", h=4), in1=sinb, op=ALU.mult),
                            reads=[pqrb, ra.b], writes=[tb.b])
                        pg.op('pool', I('tensor_tensor',
                            out=dq.t[64:96, :, tsl], in0=ta.t[64:96, :].rearrange("p (h k) -> p h k", h=4),
                            in1=tb.t[64:96, :].rearrange("p (h k) -> p h k", h=4), op=ALU.add),
                            reads=[ta.b, tb.b], pwrites=[dq.b])
                        pg.op('act', I('activation',
                            out=dk.t[0:64, :, tsl], in_=pk[0:64, :].rearrange("p (h k) -> p h k", h=4), func=AF.Copy),
                            reads=[pkb], pwrites=[dk.b])
                        pg.op('act', I('activation',
                            out=d1.t[:, t, :, 0:64], in_=pv[:, 0:256].rearrange("p (h d) -> p h d", h=4), func=AF.Copy),
                            reads=[pvb], pwrites=[d1.b])
                    tails_b.append(tail_b)
                tails.append(mla_tail)
                if len(tails_b) > 0:
                    tails_b.pop(0)()
                if len(tails) > 1:
                    tails.pop(0)()
            while tails or tails_b:
                if tails_b:
                    tails_b.pop(0)()
                if tails:
                    tails.pop(0)()
            rows = lambda ap: ap[tok, :].rearrange("(t p) c -> p t c", p=128)
            _dma(cx, ST_ENG, rows(dr['av1']), a1.t[:].rearrange("p t h d -> p t (h d)"), reads=[a1.b], key='st_' + a1.key)
            _dma(cx, ST_ENG, rows(dr['bv1']), b1v.t[:].rearrange("p t h d -> p t (h d)"), reads=[b1v.b], key='st_' + b1v.key)
            _dma(cx, ST_ENG, rows(dr['cv1']), c1.t[:].rearrange("p t h d -> p t (h d)"), reads=[c1.b], key='st_' + c1.key)
            _dma(cx, ST_ENG, rows(dr['dv1']), d1.t[:].rearrange("p t h d -> p t (h d)"), reads=[d1.b], key='st_' + d1.key)
            _dma(cx, ST_ENG, dr['dqT'][:, :, tok].rearrange("h r s -> r h s"), dq.t[0:96, :, :], reads=[dq.b],
                 key='st_' + dq.key)
            _dma(cx, ST_ENG, dr['dkT'][:, :, tok].rearrange("h r s -> r h s"), dk.t[0:64, :, :], reads=[dk.b],
                 key='st_' + dk.key)
        pg.barrier()


def phase_dense(cx, S, l, kind, dr):
    pg = cx.pg
    NT = S // 128
    NQ = S // 512
    lam_init = 0.8 - 0.6 * math.exp(-0.3 * l)
    with contextlib.ExitStack() as st:
        T = lambda name, shape, dt: cx.tile(st, name, shape, dt)
        if kind == 'A':
            nch = 2
            maps = [(a // 4, 32 * (a % 4), 32, a // 2) for a in range(8)]
            scale = 32 ** -0.5
            groups = [(0, 1), (2, 3), (4, 5), (6, 7)]
            ocol = 0
        else:
            nch = 4
            maps = [(h, 0, 96, h) for h in range(4)]
            scale = 96 ** -0.5
            groups = [(0, 1), (2, 3)]
            ocol = 768
        kT = T('kT', [128, nch, S], BF16)
        v1 = T('v1', [128, NT, 4, 65], BF16)
        NCH = min(8, NT)
        TPC = NT // NCH
        kTb = [Buf('kTc%d' % i) for i in range(NCH)]
        v1b = [Buf('v1c%d' % i) for i in range(NCH)]
        vsrc = dr['av1'] if kind == 'A' else dr['dv1']
        for i in range(NCH):
            cs = slice(i * TPC * 128, (i + 1) * TPC * 128)
            if kind == 'A':
                _dma(cx, LD_ENG, kT.t[:, :, cs], dr['akT'][:, cs].rearrange("(c p) s -> p c s", p=128),
                     writes=[kTb[i]], key='kT%d' % i)
            else:
                _dma(cx, LD_ENG, kT.t[0:64, :, cs], dr['dkT'][:, :, cs].rearrange("h r s -> r h s"),
                     pwrites=[kTb[i]], key='kT%d' % i)
                for hh in range(4):
                    _dma(cx, LD_ENG, kT.t[64:96, hh, cs], dr['dkpeT'][:, cs], pwrites=[kTb[i]], key='kT%d' % i)
            t0, t1_ = i * TPC, (i + 1) * TPC
            _dma(cx, LD_ENG, v1.t[:, t0:t1_, :, :].rearrange("p t h d -> p t (h d)"),
                 vsrc[t0 * 128:t1_ * 128, :].rearrange("(t p) c -> p t c", p=128), writes=[v1b[i]], key='v1_%d' % i)
        qT = [T('qT%d' % i, [128, 8 if kind == 'A' else nch, 512], BF16) for i in range(2)]
        if kind == 'A':
            for qq in qT:
                pg.op('pool', I('memset', qq.t[:], 0.0), writes=[qq.b])
        pT = [T('pT%d' % i, [128, 2, 512], BF16) for i in range(3)]
        accS = T('accS', [65, 2, 512], F32)
        rr_ = T('rr', [128, 8], F32)
        o1 = T('o1', [128, 4, 64], F32)
        o2 = T('o2', [128, 4, 64], F32)
        dd = T('dd', [128, 4, 64], F32)
        sq = T('sq', [128, 4, 64], F32)
        ssq = T('ssq', [128, 4], F32)
        rn = T('rn', [128, 4], F32)
        oc = [T('oc%d' % i, [128, 4, 256], F32) for i in range(2)]
        ps = cx.ps
        Sb = [Buf('S0'), Buf('S1')]
        accb = Buf('acc')
        tpb = Buf('tp')
        tp = ps[3][:].rearrange("p a b -> p (a b)").rearrange("p (i c) -> p i c", i=8)
        if kind == 'A':
            lv = T('lv', [128, 128], F32)
            _dma(cx, LD_ENG, lv.t[:], dr['a_lambda'][l:l + 1, :].partition_broadcast(128), writes=[lv.b], key='lv')
            pr2 = T('pr2', [128, 2, 32], F32)
            s2 = T('s2', [128, 2], F32)
            nlam = T('nlam', [128, 1], F32)
            pg.op('dve', I('tensor_tensor', out=pr2.t[:, 0, :], in0=lv.t[:, 0:32], in1=lv.t[:, 32:64], op=ALU.mult),
                  reads=[lv.b], writes=[pr2.b])
            pg.op('dve', I('tensor_tensor', out=pr2.t[:, 1, :], in0=lv.t[:, 64:96], in1=lv.t[:, 96:128], op=ALU.mult),
                  reads=[lv.b], pwrites=[pr2.b])
            pg.op('dve', I('tensor_reduce', out=s2.t[:], in_=pr2.t[:], axis=AX.X, op=ALU.add),
                  reads=[pr2.b], writes=[s2.b])
            pg.op('act', I('activation', out=s2.t[:], in_=s2.t[:], func=AF.Exp), reads=[s2.b], writes=[s2.b])
            pg.op('dve', I('tensor_tensor', out=nlam.t[:], in0=s2.t[:, 1:2], in1=s2.t[:, 0:1], op=ALU.subtract),
                  reads=[s2.b], writes=[nlam.b])
            pg.op('dve', I('tensor_scalar', out=nlam.t[:], in0=nlam.t[:], scalar1=-lam_init, scalar2=None,
                                                   op0=ALU.add), reads=[nlam.b], writes=[nlam.b])
            gs = T('gs', [128, 64], F32)
            _dma(cx, LD_ENG, gs.t[:], dr['a_subln_g'][l:l + 1, :].partition_broadcast(128), writes=[gs.b], key='gs')
            gs4 = T('gs4', [128, 4, 64], F32)
            for q in range(4):
                pg.op('dve', I('tensor_scalar', out=gs4.t[:, q, :], in0=gs.t[:], scalar1=1.0 - lam_init,
                                                            scalar2=None, op0=ALU.mult),
                      reads=[gs.b], pwrites=[gs4.b])

        pending = []
        pending2 = []

        def post(gi, g, ocur, qc, last):
            pg.op('dve', I('tensor_copy', out=accS.t[:, :, :], in_=ps[2][0:65, :, :]), reads=[accb], writes=[accS.b])

            def rest():
                for i in range(2):
                    for qt in range(4):
                        first = (i == 0 and qt == 0)
                        pg.op('pe', I('transpose',
                            out=tp[:, i * 4 + qt, 0:65], in_=accS.t[0:65, i, qt * 128:(qt + 1) * 128],
                            identity=cx.identf.t[0:65, 0:65]),
                            reads=[accS.b, cx.identf.b], writes=[tpb] if first else (), pwrites=() if first else [tpb])
                pg.op('dve', I('reciprocal', out=rr_.t[:], in_=tp[:, :, 64]), reads=[tpb], writes=[rr_.b])
                if kind != 'A':
                    for i in range(2):
                        h = g[i]
                        pg.op('dve', I('tensor_tensor',
                            out=ocur.t[:, :, h * 64:(h + 1) * 64], in0=tp[:, i * 4:(i + 1) * 4, 0:64],
                            in1=rr_.t[:, i * 4:(i + 1) * 4].unsqueeze(2).to_broadcast([128, 4, 64]), op=ALU.mult),
                            reads=[tpb, rr_.b], pwrites=[ocur.b])
                    pending2.append(rest2)
                if kind == 'A':
                    h = gi
                    pg.op('dve', I('tensor_tensor', out=o1.t[:], in0=tp[:, 0:4, 0:64],
                                                           in1=rr_.t[:, 0:4].unsqueeze(2).to_broadcast([128, 4, 64]),
                                                           op=ALU.mult), reads=[tpb, rr_.b], writes=[o1.b])
                    pg.op('dve', I('tensor_tensor', out=o2.t[:], in0=tp[:, 4:8, 0:64],
                                                           in1=rr_.t[:, 4:8].unsqueeze(2).to_broadcast([128, 4, 64]),
                                                           op=ALU.mult), reads=[tpb, rr_.b], writes=[o2.b])
                    pg.op('dve', I('scalar_tensor_tensor', out=dd.t[:], in0=o2.t[:], scalar=nlam.t[:, 0:1],
                                                                  in1=o1.t[:], op0=ALU.mult, op1=ALU.add),
                          reads=[o1.b, o2.b, nlam.b], writes=[dd.b])
                    pg.op('pool', I('tensor_tensor', out=sq.t[:], in0=dd.t[:], in1=dd.t[:], op=ALU.mult),
                          reads=[dd.b], writes=[sq.b])
                    pg.op('dve', I('tensor_reduce', out=ssq.t[:], in_=sq.t[:], axis=AX.X, op=ALU.add),
                          reads=[sq.b], writes=[ssq.b])
                    pg.op('dve', I('tensor_scalar', out=ssq.t[:], in0=ssq.t[:], scalar1=1.0 / 64, scalar2=None,
                                                           op0=ALU.mult), reads=[ssq.b], writes=[ssq.b])
                    pending2.append(rest2)

            def rest2():
                if kind == 'A':
                    h = gi
                    rstd_from_ms(cx, ssq, rn, 4)
                    pg.op('dve', I('tensor_tensor', out=dd.t[:], in0=dd.t[:],
                                                           in1=rn.t[:, 0:4].unsqueeze(2).to_broadcast([128, 4, 64]),
                                                           op=ALU.mult), reads=[dd.b, rn.b], writes=[dd.b])
                    pg.op('pool', I('tensor_tensor', out=ocur.t[:, :, h * 64:(h + 1) * 64], in0=dd.t[:],
                                                                 in1=gs4.t[:], op=ALU.mult),
                          reads=[dd.b, gs4.b], pwrites=[ocur.b])
                if last:
                    _dma(cx, ST_ENG,
                         dr['o'][qc * 512:(qc + 1) * 512, ocol:ocol + 256].rearrange("(t p) c -> p t c", p=128),
                         ocur.t[:], reads=[ocur.b], key='st_' + ocur.key)
            pending.append(rest)

        steps = [(qc, gi, j) for qc in range(NQ) for gi in range(len(groups)) for j in range(NT)]
        NSTEP = len(steps)
        loaded_q = set()

        def load_q(qc):
            if qc in loaded_q or qc >= NQ:
                return
            loaded_q.add(qc)
            q = qT[qc % 2]
            if kind == 'A':
                for a_ in range(8):
                    r0 = a_ * 32
                    po = 32 * (a_ % 4)
                    _dma(cx, LD_ENG, q.t[po:po + 32, a_, :], dr['aqT'][r0:r0 + 32, qc * 512:(qc + 1) * 512],
                         pwrites=[q.b], key='qT%d' % (qc % 2))
            else:
                _dma(cx, LD_ENG, q.t[0:96, :, :], dr['dqT'][:, :, qc * 512:(qc + 1) * 512].rearrange("h r s -> r h s"),
                     writes=[q.b], key='qT%d' % (qc % 2))

        def qk(n):
            qc, gi, j = steps[n]
            load_q(qc)
            q = qT[qc % 2]
            g = groups[gi]
            sb = n % 2
            for i, m in enumerate(g):
                ch, po, K, vh = maps[m]
                if kind == 'A':
                    lhs_ap = kT.t[:, ch, j * 128:(j + 1) * 128]
                    rhs_ap = q.t[:, m, :]
                else:
                    lhs_ap = kT.t[po:po + K, ch, j * 128:(j + 1) * 128]
                    rhs_ap = q.t[po:po + K, ch, :]
                pg.op('pe', I('matmul', ps[sb][:, i, :], lhsT=lhs_ap, rhs=rhs_ap, start=True, stop=True),
                      reads=[kTb[j // TPC], q.b], writes=[Sb[sb]] if i == 0 else (), pwrites=() if i == 0 else [Sb[sb]])

        def ex(n):
            sb = n % 2
            pi = n % 3
            pg.op('act', I('activation', out=pT[pi].t[:, :, :], in_=ps[sb][:, :, :], func=AF.Exp, scale=scale),
                  reads=[Sb[sb]], writes=[pT[pi].b])

        def pv(n):
            qc, gi, j = steps[n]
            g = groups[gi]
            pi = n % 3
            for i, m in enumerate(g):
                ch, po, K, vh = maps[m]
                first = (j == 0 and i == 0)
                pg.op('pe', I('matmul', ps[2][0:65, i, :], lhsT=v1.t[:, j, vh, 0:65], rhs=pT[pi].t[:, i, :],
                              start=(j == 0), stop=(j == NT - 1)),
                      reads=[v1b[j // TPC], pT[pi].b], writes=[accb] if first else (), pwrites=() if first else [accb])

        load_q(0)
        load_q(1)
        for n in range(min(2, NSTEP)):
            qk(n)
            ex(n)
        for n in range(NSTEP):
            qc, gi, j = steps[n]
            if gi == 1 and j == 0:
                load_q(qc + 1)
            if n + 2 < NSTEP:
                qk(n + 2)
            pv(n)
            if n + 2 < NSTEP:
                ex(n + 2)
            if j == min(3, NT - 1) and pending:
                pending.pop(0)()
            if j == min(9, NT - 1) and pending2:
                pending2.pop(0)()
            if j == NT - 1:
                post(gi, groups[gi], oc[qc % 2], qc, gi == len(groups) - 1)
        while pending:
            pending.pop(0)()
        while pending2:
            pending2.pop(0)()
        pg.barrier()


def phase_local(cx, S, l, kind, dr, klists, U):
    pg = cx.pg
    NT = S // 128
    ps = cx.ps
    with contextlib.ExitStack() as st:
        T = lambda name, shape, dt: cx.tile(st, name, shape, dt)
        bk = Banks(cx)
        nkh = 2 if kind == 'B' else 4
        kT = T('kT', [64, nkh, S], BF16)
        qT = T('qT', [64, 4, S], BF16)
        if kind == 'B':
            nvh = 2
            ksrc = dr['bkT'][:, 0:S].rearrange("(g r) s -> r g s", r=64)
            qsrc = dr['bqT'][:, 0:S].rearrange("(h r) s -> r h s", r=64)
            vsrc = dr['bv1']
            scale = 0.125
            ocol = 256
            tsrc = dr['bmask']
        else:
            nvh = 4
            ksrc = dr['ckT'][:, 0:S].rearrange("(h r) s -> r h s", r=64)
            qsrc = dr['cqT'][:, 0:S].rearrange("(h r) s -> r h s", r=64)
            vsrc = dr['cv1']
            scale = 1.0
            ocol = 512
            tsrc = dr['cbias'][l]
        v1 = T('v1', [128, NT, nvh, 65], BF16)
        NCH = min(8, NT)
        TPC = NT // NCH
        kTb = [Buf('kTc%d' % i) for i in range(NCH)]
        qTb = [Buf('qTc%d' % i) for i in range(NCH)]
        v1b = [Buf('v1c%d' % i) for i in range(NCH)]
        for i in range(NCH):
            cs = slice(i * TPC * 128, (i + 1) * TPC * 128)
            _dma(cx, LD_ENG, kT.t[:, :, cs], ksrc[:, :, cs], writes=[kTb[i]], key='kT%d' % i)
            _dma(cx, LD_ENG, qT.t[:, :, cs], qsrc[:, :, cs], writes=[qTb[i]], key='qTl%d' % i)
            t0, t1_ = i * TPC, (i + 1) * TPC
            _dma(cx, LD_ENG, v1.t[:, t0:t1_, :, :].rearrange("p t h d -> p t (h d)"),
                 vsrc[t0 * 128:t1_ * 128, :].rearrange("(t p) c -> p t c", p=128), writes=[v1b[i]], key='v1_%d' % i)
        tb = T('tb', [128, U, 512], F32)
        cx.stg = [T('stg0', [128, 1104], F32), T('stg1', [128, 1104], F32)]
        cx.stg_k = 0
        for u in range(U):
            s = cx.stg[u % 2]
            _dma(cx, LD_ENG, s.t[:, 0:512], tsrc[u], writes=[s.b], key='stg%d' % (u % 2))
            pg.op('act', I('activation', out=tb.t[:, u, :], in_=s.t[:, 0:512], func=AF.Exp), reads=[s.b],
                  pwrites=[tb.b])
        if kind == 'B':
            esink = T('esink', [128, 4], F32)
            _dma(cx, LD_ENG, esink.t[:], dr['b_sink'][l:l + 1, :].partition_broadcast(128), writes=[esink.b], key='esink')
            pg.op('act', I('activation', out=esink.t[:], in_=esink.t[:], func=AF.Exp), reads=[esink.b],
                  writes=[esink.b])
        pT = [T('pT%d' % i, [128, 512], BF16) for i in range(4)]
        pE = [T('pE%d' % i, [128, 512], F32) for i in range(2)]
        den = T('den', [128, 4], F32)
        oc = [T('oc%d' % i, [128, 4, 256], F32) for i in range(2)]
        accB = [Buf('acc0'), Buf('acc1')]

        steps = []
        for m in range(NT):
            kl = klists[m]
            for idx, (j, u) in enumerate(kl):
                steps.append((m, idx, len(kl), j, u))
        LA = 2

        def acc_of(m):
            ab = 4 + (m % 2)
            return ps[ab // 2][:, ab % 2, :].rearrange("p (h d) -> p h d", h=4), accB[m % 2]

        def qk(n):
            m, idx, nk, j, u = steps[n]
            pbk, pbuf = bk.bank(n % 4)
            for h in range(4):
                kh = h // 2 if kind == 'B' else h
                pg.op('pe', I('matmul',
                    pbk[:, h * 128:(h + 1) * 128], lhsT=kT.t[:, kh, j * 128:(j + 1) * 128],
                    rhs=qT.t[:, h, m * 128:(m + 1) * 128], start=(h == 0), stop=(h == 3),
                    skip_group_check=True),
                    reads=[kTb[j // TPC], qTb[m // TPC]], writes=[pbuf] if h == 0 else (), pwrites=() if h == 0 else [pbuf])
            p = pT[n % 4]
            if u is not None:
                pe_ = pE[n % 2]
                pg.op('act', I('activation', out=pe_.t[:], in_=pbk[:, 0:512], func=AF.Exp, scale=scale),
                      reads=[pbuf], writes=[pe_.b])
                pg.op('dve', I('tensor_tensor', out=p.t[:], in0=pe_.t[:], in1=tb.t[:, u, :], op=ALU.mult),
                      reads=[pe_.b, tb.b], writes=[p.b])
            else:
                pg.op('act', I('activation', out=p.t[:], in_=pbk[:, 0:512], func=AF.Exp, scale=scale),
                      reads=[pbuf], writes=[p.b])

        def pv(n):
            m, idx, nk, j, u = steps[n]
            acc_ap, accbuf = acc_of(m)
            p = pT[n % 4]
            for h in range(4):
                vh = h // 2 if kind == 'B' else h
                first = (idx == 0 and h == 0)
                pg.op('pe', I('matmul',
                    acc_ap[:, h, 0:65], lhsT=p.t[:, h * 128:(h + 1) * 128], rhs=v1.t[:, j, vh, 0:65],
                    start=(idx == 0 and h == 0), stop=(idx == nk - 1 and h == 3), skip_group_check=True),
                    reads=[p.b, v1b[j // TPC]], writes=[accbuf] if first else (), pwrites=() if first else [accbuf])
            if idx == nk - 1:
                ocur = oc[(m // 4) % 2]
                if kind == 'B':
                    pg.op('dve', I('tensor_tensor', out=den.t[:], in0=acc_ap[:, :, 64], in1=esink.t[:], op=ALU.add),
                          reads=[accbuf, esink.b], writes=[den.b])
                    pg.op('dve', I('reciprocal', out=den.t[:], in_=den.t[:]), reads=[den.b], writes=[den.b])
                else:
                    pg.op('dve', I('reciprocal', out=den.t[:], in_=acc_ap[:, :, 64]), reads=[accbuf], writes=[den.b])
                pg.op('dve', I('tensor_tensor',
                    out=ocur.t[:, m % 4, :].rearrange("p (h d) -> p h d", h=4), in0=acc_ap[:, :, 0:64],
                    in1=den.t[:, 0:4].unsqueeze(2).to_broadcast([128, 4, 64]), op=ALU.mult),
                    reads=[accbuf, den.b], pwrites=[ocur.b])
                if m % 4 == 3:
                    m0 = m - 3
                    _dma(cx, ST_ENG,
                         dr['o'][m0 * 128:(m0 + 4) * 128, ocol:ocol + 256].rearrange("(t p) c -> p t c", p=128),
                         ocur.t[:], reads=[ocur.b], key='st_' + ocur.key)

        NS_ = len(steps)
        for n in range(min(LA, NS_)):
            qk(n)
        for n in range(NS_):
            if n + LA < NS_:
                qk(n + LA)
            pv(n)
        pg.barrier()


def phase_epi(cx, S, l, last, h_src, p_src, h_dst, dr):
    pg = cx.pg
    NT = S // 128
    with contextlib.ExitStack() as st:
        T = lambda name, shape, dt: cx.tile(st, name, shape, dt)
        bk = Banks(cx)
        cx.stg = [T('stg%d' % i, [128, 1104], F32) for i in range(2)]
        cx.stg_k = 0
        wo = T('wo', [128, 8, 1024], BF16)
        wg = T('wg', [128, 8, 1024], BF16)
        wp = T('wp', [128, 2, 1024], BF16)
        gcol = T('gcol', [128, 8], F32)
        _dma(cx, LD_ENG, gcol.t[:], dr['ple_norm_g'][l], writes=[gcol.b], key='gcol')
        if last:
            fg = T('fg', [128, 1024], F32)
            _dma(cx, LD_ENG, fg.t[:], dr['final_norm_g'][0:1, :].partition_broadcast(128), writes=[fg.b], key='fg')
        NSL = 4
        SL = []
        for k in range(NSL):
            d = {}
            for nm, shp, dt in (('ot', [128, 1024], F32), ('sgt', [128, 1024], F32), ('ht', [128, 1024], F32),
                                ('pt', [128, 256], F32), ('mix', [128, 1024], BF16), ('mixT', [128, 8, 128], BF16),
                                ('h1', [128, 1024], F32), ('hn1', [128, 1024], BF16), ('hn1T', [128, 8, 128], BF16),
                                ('ss', [128, 1], F32), ('rs', [128, 1], F32),
                                ('gsig', [128, 1024], F32), ('pb16', [128, 256], BF16), ('pTt', [128, 2, 128], BF16),
                                ('yt', [128, 1024], F32)):
                if nm == 'yt' and not last:
                    continue
                d[nm] = T('%s%d' % (nm, k), shp, dt)
            SL.append(d)

        def transpose8(src, dstT, n, evac_eng):
            bi = bk.next()
            pb, pbuf = bk.bank_bf(bi)
            for c in range(n):
                pg.op('pe', I('transpose',
                    out=pb[:, c * 128:(c + 1) * 128], in_=src.t[:, c * 128:(c + 1) * 128], identity=cx.ident.t[:]),
                    reads=[src.b, cx.ident.b], writes=[pbuf] if c == 0 else (), pwrites=() if c == 0 else [pbuf])
            if evac_eng == 'act':
                pg.op('act', I('activation',
                    out=dstT.t[:, 0:n, :], in_=pb[:, 0:n * 128].rearrange("p (c k) -> p c k", c=n), func=AF.Copy),
                    reads=[pbuf], writes=[dstT.b])
            else:
                pg.op('dve', I('tensor_copy',
                    out=dstT.t[:, 0:n, :], in_=pb[:, 0:n * 128].rearrange("p (c k) -> p c k", c=n)),
                    reads=[pbuf], writes=[dstT.b])

        def proj(lT, w, nck):
            res = []
            for n in range(2):
                bi = bk.next()
                pbk, pbuf = bk.bank(bi)
                for c in range(nck):
                    pg.op('pe', I('matmul',
                        pbk[:, :], lhsT=lT.t[:, c, :], rhs=w.t[:, c, n * 512:(n + 1) * 512],
                        start=(c == 0), stop=(c == nck - 1)),
                        reads=[lT.b, w.b], writes=[pbuf] if c == 0 else (), pwrites=() if c == 0 else [pbuf])
                res.append((pbk, pbuf))
            return res

        def tile_gen(i):
            k = i % NSL
            d = SL[k]
            rows = slice(i * 128, (i + 1) * 128)
            o_, s_, h_, p_ = d['ot'], d['sgt'], d['ht'], d['pt']
            mix, mixT, h1, hn1, hn1T = d['mix'], d['mixT'], d['h1'], d['hn1'], d['hn1T']
            ss, rs, gsig, pb16, pTt = d['ss'], d['rs'], d['gsig'], d['pb16'], d['pTt']
            tt = gsig
            hh = h1
            y_ = d.get('yt')
            _dma(cx, LD_ENG, o_.t[:], dr['o'][rows, :], writes=[o_.b], key='ot%d' % k)
            _dma(cx, LD_ENG, s_.t[:], dr['sg'][rows, :], writes=[s_.b], key='sgt%d' % k)
            _dma(cx, LD_ENG, h_.t[:], h_src[rows, :], writes=[h_.b], key='ht%d' % k)
            _dma(cx, LD_ENG, p_.t[:], p_src[rows, :], writes=[p_.b], key='pt%d' % k)
            yield
            pg.op('dve', I('tensor_tensor', out=mix.t[:], in0=o_.t[:], in1=s_.t[:], op=ALU.mult),
                  reads=[o_.b, s_.b], writes=[mix.b])
            pg.op('dve', I('tensor_copy', out=pb16.t[:], in_=p_.t[:]), reads=[p_.b], writes=[pb16.b])
            yield
            transpose8(mix, mixT, 8, 'act')
            transpose8(pb16, pTt, 2, 'dve')
            yield
            r = proj(mixT, wo, 8)
            for n in range(2):
                pbk, pbuf = r[n]
                pg.op('dve', I('tensor_tensor',
                    out=h1.t[:, n * 512:(n + 1) * 512], in0=pbk[:, :], in1=h_.t[:, n * 512:(n + 1) * 512], op=ALU.add),
                    reads=[pbuf, h_.b], writes=[h1.b] if n == 0 else (), pwrites=() if n == 0 else [h1.b])
            yield
            pg.op('act', I('activation', out=hn1.t[:], in_=h1.t[:], func=AF.Square, scale=1.0 / 32,
                                                accum_out=ss.t[:, 0:1]), reads=[h1.b], writes=[ss.b, hn1.b])
            rstd_from_ms(cx, ss, rs, 1)
            pg.op('act', I('activation', out=hn1.t[:], in_=h1.t[:], func=AF.Copy, scale=rs.t[:, 0:1]),
                  reads=[h1.b, rs.b], writes=[hn1.b])
            yield
            transpose8(hn1, hn1T, 8, 'dve')
            yield
            r = proj(hn1T, wg, 8)
            for n in range(2):
                pbk, pbuf = r[n]
                pg.op('act', I('activation', out=gsig.t[:, n * 512:(n + 1) * 512], in_=pbk[:, :],
                                                                 func=AF.Sigmoid),
                      reads=[pbuf], writes=[gsig.b] if n == 0 else (), pwrites=() if n == 0 else [gsig.b])
            yield
            r = proj(pTt, wp, 2)
            for n in range(2):
                pbk, pbuf = r[n]
                sl = slice(n * 512, (n + 1) * 512)
                pg.op('dve', I('tensor_tensor', out=tt.t[:, sl], in0=pbk[:, :], in1=gsig.t[:, sl],
                                                                      op=ALU.mult),
                      reads=[pbuf, gsig.b], writes=[gsig.b])
            pg.op('pool', I('tensor_tensor', out=hh.t[:], in0=tt.t[:], in1=h1.t[:], op=ALU.add),
                  reads=[gsig.b, h1.b], writes=[h1.b])
            yield
            if not last:
                _dma(cx, ST_ENG, h_dst[rows, :], hh.t[:], reads=[hh.b], key='st_h2_%d' % k)
            else:
                pg.op('act', I('activation', out=mix.t[:], in_=hh.t[:], func=AF.Square, scale=1.0 / 32,
                                                           accum_out=ss.t[:, 0:1]), reads=[hh.b], writes=[ss.b, mix.b])
                rstd_from_ms(cx, ss, rs, 1)
                pg.op('dve', I('scalar_tensor_tensor',
                    out=y_.t[:], in0=hh.t[:], scalar=rs.t[:, 0:1], in1=fg.t[:], op0=ALU.mult, op1=ALU.mult),
                    reads=[hh.b, rs.b, fg.b], writes=[y_.b])
                _dma(cx, ST_ENG, h_dst[rows, :], y_.t[:], reads=[y_.b], key='st_yt_%d' % k)

        STAG = 1
        active = []
        nxt = 0
        step = 0
        pre = []
        for _ in range(min(NSL, NT)):
            g = tile_gen(len(pre))
            next(g)
            pre.append(g)
        load_w(cx, st, wo, dr['w_out'][l], 8, 1024)
        load_w(cx, st, wg, dr['w_ple_gate'][l], 8, 1024, gcol)
        load_w(cx, st, wp, dr['w_ple_proj'][l], 2, 1024)
        while nxt < NT or active:
            if nxt < NT and step % STAG == 0 and len(active) < NSL:
                active.append(pre[nxt] if nxt < len(pre) else tile_gen(nxt))
                nxt += 1
            for g in reversed(list(active)):
                try:
                    next(g)
                except StopIteration:
                    active.remove(g)
            step += 1
        pg.barrier()


def _rot_idx(base, nheads, hd):
    out = []
    for h in range(nheads):
        for j in range(hd):
            src = j + hd // 2 if j < hd // 2 else j - hd // 2
            out.append(base + h * hd + src)
    return out


def _w1_cols():
    aq = list(range(0, 256)); ak = list(range(256, 512)); av = list(range(512, 768))
    bq = list(range(768, 1024)); bkk = list(range(1024, 1152)); bv = list(range(1152, 1280))
    cq = list(range(1280, 1536)); ck = list(range(1536, 1792)); cv = list(range(1792, 2048))
    dcq = list(range(2048, 2304)); dckv = list(range(2304, 2432)); dkr = list(range(2432, 2464))
    gate = list(range(2464, 3488))
    aqr = _rot_idx(0, 8, 32); akr = _rot_idx(256, 8, 32)
    bqr = _rot_idx(768, 4, 64); bkr = _rot_idx(1024, 2, 64); dkrr = _rot_idx(2432, 1, 32)
    cols = (aq[0:128] + aq[128:] + ak[0:128] + ak[128:] + bq[0:128] + bq[128:] + bkk
            + cq[0:128] + cq[128:] + ck[0:128] + ck[128:] + dkr
            + av + bv + cv + dcq + dckv + gate)
    assert len(cols) == NC1
    return np.array(cols, dtype=np.int64)


def _rope_table(n, hd):
    inv = (1.0 / (np.float32(10000.0) ** (np.arange(0, hd, 2, dtype=np.float32) / np.float32(hd)))).astype(np.float32)
    ang = (np.arange(n, dtype=np.float32)[:, None] * inv[None, :]).astype(np.float32)
    cos = np.cos(ang).astype(np.float32)
    sin = np.sin(ang).astype(np.float32)
    p = np.arange(128)
    fi = p % (hd // 2)
    sign = np.where((p % hd) < hd // 2, -1.0, 1.0).astype(np.float32)
    tab = np.empty((128, 2, n), np.float32)
    tab[:, 0, :] = cos[:, fi].T
    tab[:, 1, :] = sin[:, fi].T * sign[:, None]
    return tab


def _c_patterns(S, pats):
    R = S // 64
    NT = S // 128
    res = []
    ki = np.arange(128)
    qi = np.arange(128)
    for m in range(NT):
        qr = 2 * m + qi // 64
        qcol = qi % 64
        ws = np.clip(qr - 4, 0, R - 8)
        cs = np.clip(qcol - 8, 0, 64 - 16)
        lst = []
        for j in range(max(0, m - 4), min(NT, m + 5)):
            krow = 2 * j + ki // 64
            kcol = ki % 64
            valid = ((krow[:, None] >= ws[None, :]) & (krow[:, None] < ws[None, :] + 8)
                     & (kcol[:, None] >= cs[None, :]) & (kcol[:, None] < cs[None, :] + 16))
            if not valid.any():
                continue
            drr = np.where(valid, krow[:, None] - qr[None, :] + 7, 0).astype(np.int64)
            dcc = np.where(valid, np.clip(kcol[:, None] - qcol[None, :], -15, 15) + 15, 0).astype(np.int64)
            key = (valid.tobytes(), drr.tobytes(), dcc.tobytes())
            if key not in pats:
                pats[key] = (len(pats), valid, drr, dcc)
            lst.append((j, pats[key][0]))
        res.append(lst)
    return res


def _b_klists(S):
    NT = S // 128
    res = []
    for n in range(NT):
        lst = []
        if n >= 1:
            lst.append((n - 1, 0))
        lst.append((n, None))
        if n + 1 < NT:
            lst.append((n + 1, 1))
        res.append(lst)
    return res


_CACHE = {}


def build(SP, SS, DEPTH=2, SMAX=None):
    key = (SP, SS, DEPTH)
    if key in _CACHE:
        return _CACHE[key]
    SMAX = max(SP, SS)
    pats = {}
    ckl = {S: _c_patterns(S, pats) for S in sorted({SP, SS})}
    U = len(pats)
    bkl = {S: _b_klists(S) for S in {SP, SS}}

    nc = bass.Bass("TRN2", target_bir_lowering=False)
    dr = {}

    def din(name, shape, dt=F32):
        dr[name] = nc.dram_tensor(name, list(shape), dt, kind="ExternalInput").ap()

    def dscr(name, shape, dt):
        dr[name] = nc.dram_tensor(name, list(shape), dt, kind="Internal").ap()

    din('xp', [SP, 1024]); din('xs', [SS, 1024])
    din('pp', [DEPTH, SP, 256]); din('ps_', [DEPTH, SS, 256])
    din('w1', [DEPTH, 1024, NC1]); din('norm_g', [DEPTH, 128, 8]); din('ple_norm_g', [DEPTH, 128, 8])
    din('final_norm_g', [1, 1024])
    din('a_lambda', [DEPTH, 128]); din('a_subln_g', [DEPTH, 64]); din('b_sink', [DEPTH, 4])
    din('cbias', [DEPTH, U, 128, 512]); din('bmask', [2, 128, 512])
    din('d_q_norm_g', [DEPTH, 256]); din('d_kv_norm_g', [DEPTH, 128])
    din('wuq', [DEPTH, 256, 384]); din('wuqr', [DEPTH, 256, 384]); din('wuk', [DEPTH, 128, 256]); din('wuv', [DEPTH, 128, 256])
    din('w_out', [DEPTH, 1024, 1024]); din('w_ple_gate', [DEPTH, 1024, 1024]); din('w_ple_proj', [DEPTH, 256, 1024])
    din('rope32', [128, 2, SMAX]); din('rope64', [128, 2, SMAX]); din('perm', [128, 2, 128])
    dr['yp'] = nc.dram_tensor('yp', [SP, 1024], F32, kind="ExternalOutput").ap()
    dr['ys'] = nc.dram_tensor('ys', [SS, 1024], F32, kind="ExternalOutput").ap()
    dscr('aqT', [256, SMAX], BF16); dscr('akT', [256, SMAX], BF16); dscr('av1', [SMAX, 260], BF16)
    dscr('bqT', [256, SMAX], BF16); dscr('bkT', [128, SMAX], BF16); dscr('bv1', [SMAX, 130], BF16)
    dscr('cqT', [256, SMAX], BF16); dscr('ckT', [256, SMAX], BF16); dscr('cv1', [SMAX, 260], BF16)
    dscr('dqT', [4, 96, SMAX], BF16); dscr('dkT', [4, 64, SMAX], BF16); dscr('dkpeT', [32, SMAX], BF16)
    dscr('dv1', [SMAX, 260], BF16)
    dscr('sg', [SMAX, 1024], F32); dscr('o', [SMAX, 1024], F32); dscr('hs', [SMAX, 1024], F32)

    cx = Cx(nc)
    pg = cx.pg
    with contextlib.ExitStack() as st:
        cx.ps = [st.enter_context(nc.psum_tensor('ps%d' % i, [128, 2, 512], F32)) for i in range(4)]
        cx.identf = cx.tile(st, 'identf', [128, 128], F32)
        cx.ident = cx.tile(st, 'ident', [128, 128], BF16)
        cx.epsc = cx.tile(st, 'epsc', [128, 1], F32)
        pg.op('pool', I('memset', cx.identf.t[:], 0.0), writes=[cx.identf.b])
        pg.op('pool', I('affine_select', out=cx.identf.t[:], in_=cx.identf.t[:], pattern=[[-1, 128]],
                                                compare_op=ALU.not_equal, fill=1.0, base=0, channel_multiplier=1),
              reads=[cx.identf.b], writes=[cx.identf.b])
        pg.op('dve', I('tensor_copy', out=cx.ident.t[:], in_=cx.identf.t[:]), reads=[cx.identf.b],
              writes=[cx.ident.b])
        pg.op('pool', I('memset', cx.epsc.t[:], EPS), writes=[cx.epsc.b])
        pg.barrier()
        for (S, xk, pk, yk) in ((SP, 'xp', 'pp', 'yp'), (SS, 'xs', 'ps_', 'ys')):
            for l in range(DEPTH):
                last = (l == DEPTH - 1)
                h_src = dr[xk] if l == 0 else dr['hs']
                if 'P' in PHASES:
                    phase_proj(cx, S, l, h_src, dr)
                if 'A' in PHASES:
                    phase_dense(cx, S, l, 'A', dr)
                if 'D' in PHASES:
                    phase_dense(cx, S, l, 'D', dr)
                if 'B' in PHASES:
                    phase_local(cx, S, l, 'B', dr, bkl[S], 2)
                if 'C' in PHASES:
                    phase_local(cx, S, l, 'C', dr, ckl[S], U)
                if 'E' in PHASES:
                    phase_epi(cx, S, l, last, h_src, dr[pk][l], dr[yk] if last else dr['hs'], dr)
        counts = pg.emit()
    res = (nc, pats, counts)
    _CACHE[key] = res
    return res


def prep_inputs(inputs, SP, SS, DEPTH, pats, core):
    f = lambda a: np.ascontiguousarray(np.asarray(a, dtype=np.float32))
    m = {}
    nb = inputs['x_sample'].shape[0]
    m['xp'] = f(inputs['x_prompt'][core])
    m['xs'] = f(inputs['x_sample'][core % nb])
    m['pp'] = f(inputs['p_prompt'][:, core])
    m['ps_'] = f(inputs['p_sample'][:, core % nb])
    return m


def shared_inputs(inputs, SP, SS, DEPTH, pats):
    f = lambda a: np.ascontiguousarray(np.asarray(a, dtype=np.float32))
    m = {}
    w_in = np.asarray(inputs['w_in'], dtype=np.float32)
    m['w1'] = f(w_in[:, :, _w1_cols()])
    m['norm_g'] = f(np.asarray(inputs['norm_g']).reshape(DEPTH, 8, 128).transpose(0, 2, 1))
    m['ple_norm_g'] = f(np.asarray(inputs['ple_norm_g']).reshape(DEPTH, 8, 128).transpose(0, 2, 1))
    m['final_norm_g'] = f(np.asarray(inputs['final_norm_g']).reshape(1, 1024))
    m['a_lambda'] = f(np.asarray(inputs['a_lambda']).reshape(DEPTH, 128))
    m['a_subln_g'] = f(inputs['a_subln_g'])
    m['b_sink'] = f(inputs['b_sink'])
    rpb = np.asarray(inputs['c_rpb'], dtype=np.float32)
    U = len(pats)
    cb = np.empty((DEPTH, U, 128, 4, 128), np.float32)
    for key, (u, valid, drr, dcc) in pats.items():
        g = rpb[:, :, drr, dcc]
        g = np.where(valid[None, None], g, np.float32(NEG))
        cb[:, u] = g.transpose(0, 2, 1, 3)
    m['cbias'] = f(cb.reshape(DEPTH, U, 128, 512))
    k = np.arange(128)[:, None]
    q = np.arange(128)[None, :]
    left = np.where(q <= k, 0.0, NEG).astype(np.float32)
    right = np.where(k <= q, 0.0, NEG).astype(np.float32)
    m['bmask'] = f(np.stack([np.tile(left, (1, 4)), np.tile(right, (1, 4))]))
    m['d_q_norm_g'] = f(inputs['d_q_norm_g'])
    m['d_kv_norm_g'] = f(inputs['d_kv_norm_g'])
    wuq = np.asarray(inputs['d_w_uq'], dtype=np.float32)
    idx = []
    for h in range(4):
        idx += list(range(h * 96, h * 96 + 64)) + _rot_idx(h * 96 + 64, 1, 32)
    m['wuq'] = f(wuq)
    m['wuqr'] = f(wuq[:, :, np.array(idx)])
    wukv = np.asarray(inputs['d_w_ukv'], dtype=np.float32).reshape(DEPTH, 128, 4, 128)
    m['wuk'] = f(wukv[:, :, :, 0:64].reshape(DEPTH, 128, 256))
    m['wuv'] = f(wukv[:, :, :, 64:128].reshape(DEPTH, 128, 256))
    m['w_out'] = f(inputs['w_out'])
    m['w_ple_gate'] = f(inputs['w_ple_gate'])
    m['w_ple_proj'] = f(inputs['w_ple_proj'])
    SMAX = max(SP, SS)
    pm = np.zeros((128, 2, 128), np.float32)
    for i, hd in enumerate((32, 64)):
        for p in range(128):
            j = p % hd
            src = p - j + (j + hd // 2 if j < hd // 2 else j - hd // 2)
            pm[src, i, p] = 1.0
    m['perm'] = pm
    m['rope32'] = _rope_table(SMAX, 32)
    m['rope64'] = _rope_table(SMAX, 64)
    return m


def kernel(**inputs):
    xp = inputs['x_prompt']
    xs = inputs['x_sample']
    B, SP, _ = xp.shape
    NB, SS, _ = xs.shape
    DEPTH = inputs['w_in'].shape[0]
    ncores = 8
    nc, pats, _ = build(SP, SS, DEPTH)
    sh = shared_inputs(inputs, SP, SS, DEPTH, pats)
    in_maps = []
    for c in range(ncores):
        m = dict(sh)
        m.update(prep_inputs(inputs, SP, SS, DEPTH, pats, c % B))
        in_maps.append(m)
    res = run_bass_kernel_spmd(nc, in_maps, core_ids=list(range(ncores)))
    yp = np.stack([np.asarray(res.results[c]['yp'], dtype=np.float32) for c in range(B)], axis=0)
    ys = np.stack([np.asarray(res.results[c]['ys'], dtype=np.float32) for c in range(NB)], axis=0)
    return (yp, ys)
```

```python
import contextlib
import math

import numpy as np
import concourse.bass as bass
import concourse.mybir as mybir
from concourse.bass_utils import run_bass_kernel_spmd

F32 = mybir.dt.float32
BF16 = mybir.dt.bfloat16
AF = mybir.ActivationFunctionType
ALU = mybir.AluOpType
AX = mybir.AxisListType

ENGS = ['pe', 'act', 'dve', 'pool', 'sp']
ENGOBJ = {'pe': 'tensor', 'act': 'scalar', 'dve': 'vector', 'pool': 'gpsimd', 'sp': 'sync'}

D_MODEL = 1024
EPS = 1e-6
NEG = -1e30
NC1 = 3488
F_OFF = [i * 128 for i in range(11)] + [1408]
F_M = [128] * 11 + [32]
T_AVBV = 1440
T_CV = 1824
T_DC = 2080
T_GATE = 2464
LD_ENG = 'sp'
DEBUG_NAMES = None
PHASES = 'PADBCE'
EMBED_WAIT = True
ST_ENG = 'pool'


def I(name, *args, **kw):
    return (name, args, kw)


class Buf:
    __slots__ = ('name', 'lw', 'rd', 'ord', 'fw')

    def __init__(self, name):
        self.name = name
        self.lw = []
        self.rd = []
        self.ord = []
        self.fw = None


class _Op:
    __slots__ = ('fn', 'waits', 'signal', 'dma_sem')

    def __init__(self, fn):
        self.fn = fn
        self.waits = []
        self.signal = False
        self.dma_sem = None


class Prog:
    def __init__(self, nc):
        self.nc = nc
        self.ops = {e: [] for e in ENGS}
        self.dma_cnt = {}

    def _dep(self, op, eng, ev):
        if ev[0] == 'e':
            if ev[1] == eng and eng == 'pe':
                return
            self.ops[ev[1]][ev[2]].signal = True
        op.waits.append(ev)

    def op(self, eng, fn, reads=(), writes=(), pwrites=(), dma=None):
        o = _Op(fn)
        lst = self.ops[eng]
        idx = len(lst)
        for b in reads:
            for w in b.lw:
                self._dep(o, eng, w)
        for b in writes:
            if not b.rd:
                for w in b.lw:
                    self._dep(o, eng, w)
            for r in b.rd:
                self._dep(o, eng, r)
            for r in b.ord:
                self._dep(o, eng, r)
        for b in pwrites:
            if b.rd:
                b.ord = b.rd
                b.rd = []
                b.lw = []
            for r in b.ord:
                self._dep(o, eng, r)
            if b.fw is not None:
                self._dep(o, eng, b.fw)
        if dma is not None:
            n = self.dma_cnt.get(dma, 0) + 1
            self.dma_cnt[dma] = n
            o.dma_sem = dma
            ev = ('d', dma, 16 * n)
        else:
            ev = ('e', eng, idx)
        for b in writes:
            b.lw = [ev]
            b.rd = []
            b.ord = []
            b.fw = ev
        for b in pwrites:
            b.lw.append(ev)
        for b in reads:
            b.rd.append(ev)
        lst.append(o)
        return ev

    def barrier(self):
        evs = []
        for e in ENGS:
            for i in range(len(self.ops[e]) - 1, -1, -1):
                o = self.ops[e][i]
                if o.dma_sem is None and o.fn is not None:
                    evs.append(('e', e, i))
                    break
        for k, n in self.dma_cnt.items():
            evs.append(('d', k, 16 * n))
        for e in ENGS:
            o = _Op(None)
            for ev in evs:
                if ev[0] == 'e':
                    if ev[1] == e and e == 'pe':
                        continue
                    self.ops[ev[1]][ev[2]].signal = True
                o.waits.append(ev)
            self.ops[e].append(o)

    def emit(self):
        nc = self.nc
        with contextlib.ExitStack() as st:
            esem = {e: st.enter_context(nc.semaphore('s_' + e)) for e in ENGS}
            dsem = {k: st.enter_context(nc.semaphore('d_%s' % (k,))) for k in self.dma_cnt}
            sigidx = {}
            for e in ENGS:
                c = 0
                arr = []
                for o in self.ops[e]:
                    if o.signal:
                        c += 1
                    arr.append(c)
                sigidx[e] = arr
            block = st.enter_context(nc.Block())

            def make(e):
                def body(eng):
                    seen = {}
                    for o in self.ops[e]:
                        need = {}
                        for ev in o.waits:
                            if ev[0] == 'e':
                                key = ('e', ev[1])
                                val = sigidx[ev[1]][ev[2]]
                            else:
                                key = ('d', ev[1])
                                val = ev[2]
                            if need.get(key, 0) < val:
                                need[key] = val
                        todo = []
                        for key, val in need.items():
                            if seen.get(key, 0) >= val:
                                continue
                            seen[key] = val
                            todo.append((esem[key[1]] if key[0] == 'e' else dsem[key[1]], val))
                        emb = None
                        if o.fn is not None and todo and EMBED_WAIT:
                            emb = todo.pop()
                        for sem, val in todo:
                            eng.wait_ge(sem, val)
                        if o.fn is None:
                            continue
                        ins = getattr(eng, o.fn[0])(*o.fn[1], **o.fn[2])
                        if emb is not None:
                            ins._wait_ge(emb[0], emb[1])
                        if DEBUG_NAMES is not None:
                            DEBUG_NAMES[getattr(ins.ins, 'name', None)] = (e, o.fn[0], str(o.fn[1])[:300], str({k: str(v)[:200] for k, v in o.fn[2].items()}))
                        if o.dma_sem is not None:
                            ins.then_inc(dsem[o.dma_sem], 16)
                        elif o.signal:
                            ins.then_inc(esem[e], 1)
                return body
            for e in ENGS:
                getattr(block, ENGOBJ[e])(make(e))
        return {e: len(self.ops[e]) for e in ENGS}


class Tile:
    __slots__ = ('t', 'b', 'key')

    def __init__(self, t, b, key):
        self.t = t
        self.b = b
        self.key = key


class Cx:
    def __init__(self, nc):
        self.nc = nc
        self.pg = Prog(nc)
        self.uid = 0

    def tile(self, st, name, shape, dt):
        self.uid += 1
        nm = '%s_%d' % (name, self.uid)
        t = st.enter_context(self.nc.sbuf_tensor(nm, list(shape), dt))
        return Tile(t, Buf(nm), name)


def _dma(cx, eng, out, in_, reads=(), writes=(), pwrites=(), key=None):
    return cx.pg.op(eng, I('dma_start', out=out, in_=in_), reads=reads, writes=writes,
                    pwrites=pwrites, dma=key)


def load_w(cx, st, dst, src, C, N, gcol=None, tag='w', piece_bufs=None):
    pg = cx.pg
    NS = 1104
    stg = cx.stg
    k = cx.stg_k
    order = [(c, n0) for c in range(C if C is not None else 1) for n0 in range(0, N, NS)]
    if piece_bufs is not None:
        order = [(c, n0) for n0 in range(0, N, NS) for c in range(C)]
    for (c, n0) in order:
        if True:
            w = min(NS, N - n0)
            s = stg[k % len(stg)]
            rows = src[c * 128:(c + 1) * 128, n0:n0 + w]
            si = k % len(stg)
            _dma(cx, LD_ENG, s.t[:, 0:w], rows, writes=[s.b], key='stg%d' % si)
            dview = dst.t[:, c, n0:n0 + w] if C is not None else dst.t[:, n0:n0 + w]
            dbuf = dst.b if piece_bufs is None else piece_bufs[n0 // NS]
            if gcol is not None:
                pg.op('act', I('activation', out=dview, in_=s.t[:, 0:w], func=AF.Copy, scale=gcol.t[:, c:c + 1]),
                      reads=[s.b, gcol.b], pwrites=[dbuf])
            else:
                pg.op('act', I('activation', out=dview, in_=s.t[:, 0:w], func=AF.Copy),
                      reads=[s.b], pwrites=[dbuf])
            k += 1
    cx.stg_k = k


class Banks:
    def __init__(self, cx):
        self.ps = cx.ps
        self.bufs = [Buf('bank%d' % i) for i in range(8)]
        self.pair_bufs = [Buf('pair%d' % i) for i in range(4)]
        self.rr = 0

    def bank(self, i):
        return self.ps[i // 2][:, i % 2, :], self.bufs[i]

    def bank_bf(self, i):
        return self.ps[i // 2][:, i % 2, :].bitcast(BF16), self.bufs[i]

    def next(self, lo=0, hi=8):
        i = lo + self.rr % (hi - lo)
        self.rr += 1
        return i


def rstd_from_ms(cx, ms, out, n, reads_extra=()):
    pg = cx.pg
    pg.op('act', I('activation', out=out.t[:, 0:n], in_=ms.t[:, 0:n], func=AF.Ln, bias=cx.epsc.t[:, 0:1]),
          reads=[ms.b, cx.epsc.b], writes=[out.b])
    pg.op('act', I('activation', out=out.t[:, 0:n], in_=out.t[:, 0:n], func=AF.Exp, scale=-0.5),
          reads=[out.b], writes=[out.b])


def phase_proj(cx, S, l, h_src, dr):
    pg = cx.pg
    nc = cx.nc
    NBK = S // 512
    with contextlib.ExitStack() as st:
        bk = Banks(cx)
        T = lambda name, shape, dt: cx.tile(st, name, shape, dt)
        w1 = T('w1', [128, 8, NC1], BF16)
        gcol = T('gcol', [128, 8], F32)
        _dma(cx, LD_ENG, gcol.t[:], dr['norm_g'][l], writes=[gcol.b], key='gcol')
        cx.stg = [T('stg%d' % i, [128, 1104], F32) for i in range(3)]
        cx.stg_k = 0
        w1p = [Buf('w1p%d' % i) for i in range(4)]

        def w1r(off, n):
            return [w1p[i] for i in range(off // 1104, (off + n - 1) // 1104 + 1)]
        wuq = T('wuq', [128, 2, 384], BF16)
        wuqr = T('wuqr', [128, 2, 384], BF16)
        wuk = T('wuk', [128, 256], BF16)
        wuv = T('wuv', [128, 256], BF16)
        gqkv = T('gqkv', [128, 384], F32)
        _dma(cx, LD_ENG, gqkv.t[:, 0:256], dr['d_q_norm_g'][l:l + 1, :].partition_broadcast(128),
             pwrites=[gqkv.b], key='gqkv')
        _dma(cx, LD_ENG, gqkv.t[:, 256:384], dr['d_kv_norm_g'][l:l + 1, :].partition_broadcast(128),
             pwrites=[gqkv.b], key='gqkv')

        pm32 = T('pm32', [128, 128], BF16)
        pm64 = T('pm64', [128, 128], BF16)
        pmf = T('pmf', [128, 2, 128], F32)
        _dma(cx, LD_ENG, pmf.t[:], dr['perm'][:, :, :], writes=[pmf.b], key='pmf')
        pg.op('dve', I('tensor_copy', out=pm32.t[:], in_=pmf.t[:, 0, :]), reads=[pmf.b], writes=[pm32.b])
        pg.op('dve', I('tensor_copy', out=pm64.t[:], in_=pmf.t[:, 1, :]), reads=[pmf.b], writes=[pm64.b])
        xbs = [T('xb%d' % i, [128, 512], BF16) for i in range(2)]
        for xb_ in xbs:
            pg.op('pool', I('memset', xb_.t[:], 0.0), writes=[xb_.b])
        hb = [T('hb%d' % i, [128, 4, 1024], F32) for i in range(2)]
        rA = [T('rA%d' % i, [128, 2, 512], F32) for i in range(2)]
        rB = [T('rB%d' % i, [128, 2, 512], F32) for i in range(2)]
        hn = T('hn', [128, 4, 1024], BF16)
        hnT = T('hnT', [128, 8, 512], BF16)
        junk = T('junk', [128, 1024], BF16)
        ss = T('ss', [128, 4], F32)
        rstd = T('rstd', [128, 4], F32)
        t1 = [T('t1_%d' % i, [128, 512], F32) for i in range(2)]
        t2 = [T('t2_%d' % i, [128, 512], F32) for i in range(2)]
        fo = [T('fo%d' % i, [128, 512], BF16) for i in range(4)]
        av1 = [T('av1_%d' % i, [128, 4, 4, 65], BF16) for i in range(2)]
        bv1 = [T('bv1_%d' % i, [128, 4, 2, 65], BF16) for i in range(2)]
        cv1 = [T('cv1_%d' % i, [128, 4, 4, 65], BF16) for i in range(2)]
        dv1 = [T('dv1_%d' % i, [128, 4, 4, 65], BF16) for i in range(2)]
        for tl in av1 + bv1 + cv1 + dv1:
            pg.op('pool', I('memset', tl.t[:], 1.0), writes=[tl.b])
        sg = [T('sg%d' % i, [128, 1024], F32) for i in range(2)]
        ms2_l = [T('ms2_%d' % i, [128, 2], F32) for i in range(2)]
        r2_l = [T('r2_%d' % i, [128, 2], F32) for i in range(2)]
        cn_l = [T('cn_%d' % i, [128, 384], BF16) for i in range(2)]
        cnT_l = [T('cnT%d' % i, [128, 3, 128], BF16) for i in range(2)]
        dqs = [T('dqs%d' % i, [128, 4, 512], BF16) for i in range(2)]
        dks = [T('dks%d' % i, [64, 4, 512], BF16) for i in range(2)]
        fok = 0
        sgk = 0

        def prologue(b):
            h = hb[b % 2]
            ra = rA[b % 2]
            rb = rB[b % 2]
            tok = slice(b * 512, (b + 1) * 512)
            _dma(cx, LD_ENG, h.t[:], h_src[tok, :].rearrange("(t p) f -> p t f", p=128), writes=[h.b],
                 key='hb%d' % (b % 2))
            _dma(cx, LD_ENG, ra.t[:], dr['rope32'][:, :, tok], writes=[ra.b], key='rA%d' % (b % 2))
            _dma(cx, LD_ENG, rb.t[:], dr['rope64'][:, :, tok], writes=[rb.b], key='rB%d' % (b % 2))
            for t in range(4):
                pg.op('act', I('activation', out=junk.t[:], in_=h.t[:, t, :], func=AF.Square,
                                                             scale=1.0 / 32, accum_out=ss.t[:, t:t + 1]),
                      reads=[h.b], writes=[junk.b], pwrites=[ss.b])
            rstd_from_ms(cx, ss, rstd, 4)
            for t in range(4):
                pg.op('act', I('activation', out=hn.t[:, t, :], in_=h.t[:, t, :], func=AF.Copy,
                               scale=rstd.t[:, t:t + 1]),
                      reads=[h.b, rstd.b], pwrites=[hn.b])

        prologue(0)
        load_w(cx, st, w1, dr['w1'][l], 8, NC1, gcol, piece_bufs=w1p)
        load_w(cx, st, wuq, dr['wuq'][l], 2, 384)
        load_w(cx, st, wuqr, dr['wuqr'][l], 2, 384)
        load_w(cx, st, wuk, dr['wuk'][l], None, 256)
        load_w(cx, st, wuv, dr['wuv'][l], None, 256)
        for b in range(NBK):
            h = hb[b % 2]
            ra = rA[b % 2]
            rb = rB[b % 2]
            tok = slice(b * 512, (b + 1) * 512)
            for t in range(4):
                bi = bk.next()
                pb, pbuf = bk.bank_bf(bi)
                for c in range(8):
                    pg.op('pe', I('transpose',
                        out=pb[:, c * 128:(c + 1) * 128], in_=hn.t[:, t, c * 128:(c + 1) * 128],
                        identity=cx.ident.t[:]),
                        reads=[hn.b, cx.ident.b], writes=[pbuf] if c == 0 else (), pwrites=() if c == 0 else [pbuf])
                pg.op('dve', I('tensor_copy',
                    out=hnT.t[:, :, t * 128:(t + 1) * 128],
                    in_=pb[:, 0:1024].rearrange("p (c k) -> p c k", c=8)),
                    reads=[pbuf], pwrites=[hnT.b])

            def fmm(ci, bi):
                pbk, pbuf = bk.bank(bi)
                M = F_M[ci]
                for c in range(8):
                    pg.op('pe', I('matmul',
                        pbk[0:M, :], lhsT=w1.t[:, c, F_OFF[ci]:F_OFF[ci] + M], rhs=hnT.t[:, c, :],
                        start=(c == 0), stop=(c == 7)),
                        reads=w1r(F_OFF[ci], M) + [hnT.b], writes=[pbuf] if c == 0 else (), pwrites=() if c == 0 else [pbuf])
                return pbk, pbuf

            def fout(fo_t, M, dst):
                _dma(cx, ST_ENG, dst, fo_t.t[0:M, :], reads=[fo_t.b], key='fo_' + fo_t.key)

            rope_jobs = [
                (0, pm32, ra, dr['aqT'][0:128, tok]), (1, pm32, ra, dr['aqT'][128:256, tok]),
                (2, pm32, ra, dr['akT'][0:128, tok]), (3, pm32, ra, dr['akT'][128:256, tok]),
                (4, pm64, rb, dr['bqT'][0:128, tok]), (5, pm64, rb, dr['bqT'][128:256, tok]),
                (6, pm64, rb, dr['bkT'][0:128, tok]),
                (11, pm32, ra, dr['dkpeT'][0:32, tok]),
            ]
            for (cx_i, pm, rt, dst) in rope_jobs:
                M = F_M[cx_i]
                b1 = bk.next()
                b2 = bk.next()
                px, pxb = fmm(cx_i, b1)
                xb_ = xbs[fok % 2]
                pg.op('dve', I('tensor_copy', out=xb_.t[0:M, :], in_=px[0:M, :]),
                      reads=[pxb], writes=[xb_.b])
                pr, prb = bk.bank(b2)
                pg.op('pe', I('matmul', pr[:, :], lhsT=pm.t[:, :], rhs=xb_.t[:, :], start=True, stop=True),
                      reads=[pm.b, xb_.b], writes=[prb])
                ta = t1[fok % 2]
                tb = t2[fok % 2]
                f = fo[fok % 4]
                fok += 1
                pg.op('dve', I('tensor_tensor',
                    out=ta.t[0:M, :], in0=px[0:M, :], in1=rt.t[0:M, 0, :], op=ALU.mult),
                    reads=[pxb, rt.b], writes=[ta.b])
                pg.op('dve', I('tensor_tensor',
                    out=tb.t[0:M, :], in0=pr[0:M, :], in1=rt.t[0:M, 1, :], op=ALU.mult),
                    reads=[prb, rt.b], writes=[tb.b])
                pg.op('pool', I('tensor_tensor',
                    out=f.t[0:M, :], in0=ta.t[0:M, :], in1=tb.t[0:M, :], op=ALU.add),
                    reads=[ta.b, tb.b], writes=[f.b])
                fout(f, M, dst)
            plain_jobs = [(7, 0.125, dr['cqT'][0:128, tok]), (8, 0.125, dr['cqT'][128:256, tok]),
                          (9, 1.0, dr['ckT'][0:128, tok]), (10, 1.0, dr['ckT'][128:256, tok])]
            for (ci, sc, dst) in plain_jobs:
                b1 = bk.next()
                px, pxb = fmm(ci, b1)
                f = fo[fok % 4]
                fok += 1
                pg.op('act', I('activation', out=f.t[:, :], in_=px[:, :], func=AF.Copy,
                                                                      scale=sc),
                      reads=[pxb], writes=[f.b])
                fout(f, 128, dst)

            if b + 1 < NBK:
                prologue(b + 1)
            a1 = av1[b % 2]
            b1v = bv1[b % 2]
            c1 = cv1[b % 2]
            d1 = dv1[b % 2]
            dq = dqs[b % 2]
            dk = dks[b % 2]
            tails = []
            tails_b = []
            for t in range(4):
                tsl = slice(t * 128, (t + 1) * 128)

                def tmm(off, n, bi):
                    pbk, pbuf = bk.bank(bi)
                    for c in range(8):
                        pg.op('pe', I('matmul',
                            pbk[:, 0:n], lhsT=hnT.t[:, c, tsl], rhs=w1.t[:, c, off:off + n],
                            start=(c == 0), stop=(c == 7)),
                            reads=w1r(off, n) + [hnT.b], writes=[pbuf] if c == 0 else (),
                            pwrites=() if c == 0 else [pbuf])
                    return pbk, pbuf
                p0, p0b = tmm(T_AVBV, 384, bk.next())
                pg.op('act', I('activation',
                    out=a1.t[:, t, :, 0:64], in_=p0[:, 0:256].rearrange("p (h d) -> p h d", h=4), func=AF.Copy),
                    reads=[p0b], pwrites=[a1.b])
                pg.op('act', I('activation',
                    out=b1v.t[:, t, :, 0:64], in_=p0[:, 256:384].rearrange("p (h d) -> p h d", h=2), func=AF.Copy),
                    reads=[p0b], pwrites=[b1v.b])
                p1, p1b = tmm(T_CV, 256, bk.next())
                pg.op('act', I('activation',
                    out=c1.t[:, t, :, 0:64], in_=p1[:, 0:256].rearrange("p (h d) -> p h d", h=4), func=AF.Copy),
                    reads=[p1b], pwrites=[c1.b])
                s = sg[sgk % 2]
                sgk += 1
                for n in range(2):
                    p3, p3b = tmm(T_GATE + n * 512, 512, bk.next())
                    pg.op('act', I('activation',
                        out=s.t[:, n * 512:(n + 1) * 512], in_=p3[:, :], func=AF.Silu),
                        reads=[p3b], writes=[s.b] if n == 0 else (), pwrites=() if n == 0 else [s.b])
                _dma(cx, ST_ENG, dr['sg'][b * 512 + t * 128: b * 512 + (t + 1) * 128, :], s.t[:], reads=[s.b],
                     key='sg_' + s.key)
                ms2, r2, cn = ms2_l[t % 2], r2_l[t % 2], cn_l[t % 2]
                p2, p2b = tmm(T_DC, 384, bk.next())
                pg.op('act', I('activation', out=junk.t[:, 0:256], in_=p2[:, 0:256], func=AF.Square,
                                                           scale=1.0 / 16, accum_out=ms2.t[:, 0:1]),
                      reads=[p2b], writes=[ms2.b, junk.b])
                pg.op('act', I('activation', out=junk.t[:, 0:128], in_=p2[:, 256:384], func=AF.Square,
                                                           scale=128 ** -0.5, accum_out=ms2.t[:, 1:2]),
                      reads=[p2b], writes=[junk.b], pwrites=[ms2.b])
                rstd_from_ms(cx, ms2, r2, 2)
                pg.op('dve', I('scalar_tensor_tensor',
                    out=cn.t[:, 0:256], in0=p2[:, 0:256], scalar=r2.t[:, 0:1], in1=gqkv.t[:, 0:256],
                    op0=ALU.mult, op1=ALU.mult), reads=[p2b, r2.b, gqkv.b], writes=[cn.b])
                pg.op('dve', I('scalar_tensor_tensor',
                    out=cn.t[:, 256:384], in0=p2[:, 256:384], scalar=r2.t[:, 1:2], in1=gqkv.t[:, 256:384],
                    op0=ALU.mult, op1=ALU.mult), reads=[p2b, r2.b, gqkv.b], pwrites=[cn.b])
                ta = t1[fok % 2]
                tb = t2[fok % 2]
                fok += 1

                def mla_tail(t=t, tsl=tsl, cn=cn, ta=ta, tb=tb, cnT=cnT_l[t % 2]):
                    bi = bk.next()
                    pb, pbuf = bk.bank_bf(bi)
                    for c in range(3):
                        pg.op('pe', I('transpose',
                            out=pb[:, c * 128:(c + 1) * 128], in_=cn.t[:, c * 128:(c + 1) * 128], identity=cx.ident.t[:]),
                            reads=[cn.b, cx.ident.b], writes=[pbuf] if c == 0 else (), pwrites=() if c == 0 else [pbuf])
                    pg.op('dve', I('tensor_copy',
                        out=cnT.t[:, :, :], in_=pb[:, 0:384].rearrange("p (c k) -> p c k", c=3)),
                        reads=[pbuf], writes=[cnT.b])
                    def tail_b():
                        bq_i, bqr_i, bk_i, bv_i = bk.next(), bk.next(), bk.next(), bk.next()
                        pq, pqb = bk.bank(bq_i)
                        pqr, pqrb = bk.bank(bqr_i)
                        pk, pkb = bk.bank(bk_i)
                        pv, pvb = bk.bank(bv_i)
                        for (pp, ppb, ww) in ((pq, pqb, wuq), (pqr, pqrb, wuqr)):
                            first = True
                            for hh in range(4):
                                for c in range(2):
                                    pg.op('pe', I('matmul',
                                        pp[0:96, hh * 128:(hh + 1) * 128], lhsT=ww.t[:, c, hh * 96:(hh + 1) * 96],
                                        rhs=cnT.t[:, c, :], start=(c == 0), stop=(c == 1)),
                                        reads=[ww.b, cnT.b], writes=[ppb] if first else (), pwrites=() if first else [ppb])
                                    first = False
                        for hh in range(4):
                            pg.op('pe', I('matmul',
                                pk[0:64, hh * 128:(hh + 1) * 128], lhsT=wuk.t[:, hh * 64:(hh + 1) * 64], rhs=cnT.t[:, 2, :],
                                start=True, stop=True),
                                reads=[wuk.b, cnT.b], writes=[pkb] if hh == 0 else (), pwrites=() if hh == 0 else [pkb])
                        pg.op('pe', I('matmul', pv[:, 0:256], lhsT=cnT.t[:, 2, :], rhs=wuv.t[:, :], start=True, stop=True),
                              reads=[wuv.b, cnT.b], writes=[pvb])
                        pg.op('act', I('activation',
                            out=dq.t[0:64, :, tsl], in_=pq[0:64, :].rearrange("p (h k) -> p h k", h=4), func=AF.Copy),
                            reads=[pqb], pwrites=[dq.b])
                        cosb = ra.t[64:96, 0, tsl].unsqueeze(1).to_broadcast([32, 4, 128])
                        sinb = ra.t[64:96, 1, tsl].unsqueeze(1).to_broadcast([32, 4, 128])
                        pg.op('dve', I('tensor_tensor',
                            out=ta.t[64:96, :].rearrange("p (h k) -> p h k", h=4),
                            in0=pq[64:96, :].rearrange("p (h k) -> p h k", h=4), in1=cosb, op=ALU.mult),
                            reads=[pqb, ra.b], writes=[ta.b])
                        pg.op('dve', I('tensor_tensor',
                            out=tb.t[64:96, :].rearrange("p (h k) -> p h k", h=4),
                            in0=pqr[64:96, :].rearrange("p (h k) -> p h k", h=4), in1=sinb, op=ALU.mult),
                            reads=[pqrb, ra.b], writes=[tb.b])
                        pg.op('pool', I('tensor_tensor',
                            out=dq.t[64:96, :, tsl], in0=ta.t[64:96, :].rearrange("p (h k) -> p h k", h=4),
                            in1=tb.t[64:96, :].rearrange("p (h k) -> p h k", h=4), op=ALU.add),
                            reads=[ta.b, tb.b], pwrites=[dq.b])
                        pg.op('act', I('activation',
                            out=dk.t[0:64, :, tsl], in_=pk[0:64, :].rearrange("p (h k) -> p h k", h=4), func=AF.Copy),
                            reads=[pkb], pwrites=[dk.b])
                        pg.op('act', I('activation',
                            out=d1.t[:, t, :, 0:64], in_=pv[:, 0:256].rearrange("p (h d) -> p h d", h=4), func=AF.Copy),
                            reads=[pvb], pwrites=[d1.b])
                    tails_b.append(tail_b)
                tails.append(mla_tail)
                if len(tails_b) > 0:
                    tails_b.pop(0)()
                if len(tails) > 1:
                    tails.pop(0)()
            while tails or tails_b:
                if tails_b:
                    tails_b.pop(0)()
                if tails:
                    tails.pop(0)()
            rows = lambda ap: ap[tok, :].rearrange("(t p) c -> p t c", p=128)
            _dma(cx, ST_ENG, rows(dr['av1']), a1.t[:].rearrange("p t h d -> p t (h d)"), reads=[a1.b], key='st_' + a1.key)
            _dma(cx, ST_ENG, rows(dr['bv1']), b1v.t[:].rearrange("p t h d -> p t (h d)"), reads=[b1v.b], key='st_' + b1v.key)
            _dma(cx, ST_ENG, rows(dr['cv1']), c1.t[:].rearrange("p t h d -> p t (h d)"), reads=[c1.b], key='st_' + c1.key)
            _dma(cx, ST_ENG, rows(dr['dv1']), d1.t[:].rearrange("p t h d -> p t (h d)"), reads=[d1.b], key='st_' + d1.key)
            _dma(cx, ST_ENG, dr['dqT'][:, :, tok].rearrange("h r s -> r h s"), dq.t[0:96, :, :], reads=[dq.b],
                 key='st_' + dq.key)
            _dma(cx, ST_ENG, dr['dkT'][:, :, tok].rearrange("h r s -> r h s"), dk.t[0:64, :, :], reads=[dk.b],
                 key='st_' + dk.key)
        pg.barrier()


def phase_dense(cx, S, l, kind, dr):
    pg = cx.pg
    NT = S // 128
    NQ = S // 512
    lam_init = 0.8 - 0.6 * math.exp(-0.3 * l)
    with contextlib.ExitStack() as st:
        T = lambda name, shape, dt: cx.tile(st, name, shape, dt)
        if kind == 'A':
            nch = 2
            maps = [(a // 4, 32 * (a % 4), 32, a // 2) for a in range(8)]
            scale = 32 ** -0.5
            groups = [(0, 1), (2, 3), (4, 5), (6, 7)]
            ocol = 0
        else:
            nch = 4
            maps = [(h, 0, 96, h) for h in range(4)]
            scale = 96 ** -0.5
            groups = [(0, 1), (2, 3)]
            ocol = 768
        kT = T('kT', [128, nch, S], BF16)
        v1 = T('v1', [128, NT, 4, 65], BF16)
        NCH = min(8, NT)
        TPC = NT // NCH
        kTb = [Buf('kTc%d' % i) for i in range(NCH)]
        v1b = [Buf('v1c%d' % i) for i in range(NCH)]
        vsrc = dr['av1'] if kind == 'A' else dr['dv1']
        for i in range(NCH):
            cs = slice(i * TPC * 128, (i + 1) * TPC * 128)
            if kind == 'A':
                _dma(cx, LD_ENG, kT.t[:, :, cs], dr['akT'][:, cs].rearrange("(c p) s -> p c s", p=128),
                     writes=[kTb[i]], key='kT%d' % i)
            else:
                _dma(cx, LD_ENG, kT.t[0:64, :, cs], dr['dkT'][:, :, cs].rearrange("h r s -> r h s"),
                     pwrites=[kTb[i]], key='kT%d' % i)
                for hh in range(4):
                    _dma(cx, LD_ENG, kT.t[64:96, hh, cs], dr['dkpeT'][:, cs], pwrites=[kTb[i]], key='kT%d' % i)
            t0, t1_ = i * TPC, (i + 1) * TPC
            _dma(cx, LD_ENG, v1.t[:, t0:t1_, :, :].rearrange("p t h d -> p t (h d)"),
                 vsrc[t0 * 128:t1_ * 128, :].rearrange("(t p) c -> p t c", p=128), writes=[v1b[i]], key='v1_%d' % i)
        qT = [T('qT%d' % i, [128, 8 if kind == 'A' else nch, 512], BF16) for i in range(2)]
        if kind == 'A':
            for qq in qT:
                pg.op('pool', I('memset', qq.t[:], 0.0), writes=[qq.b])
        pT = [T('pT%d' % i, [128, 2, 512], BF16) for i in range(3)]
        accS = T('accS', [65, 2, 512], F32)
        rr_ = T('rr', [128, 8], F32)
        o1 = T('o1', [128, 4, 64], F32)
        o2 = T('o2', [128, 4, 64], F32)
        dd = T('dd', [128, 4, 64], F32)
        sq = T('sq', [128, 4, 64], F32)
        ssq = T('ssq', [128, 4], F32)
        rn = T('rn', [128, 4], F32)
        oc = [T('oc%d' % i, [128, 4, 256], F32) for i in range(2)]
        ps = cx.ps
        Sb = [Buf('S0'), Buf('S1')]
        accb = Buf('acc')
        tpb = Buf('tp')
        tp = ps[3][:].rearrange("p a b -> p (a b)").rearrange("p (i c) -> p i c", i=8)
        if kind == 'A':
            lv = T('lv', [128, 128], F32)
            _dma(cx, LD_ENG, lv.t[:], dr['a_lambda'][l:l + 1, :].partition_broadcast(128), writes=[lv.b], key='lv')
            pr2 = T('pr2', [128, 2, 32], F32)
            s2 = T('s2', [128, 2], F32)
            nlam = T('nlam', [128, 1], F32)
            pg.op('dve', I('tensor_tensor', out=pr2.t[:, 0, :], in0=lv.t[:, 0:32], in1=lv.t[:, 32:64], op=ALU.mult),
                  reads=[lv.b], writes=[pr2.b])
            pg.op('dve', I('tensor_tensor', out=pr2.t[:, 1, :], in0=lv.t[:, 64:96], in1=lv.t[:, 96:128], op=ALU.mult),
                  reads=[lv.b], pwrites=[pr2.b])
            pg.op('dve', I('tensor_reduce', out=s2.t[:], in_=pr2.t[:], axis=AX.X, op=ALU.add),
                  reads=[pr2.b], writes=[s2.b])
            pg.op('act', I('activation', out=s2.t[:], in_=s2.t[:], func=AF.Exp), reads=[s2.b], writes=[s2.b])
            pg.op('dve', I('tensor_tensor', out=nlam.t[:], in0=s2.t[:, 1:2], in1=s2.t[:, 0:1], op=ALU.subtract),
                  reads=[s2.b], writes=[nlam.b])
            pg.op('dve', I('tensor_scalar', out=nlam.t[:], in0=nlam.t[:], scalar1=-lam_init, scalar2=None,
                                                   op0=ALU.add), reads=[nlam.b], writes=[nlam.b])
            gs = T('gs', [128, 64], F32)
            _dma(cx, LD_ENG, gs.t[:], dr['a_subln_g'][l:l + 1, :].partition_broadcast(128), writes=[gs.b], key='gs')
            gs4 = T('gs4', [128, 4, 64], F32)
            for q in range(4):
                pg.op('dve', I('tensor_scalar', out=gs4.t[:, q, :], in0=gs.t[:], scalar1=1.0 - lam_init,
                                                            scalar2=None, op0=ALU.mult),
                      reads=[gs.b], pwrites=[gs4.b])

        pending = []
        pending2 = []

        def post(gi, g, ocur, qc, last):
            pg.op('dve', I('tensor_copy', out=accS.t[:, :, :], in_=ps[2][0:65, :, :]), reads=[accb], writes=[accS.b])

            def rest():
                for i in range(2):
                    for qt in range(4):
                        first = (i == 0 and qt == 0)
                        pg.op('pe', I('transpose',
                            out=tp[:, i * 4 + qt, 0:65], in_=accS.t[0:65, i, qt * 128:(qt + 1) * 128],
                            identity=cx.identf.t[0:65, 0:65]),
                            reads=[accS.b, cx.identf.b], writes=[tpb] if first else (), pwrites=() if first else [tpb])
                pg.op('dve', I('reciprocal', out=rr_.t[:], in_=tp[:, :, 64]), reads=[tpb], writes=[rr_.b])
                if kind != 'A':
                    for i in range(2):
                        h = g[i]
                        pg.op('dve', I('tensor_tensor',
                            out=ocur.t[:, :, h * 64:(h + 1) * 64], in0=tp[:, i * 4:(i + 1) * 4, 0:64],
                            in1=rr_.t[:, i * 4:(i + 1) * 4].unsqueeze(2).to_broadcast([128, 4, 64]), op=ALU.mult),
                            reads=[tpb, rr_.b], pwrites=[ocur.b])
                    pending2.append(rest2)
                if kind == 'A':
                    h = gi
                    pg.op('dve', I('tensor_tensor', out=o1.t[:], in0=tp[:, 0:4, 0:64],
                                                           in1=rr_.t[:, 0:4].unsqueeze(2).to_broadcast([128, 4, 64]),
                                                           op=ALU.mult), reads=[tpb, rr_.b], writes=[o1.b])
                    pg.op('dve', I('tensor_tensor', out=o2.t[:], in0=tp[:, 4:8, 0:64],
                                                           in1=rr_.t[:, 4:8].unsqueeze(2).to_broadcast([128, 4, 64]),
                                                           op=ALU.mult), reads=[tpb, rr_.b], writes=[o2.b])
                    pg.op('dve', I('scalar_tensor_tensor', out=dd.t[:], in0=o2.t[:], scalar=nlam.t[:, 0:1],
                                                                  in1=o1.t[:], op0=ALU.mult, op1=ALU.add),
                          reads=[o1.b, o2.b, nlam.b], writes=[dd.b])
                    pg.op('pool', I('tensor_tensor', out=sq.t[:], in0=dd.t[:], in1=dd.t[:], op=ALU.mult),
                          reads=[dd.b], writes=[sq.b])
                    pg.op('dve', I('tensor_reduce', out=ssq.t[:], in_=sq.t[:], axis=AX.X, op=ALU.add),
                          reads=[sq.b], writes=[ssq.b])
                    pg.op('dve', I('tensor_scalar', out=ssq.t[:], in0=ssq.t[:], scalar1=1.0 / 64, scalar2=None,
                                                           op0=ALU.mult), reads=[ssq.b], writes=[ssq.b])
                    pending2.append(rest2)

            def rest2():
                if kind == 'A':
                    h = gi
                    rstd_from_ms(cx, ssq, rn, 4)
                    pg.op('dve', I('tensor_tensor', out=dd.t[:], in0=dd.t[:],
                                                           in1=rn.t[:, 0:4].unsqueeze(2).to_broadcast([128, 4, 64]),
                                                           op=ALU.mult), reads=[dd.b, rn.b], writes=[dd.b])
                    pg.op('pool', I('tensor_tensor', out=ocur.t[:, :, h * 64:(h + 1) * 64], in0=dd.t[:],
                                                                 in1=gs4.t[:], op=ALU.mult),
                          reads=[dd.b, gs4.b], pwrites=[ocur.b])
                if last:
                    _dma(cx, ST_ENG,
                         dr['o'][qc * 512:(qc + 1) * 512, ocol:ocol + 256].rearrange("(t p) c -> p t c", p=128),
                         ocur.t[:], reads=[ocur.b], key='st_' + ocur.key)
            pending.append(rest)

        steps = [(qc, gi, j) for qc in range(NQ) for gi in range(len(groups)) for j in range(NT)]
        NSTEP = len(steps)
        loaded_q = set()

        def load_q(qc):
            if qc in loaded_q or qc >= NQ:
                return
            loaded_q.add(qc)
            q = qT[qc % 2]
            if kind == 'A':
                for a_ in range(8):
                    r0 = a_ * 32
                    po = 32 * (a_ % 4)
                    _dma(cx, LD_ENG, q.t[po:po + 32, a_, :], dr['aqT'][r0:r0 + 32, qc * 512:(qc + 1) * 512],
                         pwrites=[q.b], key='qT%d' % (qc % 2))
            else:
                _dma(cx, LD_ENG, q.t[0:96, :, :], dr['dqT'][:, :, qc * 512:(qc + 1) * 512].rearrange("h r s -> r h s"),
                     writes=[q.b], key='qT%d' % (qc % 2))

        def qk(n):
            qc, gi, j = steps[n]
            load_q(qc)
            q = qT[qc % 2]
            g = groups[gi]
            sb = n % 2
            for i, m in enumerate(g):
                ch, po, K, vh = maps[m]
                if kind == 'A':
                    lhs_ap = kT.t[:, ch, j * 128:(j + 1) * 128]
                    rhs_ap = q.t[:, m, :]
                else:
                    lhs_ap = kT.t[po:po + K, ch, j * 128:(j + 1) * 128]
                    rhs_ap = q.t[po:po + K, ch, :]
                pg.op('pe', I('matmul', ps[sb][:, i, :], lhsT=lhs_ap, rhs=rhs_ap, start=True, stop=True),
                      reads=[kTb[j // TPC], q.b], writes=[Sb[sb]] if i == 0 else (), pwrites=() if i == 0 else [Sb[sb]])

        def ex(n):
            sb = n % 2
            pi = n % 3
            pg.op('act', I('activation', out=pT[pi].t[:, :, :], in_=ps[sb][:, :, :], func=AF.Exp, scale=scale),
                  reads=[Sb[sb]], writes=[pT[pi].b])

        def pv(n):
            qc, gi, j = steps[n]
            g = groups[gi]
            pi = n % 3
            for i, m in enumerate(g):
                ch, po, K, vh = maps[m]
                first = (j == 0 and i == 0)
                pg.op('pe', I('matmul', ps[2][0:65, i, :], lhsT=v1.t[:, j, vh, 0:65], rhs=pT[pi].t[:, i, :],
                              start=(j == 0), stop=(j == NT - 1)),
                      reads=[v1b[j // TPC], pT[pi].b], writes=[accb] if first else (), pwrites=() if first else [accb])

        load_q(0)
        load_q(1)
        for n in range(min(2, NSTEP)):
            qk(n)
            ex(n)
        for n in range(NSTEP):
            qc, gi, j = steps[n]
            if gi == 1 and j == 0:
                load_q(qc + 1)
            if n + 2 < NSTEP:
                qk(n + 2)
            pv(n)
            if n + 2 < NSTEP:
                ex(n + 2)
            if j == min(3, NT - 1) and pending:
                pending.pop(0)()
            if j == min(9, NT - 1) and pending2:
                pending2.pop(0)()
            if j == NT - 1:
                post(gi, groups[gi], oc[qc % 2], qc, gi == len(groups) - 1)
        while pending:
            pending.pop(0)()
        while pending2:
            pending2.pop(0)()
        pg.barrier()


def phase_local(cx, S, l, kind, dr, klists, U):
    pg = cx.pg
    NT = S // 128
    ps = cx.ps
    with contextlib.ExitStack() as st:
        T = lambda name, shape, dt: cx.tile(st, name, shape, dt)
        bk = Banks(cx)
        nkh = 2 if kind == 'B' else 4
        kT = T('kT', [64, nkh, S], BF16)
        qT = T('qT', [64, 4, S], BF16)
        if kind == 'B':
            nvh = 2
            ksrc = dr['bkT'][:, 0:S].rearrange("(g r) s -> r g s", r=64)
            qsrc = dr['bqT'][:, 0:S].rearrange("(h r) s -> r h s", r=64)
            vsrc = dr['bv1']
            scale = 0.125
            ocol = 256
            tsrc = dr['bmask']
        else:
            nvh = 4
            ksrc = dr['ckT'][:, 0:S].rearrange("(h r) s -> r h s", r=64)
            qsrc = dr['cqT'][:, 0:S].rearrange("(h r) s -> r h s", r=64)
            vsrc = dr['cv1']
            scale = 1.0
            ocol = 512
            tsrc = dr['cbias'][l]
        v1 = T('v1', [128, NT, nvh, 65], BF16)
        NCH = min(8, NT)
        TPC = NT // NCH
        kTb = [Buf('kTc%d' % i) for i in range(NCH)]
        qTb = [Buf('qTc%d' % i) for i in range(NCH)]
        v1b = [Buf('v1c%d' % i) for i in range(NCH)]
        for i in range(NCH):
            cs = slice(i * TPC * 128, (i + 1) * TPC * 128)
            _dma(cx, LD_ENG, kT.t[:, :, cs], ksrc[:, :, cs], writes=[kTb[i]], key='kT%d' % i)
            _dma(cx, LD_ENG, qT.t[:, :, cs], qsrc[:, :, cs], writes=[qTb[i]], key='qTl%d' % i)
            t0, t1_ = i * TPC, (i + 1) * TPC
            _dma(cx, LD_ENG, v1.t[:, t0:t1_, :, :].rearrange("p t h d -> p t (h d)"),
                 vsrc[t0 * 128:t1_ * 128, :].rearrange("(t p) c -> p t c", p=128), writes=[v1b[i]], key='v1_%d' % i)
        tb = T('tb', [128, U, 512], F32)
        cx.stg = [T('stg0', [128, 1104], F32), T('stg1', [128, 1104], F32)]
        cx.stg_k = 0
        for u in range(U):
            s = cx.stg[u % 2]
            _dma(cx, LD_ENG, s.t[:, 0:512], tsrc[u], writes=[s.b], key='stg%d' % (u % 2))
            pg.op('act', I('activation', out=tb.t[:, u, :], in_=s.t[:, 0:512], func=AF.Exp), reads=[s.b],
                  pwrites=[tb.b])
        if kind == 'B':
            esink = T('esink', [128, 4], F32)
            _dma(cx, LD_ENG, esink.t[:], dr['b_sink'][l:l + 1, :].partition_broadcast(128), writes=[esink.b], key='esink')
            pg.op('act', I('activation', out=esink.t[:], in_=esink.t[:], func=AF.Exp), reads=[esink.b],
                  writes=[esink.b])
        pT = [T('pT%d' % i, [128, 512], BF16) for i in range(4)]
        pE = [T('pE%d' % i, [128, 512], F32) for i in range(2)]
        den = T('den', [128, 4], F32)
        oc = [T('oc%d' % i, [128, 4, 256], F32) for i in range(2)]
        accB = [Buf('acc0'), Buf('acc1')]

        steps = []
        for m in range(NT):
            kl = klists[m]
            for idx, (j, u) in enumerate(kl):
                steps.append((m, idx, len(kl), j, u))
        LA = 2

        def acc_of(m):
            ab = 4 + (m % 2)
            return ps[ab // 2][:, ab % 2, :].rearrange("p (h d) -> p h d", h=4), accB[m % 2]

        def qk(n):
            m, idx, nk, j, u = steps[n]
            pbk, pbuf = bk.bank(n % 4)
            if kind == 'B':
                for g2 in range(2):
                    pg.op('pe', I('matmul',
                        pbk[:, g2 * 256:(g2 + 1) * 256].rearrange("p (h q) -> p h q", h=2),
                        lhsT=kT.t[:, g2, j * 128:(j + 1) * 128],
                        rhs=qT.t[:, 2 * g2:2 * g2 + 2, m * 128:(m + 1) * 128], start=(g2 == 0), stop=(g2 == 1),
                        skip_group_check=True),
                        reads=[kTb[j // TPC], qTb[m // TPC]], writes=[pbuf] if g2 == 0 else (),
                        pwrites=() if g2 == 0 else [pbuf])
            else:
                for h in range(4):
                    pg.op('pe', I('matmul',
                        pbk[:, h * 128:(h + 1) * 128], lhsT=kT.t[:, h, j * 128:(j + 1) * 128],
                        rhs=qT.t[:, h, m * 128:(m + 1) * 128], start=(h == 0), stop=(h == 3),
                        skip_group_check=True),
                        reads=[kTb[j // TPC], qTb[m // TPC]], writes=[pbuf] if h == 0 else (),
                        pwrites=() if h == 0 else [pbuf])
            p = pT[n % 4]
            if u is not None:
                pe_ = pE[n % 2]
                pg.op('act', I('activation', out=pe_.t[:], in_=pbk[:, 0:512], func=AF.Exp, scale=scale),
                      reads=[pbuf], writes=[pe_.b])
                pg.op('dve', I('tensor_tensor', out=p.t[:], in0=pe_.t[:], in1=tb.t[:, u, :], op=ALU.mult),
                      reads=[pe_.b, tb.b], writes=[p.b])
            else:
                pg.op('act', I('activation', out=p.t[:], in_=pbk[:, 0:512], func=AF.Exp, scale=scale),
                      reads=[pbuf], writes=[p.b])

        def pv(n):
            m, idx, nk, j, u = steps[n]
            acc_ap, accbuf = acc_of(m)
            p = pT[n % 4]
            for h in range(4):
                vh = h // 2 if kind == 'B' else h
                first = (idx == 0 and h == 0)
                pg.op('pe', I('matmul',
                    acc_ap[:, h, 0:65], lhsT=p.t[:, h * 128:(h + 1) * 128], rhs=v1.t[:, j, vh, 0:65],
                    start=(idx == 0 and h == 0), stop=(idx == nk - 1 and h == 3), skip_group_check=True),
                    reads=[p.b, v1b[j // TPC]], writes=[accbuf] if first else (), pwrites=() if first else [accbuf])
            if idx == nk - 1:
                ocur = oc[(m // 4) % 2]
                if kind == 'B':
                    pg.op('dve', I('tensor_tensor', out=den.t[:], in0=acc_ap[:, :, 64], in1=esink.t[:], op=ALU.add),
                          reads=[accbuf, esink.b], writes=[den.b])
                    pg.op('dve', I('reciprocal', out=den.t[:], in_=den.t[:]), reads=[den.b], writes=[den.b])
                else:
                    pg.op('dve', I('reciprocal', out=den.t[:], in_=acc_ap[:, :, 64]), reads=[accbuf], writes=[den.b])
                pg.op('dve', I('tensor_tensor',
                    out=ocur.t[:, m % 4, :].rearrange("p (h d) -> p h d", h=4), in0=acc_ap[:, :, 0:64],
                    in1=den.t[:, 0:4].unsqueeze(2).to_broadcast([128, 4, 64]), op=ALU.mult),
                    reads=[accbuf, den.b], pwrites=[ocur.b])
                if m % 4 == 3:
                    m0 = m - 3
                    _dma(cx, ST_ENG,
                         dr['o'][m0 * 128:(m0 + 4) * 128, ocol:ocol + 256].rearrange("(t p) c -> p t c", p=128),
                         ocur.t[:], reads=[ocur.b], key='st_' + ocur.key)

        NS_ = len(steps)
        for n in range(min(LA, NS_)):
            qk(n)
        for n in range(NS_):
            if n + LA < NS_:
                qk(n + LA)
            pv(n)
        pg.barrier()


def phase_epi(cx, S, l, last, h_src, p_src, h_dst, dr):
    pg = cx.pg
    NT = S // 128
    with contextlib.ExitStack() as st:
        T = lambda name, shape, dt: cx.tile(st, name, shape, dt)
        bk = Banks(cx)
        cx.stg = [T('stg%d' % i, [128, 1104], F32) for i in range(2)]
        cx.stg_k = 0
        wo = T('wo', [128, 8, 1024], BF16)
        wg = T('wg', [128, 8, 1024], BF16)
        wp = T('wp', [128, 2, 1024], BF16)
        gcol = T('gcol', [128, 8], F32)
        _dma(cx, LD_ENG, gcol.t[:], dr['ple_norm_g'][l], writes=[gcol.b], key='gcol')
        if last:
            fg = T('fg', [128, 1024], F32)
            _dma(cx, LD_ENG, fg.t[:], dr['final_norm_g'][0:1, :].partition_broadcast(128), writes=[fg.b], key='fg')
        NSL = 4
        SL = []
        for k in range(NSL):
            d = {}
            for nm, shp, dt in (('ot', [128, 1024], F32), ('sgt', [128, 1024], F32), ('ht', [128, 1024], F32),
                                ('pt', [128, 256], F32), ('mix', [128, 1024], BF16), ('mixT', [128, 8, 128], BF16),
                                ('h1', [128, 1024], F32), ('hn1', [128, 1024], BF16), ('hn1T', [128, 8, 128], BF16),
                                ('ss', [128, 1], F32), ('rs', [128, 1], F32),
                                ('gsig', [128, 1024], F32), ('pb16', [128, 256], BF16), ('pTt', [128, 2, 128], BF16),
                                ('yt', [128, 1024], F32)):
                if nm == 'yt' and not last:
                    continue
                d[nm] = T('%s%d' % (nm, k), shp, dt)
            SL.append(d)

        def transpose8(src, dstT, n, evac_eng):
            bi = bk.next()
            pb, pbuf = bk.bank_bf(bi)
            for c in range(n):
                pg.op('pe', I('transpose',
                    out=pb[:, c * 128:(c + 1) * 128], in_=src.t[:, c * 128:(c + 1) * 128], identity=cx.ident.t[:]),
                    reads=[src.b, cx.ident.b], writes=[pbuf] if c == 0 else (), pwrites=() if c == 0 else [pbuf])
            if evac_eng == 'act':
                pg.op('act', I('activation',
                    out=dstT.t[:, 0:n, :], in_=pb[:, 0:n * 128].rearrange("p (c k) -> p c k", c=n), func=AF.Copy),
                    reads=[pbuf], writes=[dstT.b])
            else:
                pg.op('dve', I('tensor_copy',
                    out=dstT.t[:, 0:n, :], in_=pb[:, 0:n * 128].rearrange("p (c k) -> p c k", c=n)),
                    reads=[pbuf], writes=[dstT.b])

        def proj(lT, w, nck):
            res = []
            for n in range(2):
                bi = bk.next()
                pbk, pbuf = bk.bank(bi)
                for c in range(nck):
                    pg.op('pe', I('matmul',
                        pbk[:, :], lhsT=lT.t[:, c, :], rhs=w.t[:, c, n * 512:(n + 1) * 512],
                        start=(c == 0), stop=(c == nck - 1)),
                        reads=[lT.b, w.b], writes=[pbuf] if c == 0 else (), pwrites=() if c == 0 else [pbuf])
                res.append((pbk, pbuf))
            return res

        def tile_gen(i):
            k = i % NSL
            d = SL[k]
            rows = slice(i * 128, (i + 1) * 128)
            o_, s_, h_, p_ = d['ot'], d['sgt'], d['ht'], d['pt']
            mix, mixT, h1, hn1, hn1T = d['mix'], d['mixT'], d['h1'], d['hn1'], d['hn1T']
            ss, rs, gsig, pb16, pTt = d['ss'], d['rs'], d['gsig'], d['pb16'], d['pTt']
            tt = gsig
            hh = h1
            y_ = d.get('yt')
            _dma(cx, LD_ENG, o_.t[:], dr['o'][rows, :], writes=[o_.b], key='ot%d' % k)
            _dma(cx, LD_ENG, s_.t[:], dr['sg'][rows, :], writes=[s_.b], key='sgt%d' % k)
            _dma(cx, LD_ENG, h_.t[:], h_src[rows, :], writes=[h_.b], key='ht%d' % k)
            _dma(cx, LD_ENG, p_.t[:], p_src[rows, :], writes=[p_.b], key='pt%d' % k)
            yield
            pg.op('dve', I('tensor_tensor', out=mix.t[:], in0=o_.t[:], in1=s_.t[:], op=ALU.mult),
                  reads=[o_.b, s_.b], writes=[mix.b])
            pg.op('dve', I('tensor_copy', out=pb16.t[:], in_=p_.t[:]), reads=[p_.b], writes=[pb16.b])
            yield
            transpose8(mix, mixT, 8, 'act')
            transpose8(pb16, pTt, 2, 'dve')
            yield
            r = proj(mixT, wo, 8)
            for n in range(2):
                pbk, pbuf = r[n]
                pg.op('dve', I('tensor_tensor',
                    out=h1.t[:, n * 512:(n + 1) * 512], in0=pbk[:, :], in1=h_.t[:, n * 512:(n + 1) * 512], op=ALU.add),
                    reads=[pbuf, h_.b], writes=[h1.b] if n == 0 else (), pwrites=() if n == 0 else [h1.b])
            yield
            pg.op('act', I('activation', out=hn1.t[:], in_=h1.t[:], func=AF.Square, scale=1.0 / 32,
                                                accum_out=ss.t[:, 0:1]), reads=[h1.b], writes=[ss.b, hn1.b])
            rstd_from_ms(cx, ss, rs, 1)
            pg.op('act', I('activation', out=hn1.t[:], in_=h1.t[:], func=AF.Copy, scale=rs.t[:, 0:1]),
                  reads=[h1.b, rs.b], writes=[hn1.b])
            yield
            transpose8(hn1, hn1T, 8, 'dve')
            yield
            r = proj(hn1T, wg, 8)
            for n in range(2):
                pbk, pbuf = r[n]
                pg.op('act', I('activation', out=gsig.t[:, n * 512:(n + 1) * 512], in_=pbk[:, :],
                                                                 func=AF.Sigmoid),
                      reads=[pbuf], writes=[gsig.b] if n == 0 else (), pwrites=() if n == 0 else [gsig.b])
            yield
            r = proj(pTt, wp, 2)
            for n in range(2):
                pbk, pbuf = r[n]
                sl = slice(n * 512, (n + 1) * 512)
                pg.op('dve', I('tensor_tensor', out=tt.t[:, sl], in0=pbk[:, :], in1=gsig.t[:, sl],
                                                                      op=ALU.mult),
                      reads=[pbuf, gsig.b], writes=[gsig.b])
            pg.op('pool', I('tensor_tensor', out=hh.t[:], in0=tt.t[:], in1=h1.t[:], op=ALU.add),
                  reads=[gsig.b, h1.b], writes=[h1.b])
            yield
            if not last:
                _dma(cx, ST_ENG, h_dst[rows, :], hh.t[:], reads=[hh.b], key='st_h2_%d' % k)
            else:
                pg.op('act', I('activation', out=mix.t[:], in_=hh.t[:], func=AF.Square, scale=1.0 / 32,
                                                           accum_out=ss.t[:, 0:1]), reads=[hh.b], writes=[ss.b, mix.b])
                rstd_from_ms(cx, ss, rs, 1)
                pg.op('dve', I('scalar_tensor_tensor',
                    out=y_.t[:], in0=hh.t[:], scalar=rs.t[:, 0:1], in1=fg.t[:], op0=ALU.mult, op1=ALU.mult),
                    reads=[hh.b, rs.b, fg.b], writes=[y_.b])
                _dma(cx, ST_ENG, h_dst[rows, :], y_.t[:], reads=[y_.b], key='st_yt_%d' % k)

        STAG = 2
        active = []
        nxt = 0
        step = 0
        pre = []
        for _ in range(min(NSL, NT)):
            g = tile_gen(len(pre))
            next(g)
            pre.append(g)
        load_w(cx, st, wo, dr['w_out'][l], 8, 1024)
        load_w(cx, st, wg, dr['w_ple_gate'][l], 8, 1024, gcol)
        load_w(cx, st, wp, dr['w_ple_proj'][l], 2, 1024)
        while nxt < NT or active:
            if nxt < NT and step % STAG == 0 and len(active) < NSL:
                active.append(pre[nxt] if nxt < len(pre) else tile_gen(nxt))
                nxt += 1
            for g in reversed(list(active)):
                try:
                    next(g)
                except StopIteration:
                    active.remove(g)
            step += 1
        pg.barrier()


def _rot_idx(base, nheads, hd):
    out = []
    for h in range(nheads):
        for j in range(hd):
            src = j + hd // 2 if j < hd // 2 else j - hd // 2
            out.append(base + h * hd + src)
    return out


def _w1_cols():
    aq = list(range(0, 256)); ak = list(range(256, 512)); av = list(range(512, 768))
    bq = list(range(768, 1024)); bkk = list(range(1024, 1152)); bv = list(range(1152, 1280))
    cq = list(range(1280, 1536)); ck = list(range(1536, 1792)); cv = list(range(1792, 2048))
    dcq = list(range(2048, 2304)); dckv = list(range(2304, 2432)); dkr = list(range(2432, 2464))
    gate = list(range(2464, 3488))
    aqr = _rot_idx(0, 8, 32); akr = _rot_idx(256, 8, 32)
    bqr = _rot_idx(768, 4, 64); bkr = _rot_idx(1024, 2, 64); dkrr = _rot_idx(2432, 1, 32)
    cols = (aq[0:128] + aq[128:] + ak[0:128] + ak[128:] + bq[0:128] + bq[128:] + bkk
            + cq[0:128] + cq[128:] + ck[0:128] + ck[128:] + dkr
            + av + bv + cv + dcq + dckv + gate)
    assert len(cols) == NC1
    return np.array(cols, dtype=np.int64)


def _rope_table(n, hd):
    inv = (1.0 / (np.float32(10000.0) ** (np.arange(0, hd, 2, dtype=np.float32) / np.float32(hd)))).astype(np.float32)
    ang = (np.arange(n, dtype=np.float32)[:, None] * inv[None, :]).astype(np.float32)
    cos = np.cos(ang).astype(np.float32)
    sin = np.sin(ang).astype(np.float32)
    p = np.arange(128)
    fi = p % (hd // 2)
    sign = np.where((p % hd) < hd // 2, -1.0, 1.0).astype(np.float32)
    tab = np.empty((128, 2, n), np.float32)
    tab[:, 0, :] = cos[:, fi].T
    tab[:, 1, :] = sin[:, fi].T * sign[:, None]
    return tab


def _c_patterns(S, pats):
    R = S // 64
    NT = S // 128
    res = []
    ki = np.arange(128)
    qi = np.arange(128)
    for m in range(NT):
        qr = 2 * m + qi // 64
        qcol = qi % 64
        ws = np.clip(qr - 4, 0, R - 8)
        cs = np.clip(qcol - 8, 0, 64 - 16)
        lst = []
        for j in range(max(0, m - 4), min(NT, m + 5)):
            krow = 2 * j + ki // 64
            kcol = ki % 64
            valid = ((krow[:, None] >= ws[None, :]) & (krow[:, None] < ws[None, :] + 8)
                     & (kcol[:, None] >= cs[None, :]) & (kcol[:, None] < cs[None, :] + 16))
            if not valid.any():
                continue
            drr = np.where(valid, krow[:, None] - qr[None, :] + 7, 0).astype(np.int64)
            dcc = np.where(valid, np.clip(kcol[:, None] - qcol[None, :], -15, 15) + 15, 0).astype(np.int64)
            key = (valid.tobytes(), drr.tobytes(), dcc.tobytes())
            if key not in pats:
                pats[key] = (len(pats), valid, drr, dcc)
            lst.append((j, pats[key][0]))
        res.append(lst)
    return res


def _b_klists(S):
    NT = S // 128
    res = []
    for n in range(NT):
        lst = []
        if n >= 1:
            lst.append((n - 1, 0))
        lst.append((n, None))
        if n + 1 < NT:
            lst.append((n + 1, 1))
        res.append(lst)
    return res


_CACHE = {}


def build(SP, SS, DEPTH=2, SMAX=None):
    key = (SP, SS, DEPTH)
    if key in _CACHE:
        return _CACHE[key]
    SMAX = max(SP, SS)
    pats = {}
    ckl = {S: _c_patterns(S, pats) for S in sorted({SP, SS})}
    U = len(pats)
    bkl = {S: _b_klists(S) for S in {SP, SS}}

    nc = bass.Bass("TRN2", target_bir_lowering=False)
    dr = {}

    def din(name, shape, dt=F32):
        dr[name] = nc.dram_tensor(name, list(shape), dt, kind="ExternalInput").ap()

    def dscr(name, shape, dt):
        dr[name] = nc.dram_tensor(name, list(shape), dt, kind="Internal").ap()

    din('xp', [SP, 1024]); din('xs', [SS, 1024])
    din('pp', [DEPTH, SP, 256]); din('ps_', [DEPTH, SS, 256])
    din('w1', [DEPTH, 1024, NC1]); din('norm_g', [DEPTH, 128, 8]); din('ple_norm_g', [DEPTH, 128, 8])
    din('final_norm_g', [1, 1024])
    din('a_lambda', [DEPTH, 128]); din('a_subln_g', [DEPTH, 64]); din('b_sink', [DEPTH, 4])
    din('cbias', [DEPTH, U, 128, 512]); din('bmask', [2, 128, 512])
    din('d_q_norm_g', [DEPTH, 256]); din('d_kv_norm_g', [DEPTH, 128])
    din('wuq', [DEPTH, 256, 384]); din('wuqr', [DEPTH, 256, 384]); din('wuk', [DEPTH, 128, 256]); din('wuv', [DEPTH, 128, 256])
    din('w_out', [DEPTH, 1024, 1024]); din('w_ple_gate', [DEPTH, 1024, 1024]); din('w_ple_proj', [DEPTH, 256, 1024])
    din('rope32', [128, 2, SMAX]); din('rope64', [128, 2, SMAX]); din('perm', [128, 2, 128])
    dr['yp'] = nc.dram_tensor('yp', [SP, 1024], F32, kind="ExternalOutput").ap()
    dr['ys'] = nc.dram_tensor('ys', [SS, 1024], F32, kind="ExternalOutput").ap()
    dscr('aqT', [256, SMAX], BF16); dscr('akT', [256, SMAX], BF16); dscr('av1', [SMAX, 260], BF16)
    dscr('bqT', [256, SMAX], BF16); dscr('bkT', [128, SMAX], BF16); dscr('bv1', [SMAX, 130], BF16)
    dscr('cqT', [256, SMAX], BF16); dscr('ckT', [256, SMAX], BF16); dscr('cv1', [SMAX, 260], BF16)
    dscr('dqT', [4, 96, SMAX], BF16); dscr('dkT', [4, 64, SMAX], BF16); dscr('dkpeT', [32, SMAX], BF16)
    dscr('dv1', [SMAX, 260], BF16)
    dscr('sg', [SMAX, 1024], F32); dscr('o', [SMAX, 1024], F32); dscr('hs', [SMAX, 1024], F32)

    cx = Cx(nc)
    pg = cx.pg
    with contextlib.ExitStack() as st:
        cx.ps = [st.enter_context(nc.psum_tensor('ps%d' % i, [128, 2, 512], F32)) for i in range(4)]
        cx.identf = cx.tile(st, 'identf', [128, 128], F32)
        cx.ident = cx.tile(st, 'ident', [128, 128], BF16)
        cx.epsc = cx.tile(st, 'epsc', [128, 1], F32)
        pg.op('pool', I('memset', cx.identf.t[:], 0.0), writes=[cx.identf.b])
        pg.op('pool', I('affine_select', out=cx.identf.t[:], in_=cx.identf.t[:], pattern=[[-1, 128]],
                                                compare_op=ALU.not_equal, fill=1.0, base=0, channel_multiplier=1),
              reads=[cx.identf.b], writes=[cx.identf.b])
        pg.op('dve', I('tensor_copy', out=cx.ident.t[:], in_=cx.identf.t[:]), reads=[cx.identf.b],
              writes=[cx.ident.b])
        pg.op('pool', I('memset', cx.epsc.t[:], EPS), writes=[cx.epsc.b])
        pg.barrier()
        for (S, xk, pk, yk) in ((SP, 'xp', 'pp', 'yp'), (SS, 'xs', 'ps_', 'ys')):
            for l in range(DEPTH):
                last = (l == DEPTH - 1)
                h_src = dr[xk] if l == 0 else dr['hs']
                if 'P' in PHASES:
                    phase_proj(cx, S, l, h_src, dr)
                if 'A' in PHASES:
                    phase_dense(cx, S, l, 'A', dr)
                if 'D' in PHASES:
                    phase_dense(cx, S, l, 'D', dr)
                if 'B' in PHASES:
                    phase_local(cx, S, l, 'B', dr, bkl[S], 2)
                if 'C' in PHASES:
                    phase_local(cx, S, l, 'C', dr, ckl[S], U)
                if 'E' in PHASES:
                    phase_epi(cx, S, l, last, h_src, dr[pk][l], dr[yk] if last else dr['hs'], dr)
        counts = pg.emit()
    res = (nc, pats, counts)
    _CACHE[key] = res
    return res


def prep_inputs(inputs, SP, SS, DEPTH, pats, core):
    f = lambda a: np.ascontiguousarray(np.asarray(a, dtype=np.float32))
    m = {}
    nb = inputs['x_sample'].shape[0]
    m['xp'] = f(inputs['x_prompt'][core])
    m['xs'] = f(inputs['x_sample'][core % nb])
    m['pp'] = f(inputs['p_prompt'][:, core])
    m['ps_'] = f(inputs['p_sample'][:, core % nb])
    return m


def shared_inputs(inputs, SP, SS, DEPTH, pats):
    f = lambda a: np.ascontiguousarray(np.asarray(a, dtype=np.float32))
    m = {}
    w_in = np.asarray(inputs['w_in'], dtype=np.float32)
    m['w1'] = f(w_in[:, :, _w1_cols()])
    m['norm_g'] = f(np.asarray(inputs['norm_g']).reshape(DEPTH, 8, 128).transpose(0, 2, 1))
    m['ple_norm_g'] = f(np.asarray(inputs['ple_norm_g']).reshape(DEPTH, 8, 128).transpose(0, 2, 1))
    m['final_norm_g'] = f(np.asarray(inputs['final_norm_g']).reshape(1, 1024))
    m['a_lambda'] = f(np.asarray(inputs['a_lambda']).reshape(DEPTH, 128))
    m['a_subln_g'] = f(inputs['a_subln_g'])
    m['b_sink'] = f(inputs['b_sink'])
    rpb = np.asarray(inputs['c_rpb'], dtype=np.float32)
    U = len(pats)
    cb = np.empty((DEPTH, U, 128, 4, 128), np.float32)
    for key, (u, valid, drr, dcc) in pats.items():
        g = rpb[:, :, drr, dcc]
        g = np.where(valid[None, None], g, np.float32(NEG))
        cb[:, u] = g.transpose(0, 2, 1, 3)
    m['cbias'] = f(cb.reshape(DEPTH, U, 128, 512))
    k = np.arange(128)[:, None]
    q = np.arange(128)[None, :]
    left = np.where(q <= k, 0.0, NEG).astype(np.float32)
    right = np.where(k <= q, 0.0, NEG).astype(np.float32)
    m['bmask'] = f(np.stack([np.tile(left, (1, 4)), np.tile(right, (1, 4))]))
    m['d_q_norm_g'] = f(inputs['d_q_norm_g'])
    m['d_kv_norm_g'] = f(inputs['d_kv_norm_g'])
    wuq = np.asarray(inputs['d_w_uq'], dtype=np.float32)
    idx = []
    for h in range(4):
        idx += list(range(h * 96, h * 96 + 64)) + _rot_idx(h * 96 + 64, 1, 32)
    m['wuq'] = f(wuq)
    m['wuqr'] = f(wuq[:, :, np.array(idx)])
    wukv = np.asarray(inputs['d_w_ukv'], dtype=np.float32).reshape(DEPTH, 128, 4, 128)
    m['wuk'] = f(wukv[:, :, :, 0:64].reshape(DEPTH, 128, 256))
    m['wuv'] = f(wukv[:, :, :, 64:128].reshape(DEPTH, 128, 256))
    m['w_out'] = f(inputs['w_out'])
    m['w_ple_gate'] = f(inputs['w_ple_gate'])
    m['w_ple_proj'] = f(inputs['w_ple_proj'])
    SMAX = max(SP, SS)
    pm = np.zeros((128, 2, 128), np.float32)
    for i, hd in enumerate((32, 64)):
        for p in range(128):
            j = p % hd
            src = p - j + (j + hd // 2 if j < hd // 2 else j - hd // 2)
            pm[src, i, p] = 1.0
    m['perm'] = pm
    m['rope32'] = _rope_table(SMAX, 32)
    m['rope64'] = _rope_table(SMAX, 64)
    return m


def kernel(**inputs):
    xp = inputs['x_prompt']
    xs = inputs['x_sample']
    B, SP, _ = xp.shape
    NB, SS, _ = xs.shape
    DEPTH = inputs['w_in'].shape[0]
    ncores = 8
    nc, pats, _ = build(SP, SS, DEPTH)
    sh = shared_inputs(inputs, SP, SS, DEPTH, pats)
    in_maps = []
    for c in range(ncores):
        m = dict(sh)
        m.update(prep_inputs(inputs, SP, SS, DEPTH, pats, c % B))
        in_maps.append(m)
    res = run_bass_kernel_spmd(nc, in_maps, core_ids=list(range(ncores)))
    yp = np.stack([np.asarray(res.results[c]['yp'], dtype=np.float32) for c in range(B)], axis=0)
    ys = np.stack([np.asarray(res.results[c]['ys'], dtype=np.float32) for c in range(NB)], axis=0)
    return (yp, ys)
```

```python
import contextlib
import math

import numpy as np
import concourse.bass as bass
import concourse.mybir as mybir
from concourse.bass_utils import run_bass_kernel_spmd

F32 = mybir.dt.float32
BF16 = mybir.dt.bfloat16
AF = mybir.ActivationFunctionType
ALU = mybir.AluOpType
AX = mybir.AxisListType

ENGS = ['pe', 'act', 'dve', 'pool', 'sp']
ENGOBJ = {'pe': 'tensor', 'act': 'scalar', 'dve': 'vector', 'pool': 'gpsimd', 'sp': 'sync'}

D_MODEL = 1024
EPS = 1e-6
NEG = -1e30
NC1 = 3488
F_OFF = [i * 128 for i in range(11)] + [1408]
F_M = [128] * 11 + [32]
T_AVBV = 1440
T_CV = 1824
T_DC = 2080
T_GATE = 2464
LD_ENG = 'sp'
DEBUG_NAMES = None
PHASES = 'PADBCE'
EMBED_WAIT = True
ST_ENG = 'pool'


def I(name, *args, **kw):
    return (name, args, kw)


class Buf:
    __slots__ = ('name', 'lw', 'rd', 'ord', 'fw')

    def __init__(self, name):
        self.name = name
        self.lw = []
        self.rd = []
        self.ord = []
        self.fw = None


class _Op:
    __slots__ = ('fn', 'waits', 'signal', 'dma_sem')

    def __init__(self, fn):
        self.fn = fn
        self.waits = []
        self.signal = False
        self.dma_sem = None


class Prog:
    def __init__(self, nc):
        self.nc = nc
        self.ops = {e: [] for e in ENGS}
        self.dma_cnt = {}

    def _dep(self, op, eng, ev):
        if ev[0] == 'e':
            if ev[1] == eng and eng == 'pe':
                return
            self.ops[ev[1]][ev[2]].signal = True
        op.waits.append(ev)

    def op(self, eng, fn, reads=(), writes=(), pwrites=(), dma=None):
        o = _Op(fn)
        lst = self.ops[eng]
        idx = len(lst)
        for b in reads:
            for w in b.lw:
                self._dep(o, eng, w)
        for b in writes:
            if not b.rd:
                for w in b.lw:
                    self._dep(o, eng, w)
            for r in b.rd:
                self._dep(o, eng, r)
            for r in b.ord:
                self._dep(o, eng, r)
        for b in pwrites:
            if b.rd:
                b.ord = b.rd
                b.rd = []
                b.lw = []
            for r in b.ord:
                self._dep(o, eng, r)
            if b.fw is not None:
                self._dep(o, eng, b.fw)
        if dma is not None:
            n = self.dma_cnt.get(dma, 0) + 1
            self.dma_cnt[dma] = n
            o.dma_sem = dma
            ev = ('d', dma, 16 * n)
        else:
            ev = ('e', eng, idx)
        for b in writes:
            b.lw = [ev]
            b.rd = []
            b.ord = []
            b.fw = ev
        for b in pwrites:
            b.lw.append(ev)
        for b in reads:
            b.rd.append(ev)
        lst.append(o)
        return ev

    def barrier(self):
        evs = []
        for e in ENGS:
            for i in range(len(self.ops[e]) - 1, -1, -1):
                o = self.ops[e][i]
                if o.dma_sem is None and o.fn is not None:
                    evs.append(('e', e, i))
                    break
        for k, n in self.dma_cnt.items():
            evs.append(('d', k, 16 * n))
        for e in ENGS:
            o = _Op(None)
            for ev in evs:
                if ev[0] == 'e':
                    if ev[1] == e and e == 'pe':
                        continue
                    self.ops[ev[1]][ev[2]].signal = True
                o.waits.append(ev)
            self.ops[e].append(o)

    def emit(self):
        nc = self.nc
        with contextlib.ExitStack() as st:
            esem = {e: st.enter_context(nc.semaphore('s_' + e)) for e in ENGS}
            dsem = {k: st.enter_context(nc.semaphore('d_%s' % (k,))) for k in self.dma_cnt}
            sigidx = {}
            for e in ENGS:
                c = 0
                arr = []
                for o in self.ops[e]:
                    if o.signal:
                        c += 1
                    arr.append(c)
                sigidx[e] = arr
            block = st.enter_context(nc.Block())

            def make(e):
                def body(eng):
                    seen = {}
                    for o in self.ops[e]:
                        need = {}
                        for ev in o.waits:
                            if ev[0] == 'e':
                                key = ('e', ev[1])
                                val = sigidx[ev[1]][ev[2]]
                            else:
                                key = ('d', ev[1])
                                val = ev[2]
                            if need.get(key, 0) < val:
                                need[key] = val
                        todo = []
                        for key, val in need.items():
                            if seen.get(key, 0) >= val:
                                continue
                            seen[key] = val
                            todo.append((esem[key[1]] if key[0] == 'e' else dsem[key[1]], val))
                        emb = None
                        if o.fn is not None and todo and EMBED_WAIT:
                            emb = todo.pop()
                        for sem, val in todo:
                            eng.wait_ge(sem, val)
                        if o.fn is None:
                            continue
                        ins = getattr(eng, o.fn[0])(*o.fn[1], **o.fn[2])
                        if emb is not None:
                            ins._wait_ge(emb[0], emb[1])
                        if DEBUG_NAMES is not None:
                            DEBUG_NAMES[getattr(ins.ins, 'name', None)] = (e, o.fn[0], str(o.fn[1])[:300], str({k: str(v)[:200] for k, v in o.fn[2].items()}))
                        if o.dma_sem is not None:
                            ins.then_inc(dsem[o.dma_sem], 16)
                        elif o.signal:
                            ins.then_inc(esem[e], 1)
                return body
            for e in ENGS:
                getattr(block, ENGOBJ[e])(make(e))
        return {e: len(self.ops[e]) for e in ENGS}


class Tile:
    __slots__ = ('t', 'b', 'key')

    def __init__(self, t, b, key):
        self.t = t
        self.b = b
        self.key = key


class Cx:
    def __init__(self, nc):
        self.nc = nc
        self.pg = Prog(nc)
        self.uid = 0

    def tile(self, st, name, shape, dt):
        self.uid += 1
        nm = '%s_%d' % (name, self.uid)
        t = st.enter_context(self.nc.sbuf_tensor(nm, list(shape), dt))
        return Tile(t, Buf(nm), name)


def _dma(cx, eng, out, in_, reads=(), writes=(), pwrites=(), key=None):
    return cx.pg.op(eng, I('dma_start', out=out, in_=in_), reads=reads, writes=writes,
                    pwrites=pwrites, dma=key)


def load_w(cx, st, dst, src, C, N, gcol=None, tag='w', piece_bufs=None):
    pg = cx.pg
    NS = 1104
    stg = cx.stg
    k = cx.stg_k
    order = [(c, n0) for c in range(C if C is not None else 1) for n0 in range(0, N, NS)]
    if piece_bufs is not None:
        order = [(c, n0) for n0 in range(0, N, NS) for c in range(C)]
    for (c, n0) in order:
        if True:
            w = min(NS, N - n0)
            s = stg[k % len(stg)]
            rows = src[c * 128:(c + 1) * 128, n0:n0 + w]
            si = k % len(stg)
            _dma(cx, LD_ENG, s.t[:, 0:w], rows, writes=[s.b], key='stg%d' % si)
            dview = dst.t[:, c, n0:n0 + w] if C is not None else dst.t[:, n0:n0 + w]
            dbuf = dst.b if piece_bufs is None else piece_bufs[n0 // NS]
            if gcol is not None:
                pg.op('act', I('activation', out=dview, in_=s.t[:, 0:w], func=AF.Copy, scale=gcol.t[:, c:c + 1]),
                      reads=[s.b, gcol.b], pwrites=[dbuf])
            else:
                pg.op('act', I('activation', out=dview, in_=s.t[:, 0:w], func=AF.Copy),
                      reads=[s.b], pwrites=[dbuf])
            k += 1
    cx.stg_k = k


class Banks:
    def __init__(self, cx):
        self.ps = cx.ps
        self.bufs = [Buf('bank%d' % i) for i in range(8)]
        self.pair_bufs = [Buf('pair%d' % i) for i in range(4)]
        self.rr = 0

    def bank(self, i):
        return self.ps[i // 2][:, i % 2, :], self.bufs[i]

    def bank_bf(self, i):
        return self.ps[i // 2][:, i % 2, :].bitcast(BF16), self.bufs[i]

    def next(self, lo=0, hi=8):
        i = lo + self.rr % (hi - lo)
        self.rr += 1
        return i


def rstd_from_ms(cx, ms, out, n, reads_extra=()):
    pg = cx.pg
    pg.op('act', I('activation', out=out.t[:, 0:n], in_=ms.t[:, 0:n], func=AF.Ln, bias=cx.epsc.t[:, 0:1]),
          reads=[ms.b, cx.epsc.b], writes=[out.b])
    pg.op('act', I('activation', out=out.t[:, 0:n], in_=out.t[:, 0:n], func=AF.Exp, scale=-0.5),
          reads=[out.b], writes=[out.b])


def phase_proj(cx, S, l, h_src, dr):
    pg = cx.pg
    nc = cx.nc
    NBK = S // 512
    with contextlib.ExitStack() as st:
        bk = Banks(cx)
        T = lambda name, shape, dt: cx.tile(st, name, shape, dt)
        w1 = T('w1', [128, 8, NC1], BF16)
        gcol = T('gcol', [128, 8], F32)
        _dma(cx, LD_ENG, gcol.t[:], dr['norm_g'][l], writes=[gcol.b], key='gcol')
        cx.stg = [T('stg%d' % i, [128, 1104], F32) for i in range(3)]
        cx.stg_k = 0
        w1p = [Buf('w1p%d' % i) for i in range(4)]

        def w1r(off, n):
            return [w1p[i] for i in range(off // 1104, (off + n - 1) // 1104 + 1)]
        wuq = T('wuq', [128, 2, 384], BF16)
        wuqr = T('wuqr', [128, 2, 384], BF16)
        wuk = T('wuk', [128, 256], BF16)
        wuv = T('wuv', [128, 256], BF16)
        gqkv = T('gqkv', [128, 384], F32)
        _dma(cx, LD_ENG, gqkv.t[:, 0:256], dr['d_q_norm_g'][l:l + 1, :].partition_broadcast(128),
             pwrites=[gqkv.b], key='gqkv')
        _dma(cx, LD_ENG, gqkv.t[:, 256:384], dr['d_kv_norm_g'][l:l + 1, :].partition_broadcast(128),
             pwrites=[gqkv.b], key='gqkv')

        pm32 = T('pm32', [128, 128], BF16)
        pm64 = T('pm64', [128, 128], BF16)
        pmf = T('pmf', [128, 2, 128], F32)
        _dma(cx, LD_ENG, pmf.t[:], dr['perm'][:, :, :], writes=[pmf.b], key='pmf')
        pg.op('dve', I('tensor_copy', out=pm32.t[:], in_=pmf.t[:, 0, :]), reads=[pmf.b], writes=[pm32.b])
        pg.op('dve', I('tensor_copy', out=pm64.t[:], in_=pmf.t[:, 1, :]), reads=[pmf.b], writes=[pm64.b])
        xbs = [T('xb%d' % i, [128, 512], BF16) for i in range(2)]
        for xb_ in xbs:
            pg.op('pool', I('memset', xb_.t[:], 0.0), writes=[xb_.b])
        hb = [T('hb%d' % i, [128, 4, 1024], F32) for i in range(2)]
        rA = [T('rA%d' % i, [128, 2, 512], F32) for i in range(2)]
        rB = [T('rB%d' % i, [128, 2, 512], F32) for i in range(2)]
        hn = T('hn', [128, 4, 1024], BF16)
        hnT = T('hnT', [128, 8, 512], BF16)
        junk = T('junk', [128, 1024], BF16)
        ss = T('ss', [128, 4], F32)
        rstd = T('rstd', [128, 4], F32)
        t1 = [T('t1_%d' % i, [128, 512], F32) for i in range(2)]
        t2 = [T('t2_%d' % i, [128, 512], F32) for i in range(2)]
        fo = [T('fo%d' % i, [128, 512], BF16) for i in range(4)]
        av1 = [T('av1_%d' % i, [128, 4, 4, 65], BF16) for i in range(2)]
        bv1 = [T('bv1_%d' % i, [128, 4, 2, 65], BF16) for i in range(2)]
        cv1 = [T('cv1_%d' % i, [128, 4, 4, 65], BF16) for i in range(2)]
        dv1 = [T('dv1_%d' % i, [128, 4, 4, 65], BF16) for i in range(2)]
        for tl in av1 + bv1 + cv1 + dv1:
            pg.op('pool', I('memset', tl.t[:], 1.0), writes=[tl.b])
        sg = [T('sg%d' % i, [128, 1024], F32) for i in range(2)]
        ms2_l = [T('ms2_%d' % i, [128, 2], F32) for i in range(2)]
        r2_l = [T('r2_%d' % i, [128, 2], F32) for i in range(2)]
        cn_l = [T('cn_%d' % i, [128, 384], BF16) for i in range(2)]
        cnT_l = [T('cnT%d' % i, [128, 3, 128], BF16) for i in range(2)]
        dqs = [T('dqs%d' % i, [128, 4, 512], BF16) for i in range(2)]
        dks = [T('dks%d' % i, [64, 4, 512], BF16) for i in range(2)]
        fok = 0
        sgk = 0

        def prologue(b):
            h = hb[b % 2]
            ra = rA[b % 2]
            rb = rB[b % 2]
            tok = slice(b * 512, (b + 1) * 512)
            _dma(cx, LD_ENG, h.t[:], h_src[tok, :].rearrange("(t p) f -> p t f", p=128), writes=[h.b],
                 key='hb%d' % (b % 2))
            _dma(cx, LD_ENG, ra.t[:], dr['rope32'][:, :, tok], writes=[ra.b], key='rA%d' % (b % 2))
            _dma(cx, LD_ENG, rb.t[:], dr['rope64'][:, :, tok], writes=[rb.b], key='rB%d' % (b % 2))
            for t in range(4):
                pg.op('act', I('activation', out=junk.t[:], in_=h.t[:, t, :], func=AF.Square,
                                                             scale=1.0 / 32, accum_out=ss.t[:, t:t + 1]),
                      reads=[h.b], writes=[junk.b], pwrites=[ss.b])
            rstd_from_ms(cx, ss, rstd, 4)
            for t in range(4):
                pg.op('act', I('activation', out=hn.t[:, t, :], in_=h.t[:, t, :], func=AF.Copy,
                               scale=rstd.t[:, t:t + 1]),
                      reads=[h.b, rstd.b], pwrites=[hn.b])

        prologue(0)
        load_w(cx, st, w1, dr['w1'][l], 8, NC1, gcol, piece_bufs=w1p)
        load_w(cx, st, wuq, dr['wuq'][l], 2, 384)
        load_w(cx, st, wuqr, dr['wuqr'][l], 2, 384)
        load_w(cx, st, wuk, dr['wuk'][l], None, 256)
        load_w(cx, st, wuv, dr['wuv'][l], None, 256)
        for b in range(NBK):
            h = hb[b % 2]
            ra = rA[b % 2]
            rb = rB[b % 2]
            tok = slice(b * 512, (b + 1) * 512)
            for t in range(4):
                bi = bk.next()
                pb, pbuf = bk.bank_bf(bi)
                for c in range(8):
                    pg.op('pe', I('transpose',
                        out=pb[:, c * 128:(c + 1) * 128], in_=hn.t[:, t, c * 128:(c + 1) * 128],
                        identity=cx.ident.t[:]),
                        reads=[hn.b, cx.ident.b], writes=[pbuf] if c == 0 else (), pwrites=() if c == 0 else [pbuf])
                pg.op('dve', I('tensor_copy',
                    out=hnT.t[:, :, t * 128:(t + 1) * 128],
                    in_=pb[:, 0:1024].rearrange("p (c k) -> p c k", c=8)),
                    reads=[pbuf], pwrites=[hnT.b])

            def fmm(ci, bi):
                pbk, pbuf = bk.bank(bi)
                M = F_M[ci]
                for c in range(8):
                    pg.op('pe', I('matmul',
                        pbk[0:M, :], lhsT=w1.t[:, c, F_OFF[ci]:F_OFF[ci] + M], rhs=hnT.t[:, c, :],
                        start=(c == 0), stop=(c == 7)),
                        reads=w1r(F_OFF[ci], M) + [hnT.b], writes=[pbuf] if c == 0 else (), pwrites=() if c == 0 else [pbuf])
                return pbk, pbuf

            def fout(fo_t, M, dst):
                _dma(cx, ST_ENG, dst, fo_t.t[0:M, :], reads=[fo_t.b], key='fo_' + fo_t.key)

            rope_jobs = [
                (0, pm32, ra, dr['aqT'][0:128, tok]), (1, pm32, ra, dr['aqT'][128:256, tok]),
                (2, pm32, ra, dr['akT'][0:128, tok]), (3, pm32, ra, dr['akT'][128:256, tok]),
                (4, pm64, rb, dr['bqT'][0:128, tok]), (5, pm64, rb, dr['bqT'][128:256, tok]),
                (6, pm64, rb, dr['bkT'][0:128, tok]),
                (11, pm32, ra, dr['dkpeT'][0:32, tok]),
            ]
            for (cx_i, pm, rt, dst) in rope_jobs:
                M = F_M[cx_i]
                b1 = bk.next()
                b2 = bk.next()
                px, pxb = fmm(cx_i, b1)
                xb_ = xbs[fok % 2]
                pg.op('dve', I('tensor_copy', out=xb_.t[0:M, :], in_=px[0:M, :]),
                      reads=[pxb], writes=[xb_.b])
                pr, prb = bk.bank(b2)
                pg.op('pe', I('matmul', pr[:, :], lhsT=pm.t[:, :], rhs=xb_.t[:, :], start=True, stop=True),
                      reads=[pm.b, xb_.b], writes=[prb])
                ta = t1[fok % 2]
                tb = t2[fok % 2]
                f = fo[fok % 4]
                fok += 1
                pg.op('dve', I('tensor_tensor',
                    out=ta.t[0:M, :], in0=px[0:M, :], in1=rt.t[0:M, 0, :], op=ALU.mult),
                    reads=[pxb, rt.b], writes=[ta.b])
                pg.op('dve', I('tensor_tensor',
                    out=tb.t[0:M, :], in0=pr[0:M, :], in1=rt.t[0:M, 1, :], op=ALU.mult),
                    reads=[prb, rt.b], writes=[tb.b])
                pg.op('pool', I('tensor_tensor',
                    out=f.t[0:M, :], in0=ta.t[0:M, :], in1=tb.t[0:M, :], op=ALU.add),
                    reads=[ta.b, tb.b], writes=[f.b])
                fout(f, M, dst)
            plain_jobs = [(7, 0.125, dr['cqT'][0:128, tok]), (8, 0.125, dr['cqT'][128:256, tok]),
                          (9, 1.0, dr['ckT'][0:128, tok]), (10, 1.0, dr['ckT'][128:256, tok])]
            for (ci, sc, dst) in plain_jobs:
                b1 = bk.next()
                px, pxb = fmm(ci, b1)
                f = fo[fok % 4]
                fok += 1
                pg.op('act', I('activation', out=f.t[:, :], in_=px[:, :], func=AF.Copy,
                                                                      scale=sc),
                      reads=[pxb], writes=[f.b])
                fout(f, 128, dst)

            if b + 1 < NBK:
                prologue(b + 1)
            a1 = av1[b % 2]
            b1v = bv1[b % 2]
            c1 = cv1[b % 2]
            d1 = dv1[b % 2]
            dq = dqs[b % 2]
            dk = dks[b % 2]
            tails = []
            tails_b = []
            for t in range(4):
                tsl = slice(t * 128, (t + 1) * 128)

                def tmm(off, n, bi):
                    pbk, pbuf = bk.bank(bi)
                    for c in range(8):
                        pg.op('pe', I('matmul',
                            pbk[:, 0:n], lhsT=hnT.t[:, c, tsl], rhs=w1.t[:, c, off:off + n],
                            start=(c == 0), stop=(c == 7)),
                            reads=w1r(off, n) + [hnT.b], writes=[pbuf] if c == 0 else (),
                            pwrites=() if c == 0 else [pbuf])
                    return pbk, pbuf
                p0, p0b = tmm(T_AVBV, 384, bk.next())
                pg.op('act', I('activation',
                    out=a1.t[:, t, :, 0:64], in_=p0[:, 0:256].rearrange("p (h d) -> p h d", h=4), func=AF.Copy),
                    reads=[p0b], pwrites=[a1.b])
                pg.op('act', I('activation',
                    out=b1v.t[:, t, :, 0:64], in_=p0[:, 256:384].rearrange("p (h d) -> p h d", h=2), func=AF.Copy),
                    reads=[p0b], pwrites=[b1v.b])
                p1, p1b = tmm(T_CV, 256, bk.next())
                pg.op('act', I('activation',
                    out=c1.t[:, t, :, 0:64], in_=p1[:, 0:256].rearrange("p (h d) -> p h d", h=4), func=AF.Copy),
                    reads=[p1b], pwrites=[c1.b])
                s = sg[sgk % 2]
                sgk += 1
                for n in range(2):
                    p3, p3b = tmm(T_GATE + n * 512, 512, bk.next())
                    pg.op('act', I('activation',
                        out=s.t[:, n * 512:(n + 1) * 512], in_=p3[:, :], func=AF.Silu),
                        reads=[p3b], writes=[s.b] if n == 0 else (), pwrites=() if n == 0 else [s.b])
                _dma(cx, ST_ENG, dr['sg'][b * 512 + t * 128: b * 512 + (t + 1) * 128, :], s.t[:], reads=[s.b],
                     key='sg_' + s.key)
                ms2, r2, cn = ms2_l[t % 2], r2_l[t % 2], cn_l[t % 2]
                p2, p2b = tmm(T_DC, 384, bk.next())
                pg.op('act', I('activation', out=junk.t[:, 0:256], in_=p2[:, 0:256], func=AF.Square,
                                                           scale=1.0 / 16, accum_out=ms2.t[:, 0:1]),
                      reads=[p2b], writes=[ms2.b, junk.b])
                pg.op('act', I('activation', out=junk.t[:, 0:128], in_=p2[:, 256:384], func=AF.Square,
                                                           scale=128 ** -0.5, accum_out=ms2.t[:, 1:2]),
                      reads=[p2b], writes=[junk.b], pwrites=[ms2.b])
                rstd_from_ms(cx, ms2, r2, 2)
                pg.op('dve', I('scalar_tensor_tensor',
                    out=cn.t[:, 0:256], in0=p2[:, 0:256], scalar=r2.t[:, 0:1], in1=gqkv.t[:, 0:256],
                    op0=ALU.mult, op1=ALU.mult), reads=[p2b, r2.b, gqkv.b], writes=[cn.b])
                pg.op('dve', I('scalar_tensor_tensor',
                    out=cn.t[:, 256:384], in0=p2[:, 256:384], scalar=r2.t[:, 1:2], in1=gqkv.t[:, 256:384],
                    op0=ALU.mult, op1=ALU.mult), reads=[p2b, r2.b, gqkv.b], pwrites=[cn.b])
                ta = t1[fok % 2]
                tb = t2[fok % 2]
                fok += 1

                def mla_tail(t=t, tsl=tsl, cn=cn, ta=ta, tb=tb, cnT=cnT_l[t % 2]):
                    bi = bk.next()
                    pb, pbuf = bk.bank_bf(bi)
                    for c in range(3):
                        pg.op('pe', I('transpose',
                            out=pb[:, c * 128:(c + 1) * 128], in_=cn.t[:, c * 128:(c + 1) * 128], identity=cx.ident.t[:]),
                            reads=[cn.b, cx.ident.b], writes=[pbuf] if c == 0 else (), pwrites=() if c == 0 else [pbuf])
                    pg.op('dve', I('tensor_copy',
                        out=cnT.t[:, :, :], in_=pb[:, 0:384].rearrange("p (c k) -> p c k", c=3)),
                        reads=[pbuf], writes=[cnT.b])
                    def tail_b():
                        bq_i, bqr_i, bk_i, bv_i = bk.next(), bk.next(), bk.next(), bk.next()
                        pq, pqb = bk.bank(bq_i)
                        pqr, pqrb = bk.bank(bqr_i)
                        pk, pkb = bk.bank(bk_i)
                        pv, pvb = bk.bank(bv_i)
                        for (pp, ppb, ww) in ((pq, pqb, wuq), (pqr, pqrb, wuqr)):
                            first = True
                            for hh in range(4):
                                for c in range(2):
                                    pg.op('pe', I('matmul',
                                        pp[0:96, hh * 128:(hh + 1) * 128], lhsT=ww.t[:, c, hh * 96:(hh + 1) * 96],
                                        rhs=cnT.t[:, c, :], start=(c == 0), stop=(c == 1)),
                                        reads=[ww.b, cnT.b], writes=[ppb] if first else (), pwrites=() if first else [ppb])
                                    first = False
                        for hh in range(4):
                            pg.op('pe', I('matmul',
                                pk[0:64, hh * 128:(hh + 1) * 128], lhsT=wuk.t[:, hh * 64:(hh + 1) * 64], rhs=cnT.t[:, 2, :],
                                start=True, stop=True),
                                reads=[wuk.b, cnT.b], writes=[pkb] if hh == 0 else (), pwrites=() if hh == 0 else [pkb])
                        pg.op('pe', I('matmul', pv[:, 0:256], lhsT=cnT.t[:, 2, :], rhs=wuv.t[:, :], start=True, stop=True),
                              reads=[wuv.b, cnT.b], writes=[pvb])
                        pg.op('act', I('activation',
                            out=dq.t[0:64, :, tsl], in_=pq[0:64, :].rearrange("p (h k) -> p h k", h=4), func=AF.Copy),
                            reads=[pqb], pwrites=[dq.b])
                        cosb = ra.t[64:96, 0, tsl].unsqueeze(1).to_broadcast([32, 4, 128])
                        sinb = ra.t[64:96, 1, tsl].unsqueeze(1).to_broadcast([32, 4, 128])
                        pg.op('dve', I('tensor_tensor',
                            out=ta.t[64:96, :].rearrange("p (h k) -> p h k", h=4),
                            in0=pq[64:96, :].rearrange("p (h k) -> p h k", h=4), in1=cosb, op=ALU.mult),
                            reads=[pqb, ra.b], writes=[ta.b])
                        pg.op('dve', I('tensor_tensor',
                            out=tb.t[64:96, :].rearrange("p (h k) -> p h k", h=4),
                            in0=pqr[64:96, :].rearrange("p (h k) -> p h k", h=4), in1=sinb, op=ALU.mult),
                            reads=[pqrb, ra.b], writes=[tb.b])
                        pg.op('pool', I('tensor_tensor',
                            out=dq.t[64:96, :, tsl], in0=ta.t[64:96, :].rearrange("p (h k) -> p h k", h=4),
                            in1=tb.t[64:96, :].rearrange("p (h k) -> p h k", h=4), op=ALU.add),
                            reads=[ta.b, tb.b], pwrites=[dq.b])
                        pg.op('act', I('activation',
                            out=dk.t[0:64, :, tsl], in_=pk[0:64, :].rearrange("p (h k) -> p h k", h=4), func=AF.Copy),
                            reads=[pkb], pwrites=[dk.b])
                        pg.op('act', I('activation',
                            out=d1.t[:, t, :, 0:64], in_=pv[:, 0:256].rearrange("p (h d) -> p h d", h=4), func=AF.Copy),
                            reads=[pvb], pwrites=[d1.b])
                    tails_b.append(tail_b)
                tails.append(mla_tail)
                if len(tails_b) > 0:
                    tails_b.pop(0)()
                if len(tails) > 1:
                    tails.pop(0)()
            while tails or tails_b:
                if tails_b:
                    tails_b.pop(0)()
                if tails:
                    tails.pop(0)()
            rows = lambda ap: ap[tok, :].rearrange("(t p) c -> p t c", p=128)
            _dma(cx, ST_ENG, rows(dr['av1']), a1.t[:].rearrange("p t h d -> p t (h d)"), reads=[a1.b], key='st_' + a1.key)
            _dma(cx, ST_ENG, rows(dr['bv1']), b1v.t[:].rearrange("p t h d -> p t (h d)"), reads=[b1v.b], key='st_' + b1v.key)
            _dma(cx, ST_ENG, rows(dr['cv1']), c1.t[:].rearrange("p t h d -> p t (h d)"), reads=[c1.b], key='st_' + c1.key)
            _dma(cx, ST_ENG, rows(dr['dv1']), d1.t[:].rearrange("p t h d -> p t (h d)"), reads=[d1.b], key='st_' + d1.key)
            _dma(cx, ST_ENG, dr['dqT'][:, :, tok].rearrange("h r s -> r h s"), dq.t[0:96, :, :], reads=[dq.b],
                 key='st_' + dq.key)
            _dma(cx, ST_ENG, dr['dkT'][:, :, tok].rearrange("h r s -> r h s"), dk.t[0:64, :, :], reads=[dk.b],
                 key='st_' + dk.key)
        pg.barrier()


def phase_dense(cx, S, l, kind, dr):
    pg = cx.pg
    NT = S // 128
    NQ = S // 512
    lam_init = 0.8 - 0.6 * math.exp(-0.3 * l)
    with contextlib.ExitStack() as st:
        T = lambda name, shape, dt: cx.tile(st, name, shape, dt)
        if kind == 'A':
            nch = 2
            maps = [(a // 4, 32 * (a % 4), 32, a // 2) for a in range(8)]
            scale = 32 ** -0.5
            groups = [(0, 1), (2, 3), (4, 5), (6, 7)]
            ocol = 0
        else:
            nch = 4
            maps = [(h, 0, 96, h) for h in range(4)]
            scale = 96 ** -0.5
            groups = [(0, 1), (2, 3)]
            ocol = 768
        kT = T('kT', [128, nch, S], BF16)
        v1 = T('v1', [128, NT, 4, 65], BF16)
        NCH = min(8, NT)
        TPC = NT // NCH
        kTb = [Buf('kTc%d' % i) for i in range(NCH)]
        v1b = [Buf('v1c%d' % i) for i in range(NCH)]
        vsrc = dr['av1'] if kind == 'A' else dr['dv1']
        for i in range(NCH):
            cs = slice(i * TPC * 128, (i + 1) * TPC * 128)
            if kind == 'A':
                _dma(cx, LD_ENG, kT.t[:, :, cs], dr['akT'][:, cs].rearrange("(c p) s -> p c s", p=128),
                     writes=[kTb[i]], key='kT%d' % i)
            else:
                _dma(cx, LD_ENG, kT.t[0:64, :, cs], dr['dkT'][:, :, cs].rearrange("h r s -> r h s"),
                     pwrites=[kTb[i]], key='kT%d' % i)
                for hh in range(4):
                    _dma(cx, LD_ENG, kT.t[64:96, hh, cs], dr['dkpeT'][:, cs], pwrites=[kTb[i]], key='kT%d' % i)
            t0, t1_ = i * TPC, (i + 1) * TPC
            _dma(cx, LD_ENG, v1.t[:, t0:t1_, :, :].rearrange("p t h d -> p t (h d)"),
                 vsrc[t0 * 128:t1_ * 128, :].rearrange("(t p) c -> p t c", p=128), writes=[v1b[i]], key='v1_%d' % i)
        qT = [T('qT%d' % i, [128, 8 if kind == 'A' else nch, 512], BF16) for i in range(2)]
        if kind == 'A':
            for qq in qT:
                pg.op('pool', I('memset', qq.t[:], 0.0), writes=[qq.b])
        pT = [T('pT%d' % i, [128, 2, 512], BF16) for i in range(3)]
        accS = T('accS', [65, 2, 512], F32)
        rr_ = T('rr', [128, 8], F32)
        o1 = T('o1', [128, 4, 64], F32)
        o2 = T('o2', [128, 4, 64], F32)
        dd = T('dd', [128, 4, 64], F32)
        sq = T('sq', [128, 4, 64], F32)
        ssq = T('ssq', [128, 4], F32)
        rn = T('rn', [128, 4], F32)
        oc = [T('oc%d' % i, [128, 4, 256], F32) for i in range(2)]
        ps = cx.ps
        Sb = [Buf('S0'), Buf('S1')]
        accb = Buf('acc')
        tpb = Buf('tp')
        tp = ps[3][:].rearrange("p a b -> p (a b)").rearrange("p (i c) -> p i c", i=8)
        if kind == 'A':
            lv = T('lv', [128, 128], F32)
            _dma(cx, LD_ENG, lv.t[:], dr['a_lambda'][l:l + 1, :].partition_broadcast(128), writes=[lv.b], key='lv')
            pr2 = T('pr2', [128, 2, 32], F32)
            s2 = T('s2', [128, 2], F32)
            nlam = T('nlam', [128, 1], F32)
            pg.op('dve', I('tensor_tensor', out=pr2.t[:, 0, :], in0=lv.t[:, 0:32], in1=lv.t[:, 32:64], op=ALU.mult),
                  reads=[lv.b], writes=[pr2.b])
            pg.op('dve', I('tensor_tensor', out=pr2.t[:, 1, :], in0=lv.t[:, 64:96], in1=lv.t[:, 96:128], op=ALU.mult),
                  reads=[lv.b], pwrites=[pr2.b])
            pg.op('dve', I('tensor_reduce', out=s2.t[:], in_=pr2.t[:], axis=AX.X, op=ALU.add),
                  reads=[pr2.b], writes=[s2.b])
            pg.op('act', I('activation', out=s2.t[:], in_=s2.t[:], func=AF.Exp), reads=[s2.b], writes=[s2.b])
            pg.op('dve', I('tensor_tensor', out=nlam.t[:], in0=s2.t[:, 1:2], in1=s2.t[:, 0:1], op=ALU.subtract),
                  reads=[s2.b], writes=[nlam.b])
            pg.op('dve', I('tensor_scalar', out=nlam.t[:], in0=nlam.t[:], scalar1=-lam_init, scalar2=None,
                                                   op0=ALU.add), reads=[nlam.b], writes=[nlam.b])
            gs = T('gs', [128, 64], F32)
            _dma(cx, LD_ENG, gs.t[:], dr['a_subln_g'][l:l + 1, :].partition_broadcast(128), writes=[gs.b], key='gs')
            gs4 = T('gs4', [128, 4, 64], F32)
            for q in range(4):
                pg.op('dve', I('tensor_scalar', out=gs4.t[:, q, :], in0=gs.t[:], scalar1=1.0 - lam_init,
                                                            scalar2=None, op0=ALU.mult),
                      reads=[gs.b], pwrites=[gs4.b])

        pending = []
        pending2 = []

        def post(gi, g, ocur, qc, last):
            pg.op('dve', I('tensor_copy', out=accS.t[:, :, :], in_=ps[2][0:65, :, :]), reads=[accb], writes=[accS.b])

            def rest():
                for i in range(2):
                    for qt in range(4):
                        first = (i == 0 and qt == 0)
                        pg.op('pe', I('transpose',
                            out=tp[:, i * 4 + qt, 0:65], in_=accS.t[0:65, i, qt * 128:(qt + 1) * 128],
                            identity=cx.identf.t[0:65, 0:65]),
                            reads=[accS.b, cx.identf.b], writes=[tpb] if first else (), pwrites=() if first else [tpb])
                pg.op('dve', I('reciprocal', out=rr_.t[:], in_=tp[:, :, 64]), reads=[tpb], writes=[rr_.b])
                if kind != 'A':
                    for i in range(2):
                        h = g[i]
                        pg.op('dve', I('tensor_tensor',
                            out=ocur.t[:, :, h * 64:(h + 1) * 64], in0=tp[:, i * 4:(i + 1) * 4, 0:64],
                            in1=rr_.t[:, i * 4:(i + 1) * 4].unsqueeze(2).to_broadcast([128, 4, 64]), op=ALU.mult),
                            reads=[tpb, rr_.b], pwrites=[ocur.b])
                    pending2.append(rest2)
                if kind == 'A':
                    h = gi
                    pg.op('dve', I('tensor_tensor', out=o1.t[:], in0=tp[:, 0:4, 0:64],
                                                           in1=rr_.t[:, 0:4].unsqueeze(2).to_broadcast([128, 4, 64]),
                                                           op=ALU.mult), reads=[tpb, rr_.b], writes=[o1.b])
                    pg.op('dve', I('tensor_tensor', out=o2.t[:], in0=tp[:, 4:8, 0:64],
                                                           in1=rr_.t[:, 4:8].unsqueeze(2).to_broadcast([128, 4, 64]),
                                                           op=ALU.mult), reads=[tpb, rr_.b], writes=[o2.b])
                    pg.op('dve', I('scalar_tensor_tensor', out=dd.t[:], in0=o2.t[:], scalar=nlam.t[:, 0:1],
                                                                  in1=o1.t[:], op0=ALU.mult, op1=ALU.add),
                          reads=[o1.b, o2.b, nlam.b], writes=[dd.b])
                    pg.op('pool', I('tensor_tensor', out=sq.t[:], in0=dd.t[:], in1=dd.t[:], op=ALU.mult),
                          reads=[dd.b], writes=[sq.b])
                    pg.op('dve', I('tensor_reduce', out=ssq.t[:], in_=sq.t[:], axis=AX.X, op=ALU.add),
                          reads=[sq.b], writes=[ssq.b])
                    pg.op('dve', I('tensor_scalar', out=ssq.t[:], in0=ssq.t[:], scalar1=1.0 / 64, scalar2=None,
                                                           op0=ALU.mult), reads=[ssq.b], writes=[ssq.b])
                    pending2.append(rest2)

            def rest2():
                if kind == 'A':
                    h = gi
                    rstd_from_ms(cx, ssq, rn, 4)
                    pg.op('dve', I('tensor_tensor', out=dd.t[:], in0=dd.t[:],
                                                           in1=rn.t[:, 0:4].unsqueeze(2).to_broadcast([128, 4, 64]),
                                                           op=ALU.mult), reads=[dd.b, rn.b], writes=[dd.b])
                    pg.op('pool', I('tensor_tensor', out=ocur.t[:, :, h * 64:(h + 1) * 64], in0=dd.t[:],
                                                                 in1=gs4.t[:], op=ALU.mult),
                          reads=[dd.b, gs4.b], pwrites=[ocur.b])
                if last:
                    _dma(cx, ST_ENG,
                         dr['o'][qc * 512:(qc + 1) * 512, ocol:ocol + 256].rearrange("(t p) c -> p t c", p=128),
                         ocur.t[:], reads=[ocur.b], key='st_' + ocur.key)
            pending.append(rest)

        steps = [(qc, gi, j) for qc in range(NQ) for gi in range(len(groups)) for j in range(NT)]
        NSTEP = len(steps)
        loaded_q = set()

        def load_q(qc):
            if qc in loaded_q or qc >= NQ:
                return
            loaded_q.add(qc)
            q = qT[qc % 2]
            if kind == 'A':
                for a_ in range(8):
                    r0 = a_ * 32
                    po = 32 * (a_ % 4)
                    _dma(cx, LD_ENG, q.t[po:po + 32, a_, :], dr['aqT'][r0:r0 + 32, qc * 512:(qc + 1) * 512],
                         pwrites=[q.b], key='qT%d' % (qc % 2))
            else:
                _dma(cx, LD_ENG, q.t[0:96, :, :], dr['dqT'][:, :, qc * 512:(qc + 1) * 512].rearrange("h r s -> r h s"),
                     writes=[q.b], key='qT%d' % (qc % 2))

        def qk(n):
            qc, gi, j = steps[n]
            load_q(qc)
            q = qT[qc % 2]
            g = groups[gi]
            sb = n % 2
            for i, m in enumerate(g):
                ch, po, K, vh = maps[m]
                if kind == 'A':
                    lhs_ap = kT.t[:, ch, j * 128:(j + 1) * 128]
                    rhs_ap = q.t[:, m, :]
                else:
                    lhs_ap = kT.t[po:po + K, ch, j * 128:(j + 1) * 128]
                    rhs_ap = q.t[po:po + K, ch, :]
                pg.op('pe', I('matmul', ps[sb][:, i, :], lhsT=lhs_ap, rhs=rhs_ap, start=True, stop=True),
                      reads=[kTb[j // TPC], q.b], writes=[Sb[sb]] if i == 0 else (), pwrites=() if i == 0 else [Sb[sb]])

        def ex(n):
            sb = n % 2
            pi = n % 3
            pg.op('act', I('activation', out=pT[pi].t[:, :, :], in_=ps[sb][:, :, :], func=AF.Exp, scale=scale),
                  reads=[Sb[sb]], writes=[pT[pi].b])

        def pv(n):
            qc, gi, j = steps[n]
            g = groups[gi]
            pi = n % 3
            for i, m in enumerate(g):
                ch, po, K, vh = maps[m]
                first = (j == 0 and i == 0)
                pg.op('pe', I('matmul', ps[2][0:65, i, :], lhsT=v1.t[:, j, vh, 0:65], rhs=pT[pi].t[:, i, :],
                              start=(j == 0), stop=(j == NT - 1)),
                      reads=[v1b[j // TPC], pT[pi].b], writes=[accb] if first else (), pwrites=() if first else [accb])

        load_q(0)
        load_q(1)
        for n in range(min(2, NSTEP)):
            qk(n)
            ex(n)
        for n in range(NSTEP):
            qc, gi, j = steps[n]
            if gi == 1 and j == 0:
                load_q(qc + 1)
            if n + 2 < NSTEP:
                qk(n + 2)
            pv(n)
            if n + 2 < NSTEP:
                ex(n + 2)
            if j == min(3, NT - 1) and pending:
                pending.pop(0)()
            if j == min(9, NT - 1) and pending2:
                pending2.pop(0)()
            if j == NT - 1:
                post(gi, groups[gi], oc[qc % 2], qc, gi == len(groups) - 1)
        while pending:
            pending.pop(0)()
        while pending2:
            pending2.pop(0)()
        pg.barrier()


def phase_local(cx, S, l, kind, dr, klists, U):
    pg = cx.pg
    NT = S // 128
    ps = cx.ps
    with contextlib.ExitStack() as st:
        T = lambda name, shape, dt: cx.tile(st, name, shape, dt)
        bk = Banks(cx)
        nkh = 2 if kind == 'B' else 4
        kT = T('kT', [64, nkh, S], BF16)
        qT = T('qT', [64, 4, S], BF16)
        if kind == 'B':
            nvh = 2
            ksrc = dr['bkT'][:, 0:S].rearrange("(g r) s -> r g s", r=64)
            qsrc = dr['bqT'][:, 0:S].rearrange("(h r) s -> r h s", r=64)
            vsrc = dr['bv1']
            scale = 0.125
            ocol = 256
            tsrc = dr['bmask']
        else:
            nvh = 4
            ksrc = dr['ckT'][:, 0:S].rearrange("(h r) s -> r h s", r=64)
            qsrc = dr['cqT'][:, 0:S].rearrange("(h r) s -> r h s", r=64)
            vsrc = dr['cv1']
            scale = 1.0
            ocol = 512
            tsrc = dr['cbias'][l]
        v1 = T('v1', [128, NT, nvh, 65], BF16)
        NCH = min(8, NT)
        TPC = NT // NCH
        kTb = [Buf('kTc%d' % i) for i in range(NCH)]
        qTb = [Buf('qTc%d' % i) for i in range(NCH)]
        v1b = [Buf('v1c%d' % i) for i in range(NCH)]
        for i in range(NCH):
            cs = slice(i * TPC * 128, (i + 1) * TPC * 128)
            _dma(cx, LD_ENG, kT.t[:, :, cs], ksrc[:, :, cs], writes=[kTb[i]], key='kT%d' % i)
            _dma(cx, LD_ENG, qT.t[:, :, cs], qsrc[:, :, cs], writes=[qTb[i]], key='qTl%d' % i)
            t0, t1_ = i * TPC, (i + 1) * TPC
            _dma(cx, LD_ENG, v1.t[:, t0:t1_, :, :].rearrange("p t h d -> p t (h d)"),
                 vsrc[t0 * 128:t1_ * 128, :].rearrange("(t p) c -> p t c", p=128), writes=[v1b[i]], key='v1_%d' % i)
        tb = T('tb', [128, U, 512], F32)
        cx.stg = [T('stg0', [128, 1104], F32), T('stg1', [128, 1104], F32)]
        cx.stg_k = 0
        for u in range(U):
            s = cx.stg[u % 2]
            _dma(cx, LD_ENG, s.t[:, 0:512], tsrc[u], writes=[s.b], key='stg%d' % (u % 2))
            pg.op('act', I('activation', out=tb.t[:, u, :], in_=s.t[:, 0:512], func=AF.Exp), reads=[s.b],
                  pwrites=[tb.b])
        if kind == 'B':
            esink = T('esink', [128, 4], F32)
            _dma(cx, LD_ENG, esink.t[:], dr['b_sink'][l:l + 1, :].partition_broadcast(128), writes=[esink.b], key='esink')
            pg.op('act', I('activation', out=esink.t[:], in_=esink.t[:], func=AF.Exp), reads=[esink.b],
                  writes=[esink.b])
        pT = [T('pT%d' % i, [128, 512], BF16) for i in range(4)]
        pE = [T('pE%d' % i, [128, 512], F32) for i in range(2)]
        den = T('den', [128, 4], F32)
        oc = [T('oc%d' % i, [128, 4, 256], F32) for i in range(2)]
        accB = [Buf('acc0'), Buf('acc1')]

        steps = []
        for m in range(NT):
            kl = klists[m]
            for idx, (j, u) in enumerate(kl):
                steps.append((m, idx, len(kl), j, u))
        LA = 2

        def acc_of(m):
            ab = 4 + (m % 2)
            return ps[ab // 2][:, ab % 2, :].rearrange("p (h d) -> p h d", h=4), accB[m % 2]

        def qk(n):
            m, idx, nk, j, u = steps[n]
            pbk, pbuf = bk.bank(n % 4)
            if kind == 'B':
                for g2 in range(2):
                    pg.op('pe', I('matmul',
                        pbk[:, g2 * 256:(g2 + 1) * 256].rearrange("p (h q) -> p h q", h=2),
                        lhsT=kT.t[:, g2, j * 128:(j + 1) * 128],
                        rhs=qT.t[:, 2 * g2:2 * g2 + 2, m * 128:(m + 1) * 128], start=(g2 == 0), stop=(g2 == 1),
                        skip_group_check=True),
                        reads=[kTb[j // TPC], qTb[m // TPC]], writes=[pbuf] if g2 == 0 else (),
                        pwrites=() if g2 == 0 else [pbuf])
            else:
                for h in range(4):
                    pg.op('pe', I('matmul',
                        pbk[:, h * 128:(h + 1) * 128], lhsT=kT.t[:, h, j * 128:(j + 1) * 128],
                        rhs=qT.t[:, h, m * 128:(m + 1) * 128], start=(h == 0), stop=(h == 3),
                        skip_group_check=True),
                        reads=[kTb[j // TPC], qTb[m // TPC]], writes=[pbuf] if h == 0 else (),
                        pwrites=() if h == 0 else [pbuf])
            p = pT[n % 4]
            if u is not None:
                pe_ = pE[n % 2]
                pg.op('act', I('activation', out=pe_.t[:], in_=pbk[:, 0:512], func=AF.Exp, scale=scale),
                      reads=[pbuf], writes=[pe_.b])
                pg.op('dve', I('tensor_tensor', out=p.t[:], in0=pe_.t[:], in1=tb.t[:, u, :], op=ALU.mult),
                      reads=[pe_.b, tb.b], writes=[p.b])
            else:
                pg.op('act', I('activation', out=p.t[:], in_=pbk[:, 0:512], func=AF.Exp, scale=scale),
                      reads=[pbuf], writes=[p.b])

        def pv(n):
            m, idx, nk, j, u = steps[n]
            acc_ap, accbuf = acc_of(m)
            p = pT[n % 4]
            for h in range(4):
                vh = h // 2 if kind == 'B' else h
                first = (idx == 0 and h == 0)
                pg.op('pe', I('matmul',
                    acc_ap[:, h, 0:65], lhsT=p.t[:, h * 128:(h + 1) * 128], rhs=v1.t[:, j, vh, 0:65],
                    start=(idx == 0 and h == 0), stop=(idx == nk - 1 and h == 3), skip_group_check=True),
                    reads=[p.b, v1b[j // TPC]], writes=[accbuf] if first else (), pwrites=() if first else [accbuf])
            if idx == nk - 1:
                ocur = oc[(m // 4) % 2]
                if kind == 'B':
                    pg.op('dve', I('tensor_tensor', out=den.t[:], in0=acc_ap[:, :, 64], in1=esink.t[:], op=ALU.add),
                          reads=[accbuf, esink.b], writes=[den.b])
                    pg.op('dve', I('reciprocal', out=den.t[:], in_=den.t[:]), reads=[den.b], writes=[den.b])
                else:
                    pg.op('dve', I('reciprocal', out=den.t[:], in_=acc_ap[:, :, 64]), reads=[accbuf], writes=[den.b])
                pg.op('dve', I('tensor_tensor',
                    out=ocur.t[:, m % 4, :].rearrange("p (h d) -> p h d", h=4), in0=acc_ap[:, :, 0:64],
                    in1=den.t[:, 0:4].unsqueeze(2).to_broadcast([128, 4, 64]), op=ALU.mult),
                    reads=[accbuf, den.b], pwrites=[ocur.b])
                if m % 4 == 3:
                    m0 = m - 3
                    _dma(cx, ST_ENG,
                         dr['o'][m0 * 128:(m0 + 4) * 128, ocol:ocol + 256].rearrange("(t p) c -> p t c", p=128),
                         ocur.t[:], reads=[ocur.b], key='st_' + ocur.key)

        NS_ = len(steps)
        for n in range(min(LA, NS_)):
            qk(n)
        for n in range(NS_):
            if n + LA < NS_:
                qk(n + LA)
            pv(n)
        pg.barrier()


def phase_epi(cx, S, l, last, h_src, p_src, h_dst, dr):
    pg = cx.pg
    NT = S // 128
    with contextlib.ExitStack() as st:
        T = lambda name, shape, dt: cx.tile(st, name, shape, dt)
        bk = Banks(cx)
        cx.stg = [T('stg%d' % i, [128, 1104], F32) for i in range(2)]
        cx.stg_k = 0
        wo = T('wo', [128, 8, 1024], BF16)
        wg = T('wg', [128, 8, 1024], BF16)
        wp = T('wp', [128, 2, 1024], BF16)
        gcol = T('gcol', [128, 8], F32)
        _dma(cx, LD_ENG, gcol.t[:], dr['ple_norm_g'][l], writes=[gcol.b], key='gcol')
        if last:
            fg = T('fg', [128, 1024], F32)
            _dma(cx, LD_ENG, fg.t[:], dr['final_norm_g'][0:1, :].partition_broadcast(128), writes=[fg.b], key='fg')
        NSL = 4
        SL = []
        for k in range(NSL):
            d = {}
            for nm, shp, dt in (('ot', [128, 1024], F32), ('sgt', [128, 1024], F32), ('ht', [128, 1024], F32),
                                ('pt', [128, 256], F32), ('mix', [128, 1024], BF16), ('mixT', [128, 8, 128], BF16),
                                ('h1', [128, 1024], F32), ('hn1', [128, 1024], BF16), ('hn1T', [128, 8, 128], BF16),
                                ('ss', [128, 1], F32), ('rs', [128, 1], F32),
                                ('gsig', [128, 1024], F32), ('pb16', [128, 256], BF16), ('pTt', [128, 2, 128], BF16),
                                ('yt', [128, 1024], F32)):
                if nm == 'yt' and not last:
                    continue
                d[nm] = T('%s%d' % (nm, k), shp, dt)
            SL.append(d)

        def transpose8(src, dstT, n, evac_eng):
            bi = bk.next()
            pb, pbuf = bk.bank_bf(bi)
            for c in range(n):
                pg.op('pe', I('transpose',
                    out=pb[:, c * 128:(c + 1) * 128], in_=src.t[:, c * 128:(c + 1) * 128], identity=cx.ident.t[:]),
                    reads=[src.b, cx.ident.b], writes=[pbuf] if c == 0 else (), pwrites=() if c == 0 else [pbuf])
            if evac_eng == 'act':
                pg.op('act', I('activation',
                    out=dstT.t[:, 0:n, :], in_=pb[:, 0:n * 128].rearrange("p (c k) -> p c k", c=n), func=AF.Copy),
                    reads=[pbuf], writes=[dstT.b])
            else:
                pg.op('dve', I('tensor_copy',
                    out=dstT.t[:, 0:n, :], in_=pb[:, 0:n * 128].rearrange("p (c k) -> p c k", c=n)),
                    reads=[pbuf], writes=[dstT.b])

        def proj(lT, w, nck):
            res = []
            for n in range(2):
                bi = bk.next()
                pbk, pbuf = bk.bank(bi)
                for c in range(nck):
                    pg.op('pe', I('matmul',
                        pbk[:, :], lhsT=lT.t[:, c, :], rhs=w.t[:, c, n * 512:(n + 1) * 512],
                        start=(c == 0), stop=(c == nck - 1)),
                        reads=[lT.b, w.b], writes=[pbuf] if c == 0 else (), pwrites=() if c == 0 else [pbuf])
                res.append((pbk, pbuf))
            return res

        def tile_gen(i):
            k = i % NSL
            d = SL[k]
            rows = slice(i * 128, (i + 1) * 128)
            o_, s_, h_, p_ = d['ot'], d['sgt'], d['ht'], d['pt']
            mix, mixT, h1, hn1, hn1T = d['mix'], d['mixT'], d['h1'], d['hn1'], d['hn1T']
            ss, rs, gsig, pb16, pTt = d['ss'], d['rs'], d['gsig'], d['pb16'], d['pTt']
            tt = gsig
            hh = h1
            y_ = d.get('yt')
            _dma(cx, LD_ENG, o_.t[:], dr['o'][rows, :], writes=[o_.b], key='ot%d' % k)
            _dma(cx, LD_ENG, s_.t[:], dr['sg'][rows, :], writes=[s_.b], key='sgt%d' % k)
            _dma(cx, LD_ENG, h_.t[:], h_src[rows, :], writes=[h_.b], key='ht%d' % k)
            _dma(cx, LD_ENG, p_.t[:], p_src[rows, :], writes=[p_.b], key='pt%d' % k)
            yield
            pg.op('dve', I('tensor_tensor', out=mix.t[:], in0=o_.t[:], in1=s_.t[:], op=ALU.mult),
                  reads=[o_.b, s_.b], writes=[mix.b])
            pg.op('dve', I('tensor_copy', out=pb16.t[:], in_=p_.t[:]), reads=[p_.b], writes=[pb16.b])
            yield
            transpose8(mix, mixT, 8, 'act')
            transpose8(pb16, pTt, 2, 'dve')
            yield
            r = proj(mixT, wo, 8)
            for n in range(2):
                pbk, pbuf = r[n]
                pg.op('dve', I('tensor_tensor',
                    out=h1.t[:, n * 512:(n + 1) * 512], in0=pbk[:, :], in1=h_.t[:, n * 512:(n + 1) * 512], op=ALU.add),
                    reads=[pbuf, h_.b], writes=[h1.b] if n == 0 else (), pwrites=() if n == 0 else [h1.b])
            yield
            pg.op('act', I('activation', out=hn1.t[:], in_=h1.t[:], func=AF.Square, scale=1.0 / 32,
                                                accum_out=ss.t[:, 0:1]), reads=[h1.b], writes=[ss.b, hn1.b])
            rstd_from_ms(cx, ss, rs, 1)
            pg.op('act', I('activation', out=hn1.t[:], in_=h1.t[:], func=AF.Copy, scale=rs.t[:, 0:1]),
                  reads=[h1.b, rs.b], writes=[hn1.b])
            yield
            transpose8(hn1, hn1T, 8, 'dve')
            yield
            r = proj(hn1T, wg, 8)
            for n in range(2):
                pbk, pbuf = r[n]
                pg.op('act', I('activation', out=gsig.t[:, n * 512:(n + 1) * 512], in_=pbk[:, :],
                                                                 func=AF.Sigmoid),
                      reads=[pbuf], writes=[gsig.b] if n == 0 else (), pwrites=() if n == 0 else [gsig.b])
            yield
            r = proj(pTt, wp, 2)
            for n in range(2):
                pbk, pbuf = r[n]
                sl = slice(n * 512, (n + 1) * 512)
                pg.op('dve', I('tensor_tensor', out=tt.t[:, sl], in0=pbk[:, :], in1=gsig.t[:, sl],
                                                                      op=ALU.mult),
                      reads=[pbuf, gsig.b], writes=[gsig.b])
            pg.op('pool', I('tensor_tensor', out=hh.t[:], in0=tt.t[:], in1=h1.t[:], op=ALU.add),
                  reads=[gsig.b, h1.b], writes=[h1.b])
            yield
            if not last:
                _dma(cx, ST_ENG, h_dst[rows, :], hh.t[:], reads=[hh.b], key='st_h2_%d' % k)
            else:
                pg.op('act', I('activation', out=mix.t[:], in_=hh.t[:], func=AF.Square, scale=1.0 / 32,
                                                           accum_out=ss.t[:, 0:1]), reads=[hh.b], writes=[ss.b, mix.b])
                rstd_from_ms(cx, ss, rs, 1)
                pg.op('dve', I('scalar_tensor_tensor',
                    out=y_.t[:], in0=hh.t[:], scalar=rs.t[:, 0:1], in1=fg.t[:], op0=ALU.mult, op1=ALU.mult),
                    reads=[hh.b, rs.b, fg.b], writes=[y_.b])
                _dma(cx, ST_ENG, h_dst[rows, :], y_.t[:], reads=[y_.b], key='st_yt_%d' % k)

        STAG = 1
        active = []
        nxt = 0
        step = 0
        pre = []
        for _ in range(min(NSL, NT)):
            g = tile_gen(len(pre))
            next(g)
            pre.append(g)
        load_w(cx, st, wo, dr['w_out'][l], 8, 1024)
        load_w(cx, st, wg, dr['w_ple_gate'][l], 8, 1024, gcol)
        load_w(cx, st, wp, dr['w_ple_proj'][l], 2, 1024)
        while nxt < NT or active:
            if nxt < NT and step % STAG == 0 and len(active) < NSL:
                active.append(pre[nxt] if nxt < len(pre) else tile_gen(nxt))
                nxt += 1
            for g in reversed(list(active)):
                try:
                    next(g)
                except StopIteration:
                    active.remove(g)
            step += 1
        pg.barrier()


def _rot_idx(base, nheads, hd):
    out = []
    for h in range(nheads):
        for j in range(hd):
            src = j + hd // 2 if j < hd // 2 else j - hd // 2
            out.append(base + h * hd + src)
    return out


def _w1_cols():
    aq = list(range(0, 256)); ak = list(range(256, 512)); av = list(range(512, 768))
    bq = list(range(768, 1024)); bkk = list(range(1024, 1152)); bv = list(range(1152, 1280))
    cq = list(range(1280, 1536)); ck = list(range(1536, 1792)); cv = list(range(1792, 2048))
    dcq = list(range(2048, 2304)); dckv = list(range(2304, 2432)); dkr = list(range(2432, 2464))
    gate = list(range(2464, 3488))
    aqr = _rot_idx(0, 8, 32); akr = _rot_idx(256, 8, 32)
    bqr = _rot_idx(768, 4, 64); bkr = _rot_idx(1024, 2, 64); dkrr = _rot_idx(2432, 1, 32)
    cols = (aq[0:128] + aq[128:] + ak[0:128] + ak[128:] + bq[0:128] + bq[128:] + bkk
            + cq[0:128] + cq[128:] + ck[0:128] + ck[128:] + dkr
            + av + bv + cv + dcq + dckv + gate)
    assert len(cols) == NC1
    return np.array(cols, dtype=np.int64)


def _rope_table(n, hd):
    inv = (1.0 / (np.float32(10000.0) ** (np.arange(0, hd, 2, dtype=np.float32) / np.float32(hd)))).astype(np.float32)
    ang = (np.arange(n, dtype=np.float32)[:, None] * inv[None, :]).astype(np.float32)
    cos = np.cos(ang).astype(np.float32)
    sin = np.sin(ang).astype(np.float32)
    p = np.arange(128)
    fi = p % (hd // 2)
    sign = np.where((p % hd) < hd // 2, -1.0, 1.0).astype(np.float32)
    tab = np.empty((128, 2, n), np.float32)
    tab[:, 0, :] = cos[:, fi].T
    tab[:, 1, :] = sin[:, fi].T * sign[:, None]
    return tab


def _c_patterns(S, pats):
    R = S // 64
    NT = S // 128
    res = []
    ki = np.arange(128)
    qi = np.arange(128)
    for m in range(NT):
        qr = 2 * m + qi // 64
        qcol = qi % 64
        ws = np.clip(qr - 4, 0, R - 8)
        cs = np.clip(qcol - 8, 0, 64 - 16)
        lst = []
        for j in range(max(0, m - 4), min(NT, m + 5)):
            krow = 2 * j + ki // 64
            kcol = ki % 64
            valid = ((krow[:, None] >= ws[None, :]) & (krow[:, None] < ws[None, :] + 8)
                     & (kcol[:, None] >= cs[None, :]) & (kcol[:, None] < cs[None, :] + 16))
            if not valid.any():
                continue
            drr = np.where(valid, krow[:, None] - qr[None, :] + 7, 0).astype(np.int64)
            dcc = np.where(valid, np.clip(kcol[:, None] - qcol[None, :], -15, 15) + 15, 0).astype(np.int64)
            key = (valid.tobytes(), drr.tobytes(), dcc.tobytes())
            if key not in pats:
                pats[key] = (len(pats), valid, drr, dcc)
            lst.append((j, pats[key][0]))
        res.append(lst)
    return res


def _b_klists(S):
    NT = S // 128
    res = []
    for n in range(NT):
        lst = []
        if n >= 1:
            lst.append((n - 1, 0))
        lst.append((n, None))
        if n + 1 < NT:
            lst.append((n + 1, 1))
        res.append(lst)
    return res


_CACHE = {}


def build(SP, SS, DEPTH=2, SMAX=None):
    key = (SP, SS, DEPTH)
    if key in _CACHE:
        return _CACHE[key]
    SMAX = max(SP, SS)
    pats = {}
    ckl = {S: _c_patterns(S, pats) for S in sorted({SP, SS})}
    U = len(pats)
    bkl = {S: _b_klists(S) for S in {SP, SS}}

    nc = bass.Bass("TRN2", target_bir_lowering=False)
    dr = {}

    def din(name, shape, dt=F32):
        dr[name] = nc.dram_tensor(name, list(shape), dt, kind="ExternalInput").ap()

    def dscr(name, shape, dt):
        dr[name] = nc.dram_tensor(name, list(shape), dt, kind="Internal").ap()

    din('xp', [SP, 1024]); din('xs', [SS, 1024])
    din('pp', [DEPTH, SP, 256]); din('ps_', [DEPTH, SS, 256])
    din('w1', [DEPTH, 1024, NC1]); din('norm_g', [DEPTH, 128, 8]); din('ple_norm_g', [DEPTH, 128, 8])
    din('final_norm_g', [1, 1024])
    din('a_lambda', [DEPTH, 128]); din('a_subln_g', [DEPTH, 64]); din('b_sink', [DEPTH, 4])
    din('cbias', [DEPTH, U, 128, 512]); din('bmask', [2, 128, 512])
    din('d_q_norm_g', [DEPTH, 256]); din('d_kv_norm_g', [DEPTH, 128])
    din('wuq', [DEPTH, 256, 384]); din('wuqr', [DEPTH, 256, 384]); din('wuk', [DEPTH, 128, 256]); din('wuv', [DEPTH, 128, 256])
    din('w_out', [DEPTH, 1024, 1024]); din('w_ple_gate', [DEPTH, 1024, 1024]); din('w_ple_proj', [DEPTH, 256, 1024])
    din('rope32', [128, 2, SMAX]); din('rope64', [128, 2, SMAX]); din('perm', [128, 2, 128])
    dr['yp'] = nc.dram_tensor('yp', [SP, 1024], F32, kind="ExternalOutput").ap()
    dr['ys'] = nc.dram_tensor('ys', [SS, 1024], F32, kind="ExternalOutput").ap()
    dscr('aqT', [256, SMAX], BF16); dscr('akT', [256, SMAX], BF16); dscr('av1', [SMAX, 260], BF16)
    dscr('bqT', [256, SMAX], BF16); dscr('bkT', [128, SMAX], BF16); dscr('bv1', [SMAX, 130], BF16)
    dscr('cqT', [256, SMAX], BF16); dscr('ckT', [256, SMAX], BF16); dscr('cv1', [SMAX, 260], BF16)
    dscr('dqT', [4, 96, SMAX], BF16); dscr('dkT', [4, 64, SMAX], BF16); dscr('dkpeT', [32, SMAX], BF16)
    dscr('dv1', [SMAX, 260], BF16)
    dscr('sg', [SMAX, 1024], F32); dscr('o', [SMAX, 1024], F32); dscr('hs', [SMAX, 1024], F32)

    cx = Cx(nc)
    pg = cx.pg
    with contextlib.ExitStack() as st:
        cx.ps = [st.enter_context(nc.psum_tensor('ps%d' % i, [128, 2, 512], F32)) for i in range(4)]
        cx.identf = cx.tile(st, 'identf', [128, 128], F32)
        cx.ident = cx.tile(st, 'ident', [128, 128], BF16)
        cx.epsc = cx.tile(st, 'epsc', [128, 1], F32)
        pg.op('pool', I('memset', cx.identf.t[:], 0.0), writes=[cx.identf.b])
        pg.op('pool', I('affine_select', out=cx.identf.t[:], in_=cx.identf.t[:], pattern=[[-1, 128]],
                                                compare_op=ALU.not_equal, fill=1.0, base=0, channel_multiplier=1),
              reads=[cx.identf.b], writes=[cx.identf.b])
        pg.op('dve', I('tensor_copy', out=cx.ident.t[:], in_=cx.identf.t[:]), reads=[cx.identf.b],
              writes=[cx.ident.b])
        pg.op('pool', I('memset', cx.epsc.t[:], EPS), writes=[cx.epsc.b])
        pg.barrier()
        for (S, xk, pk, yk) in ((SP, 'xp', 'pp', 'yp'), (SS, 'xs', 'ps_', 'ys')):
            for l in range(DEPTH):
                last = (l == DEPTH - 1)
                h_src = dr[xk] if l == 0 else dr['hs']
                if 'P' in PHASES:
                    phase_proj(cx, S, l, h_src, dr)
                if 'A' in PHASES:
                    phase_dense(cx, S, l, 'A', dr)
                if 'D' in PHASES:
                    phase_dense(cx, S, l, 'D', dr)
                if 'B' in PHASES:
                    phase_local(cx, S, l, 'B', dr, bkl[S], 2)
                if 'C' in PHASES:
                    phase_local(cx, S, l, 'C', dr, ckl[S], U)
                if 'E' in PHASES:
                    phase_epi(cx, S, l, last, h_src, dr[pk][l], dr[yk] if last else dr['hs'], dr)
        counts = pg.emit()
    res = (nc, pats, counts)
    _CACHE[key] = res
    return res


def prep_inputs(inputs, SP, SS, DEPTH, pats, core):
    f = lambda a: np.ascontiguousarray(np.asarray(a, dtype=np.float32))
    m = {}
    nb = inputs['x_sample'].shape[0]
    m['xp'] = f(inputs['x_prompt'][core])
    m['xs'] = f(inputs['x_sample'][core % nb])
    m['pp'] = f(inputs['p_prompt'][:, core])
    m['ps_'] = f(inputs['p_sample'][:, core % nb])
    return m


def shared_inputs(inputs, SP, SS, DEPTH, pats):
    f = lambda a: np.ascontiguousarray(np.asarray(a, dtype=np.float32))
    m = {}
    w_in = np.asarray(inputs['w_in'], dtype=np.float32)
    m['w1'] = f(w_in[:, :, _w1_cols()])
    m['norm_g'] = f(np.asarray(inputs['norm_g']).reshape(DEPTH, 8, 128).transpose(0, 2, 1))
    m['ple_norm_g'] = f(np.asarray(inputs['ple_norm_g']).reshape(DEPTH, 8, 128).transpose(0, 2, 1))
    m['final_norm_g'] = f(np.asarray(inputs['final_norm_g']).reshape(1, 1024))
    m['a_lambda'] = f(np.asarray(inputs['a_lambda']).reshape(DEPTH, 128))
    m['a_subln_g'] = f(inputs['a_subln_g'])
    m['b_sink'] = f(inputs['b_sink'])
    rpb = np.asarray(inputs['c_rpb'], dtype=np.float32)
    U = len(pats)
    cb = np.empty((DEPTH, U, 128, 4, 128), np.float32)
    for key, (u, valid, drr, dcc) in pats.items():
        g = rpb[:, :, drr, dcc]
        g = np.where(valid[None, None], g, np.float32(NEG))
        cb[:, u] = g.transpose(0, 2, 1, 3)
    m['cbias'] = f(cb.reshape(DEPTH, U, 128, 512))
    k = np.arange(128)[:, None]
    q = np.arange(128)[None, :]
    left = np.where(q <= k, 0.0, NEG).astype(np.float32)
    right = np.where(k <= q, 0.0, NEG).astype(np.float32)
    m['bmask'] = f(np.stack([np.tile(left, (1, 4)), np.tile(right, (1, 4))]))
    m['d_q_norm_g'] = f(inputs['d_q_norm_g'])
    m['d_kv_norm_g'] = f(inputs['d_kv_norm_g'])
    wuq = np.asarray(inputs['d_w_uq'], dtype=np.float32)
    idx = []
    for h in range(4):
        idx += list(range(h * 96, h * 96 + 64)) + _rot_idx(h * 96 + 64, 1, 32)
    m['wuq'] = f(wuq)
    m['wuqr'] = f(wuq[:, :, np.array(idx)])
    wukv = np.asarray(inputs['d_w_ukv'], dtype=np.float32).reshape(DEPTH, 128, 4, 128)
    m['wuk'] = f(wukv[:, :, :, 0:64].reshape(DEPTH, 128, 256))
    m['wuv'] = f(wukv[:, :, :, 64:128].reshape(DEPTH, 128, 256))
    m['w_out'] = f(inputs['w_out'])
    m['w_ple_gate'] = f(inputs['w_ple_gate'])
    m['w_ple_proj'] = f(inputs['w_ple_proj'])
    SMAX = max(SP, SS)
    pm = np.zeros((128, 2, 128), np.float32)
    for i, hd in enumerate((32, 64)):
        for p in range(128):
            j = p % hd
            src = p - j + (j + hd // 2 if j < hd // 2 else j - hd // 2)
            pm[src, i, p] = 1.0
    m['perm'] = pm
    m['rope32'] = _rope_table(SMAX, 32)
    m['rope64'] = _rope_table(SMAX, 64)
    return m


def kernel(**inputs):
    xp = inputs['x_prompt']
    xs = inputs['x_sample']
    B, SP, _ = xp.shape
    NB, SS, _ = xs.shape
    DEPTH = inputs['w_in'].shape[0]
    ncores = 8
    nc, pats, _ = build(SP, SS, DEPTH)
    sh = shared_inputs(inputs, SP, SS, DEPTH, pats)
    in_maps = []
    for c in range(ncores):
        m = dict(sh)
        m.update(prep_inputs(inputs, SP, SS, DEPTH, pats, c % B))
        in_maps.append(m)
    res = run_bass_kernel_spmd(nc, in_maps, core_ids=list(range(ncores)))
    yp = np.stack([np.asarray(res.results[c]['yp'], dtype=np.float32) for c in range(B)], axis=0)
    ys = np.stack([np.asarray(res.results[c]['ys'], dtype=np.float32) for c in range(NB)], axis=0)
    return (yp, ys)
```
